# Optimizing a Trainium2 kernel written in Bass

```python
import math
import jax, jax.numpy as jnp
from jax import lax
import numpy as np

D_MODEL = 1024
BATCH = 8
SEQ = 8192
DEPTH = 4

GRID_W = 64
D_FF = 2816
N_AB = (DEPTH + 1) // 2
N_C = DEPTH // 2
D_POOL = D_MODEL // 2
POOL_WINDOWS = (2, 4, 8, 16)
N_POOL_GROUPS = len(POOL_WINDOWS)
POOL_GROUP = D_POOL // N_POOL_GROUPS
D_SSM = D_MODEL - D_POOL
SSM_GROUP = 16
N_SSM_GROUPS = D_SSM // SSM_GROUP
SSM_STATE = 64
N_HEADS = 16
HEAD_DIM = D_MODEL // N_HEADS
D_ATTN = N_HEADS * HEAD_DIM
MAX_KH = 8
KW = 16
RMS_EPS = 1e-6
DT_MIN = 1e-3
DT_MAX = 1e-1
A_RE_MAX = -1e-4

kernel_name = 'hybrid_pool_s5_natten_macaron'

F32 = jnp.float32


def rms_norm(x, g):
    xf = x.astype(F32)
    y = xf * lax.rsqrt(jnp.mean(xf * xf, axis=-1, keepdims=True) + RMS_EPS)
    return (y * g.astype(F32)).astype(x.dtype)


def swiglu_ffn(x, w_gate, w_up, w_down):
    return (jax.nn.silu(x @ w_gate) * (x @ w_up)) @ w_down


def pool_mixer(u, w_grp, scale):
    bsz, L, _ = u.shape
    uf = u.astype(F32)
    cs = jnp.concatenate([jnp.zeros((bsz, 1, D_POOL), F32), jnp.cumsum(uf, axis=1)], axis=1)
    t = jnp.arange(L)
    outs = []
    for gi, w in enumerate(POOL_WINDOWS):
        lo = w // 2
        hi = w - 1 - lo
        start = jnp.clip(t - lo, 0, L)
        end = jnp.clip(t + hi + 1, 0, L)
        c0, c1 = gi * POOL_GROUP, (gi + 1) * POOL_GROUP
        csg = cs[:, :, c0:c1]
        cnt = (end - start).astype(F32)[None, :, None]
        outs.append((csg[:, end] - csg[:, start]) / cnt - uf[:, :, c0:c1])
    p = jnp.stack(outs, axis=2)
    y = jnp.einsum('blgc,gcd->blgd', p, w_grp.astype(F32)).reshape(bsz, L, D_POOL)
    return (y * scale.astype(F32)).astype(u.dtype)


def ssm_scan_dir(u, a_re, a_im, log_dt, b_re, b_im, c_re, c_im):
    L = u.shape[0]
    a_re = jnp.minimum(a_re.astype(F32), A_RE_MAX)
    a_im = a_im.astype(F32)
    dt = jnp.exp(log_dt.astype(F32))[:, None]
    mag = jnp.exp(a_re * dt)
    lam_re = mag * jnp.cos(a_im * dt)
    lam_im = mag * jnp.sin(a_im * dt)
    num_re = lam_re - 1.0
    num_im = lam_im
    den = a_re * a_re + a_im * a_im
    f_re = ((num_re * a_re + num_im * a_im) / den)[..., None]
    f_im = ((num_im * a_re - num_re * a_im) / den)[..., None]
    b_re = b_re.astype(F32)
    b_im = b_im.astype(F32)
    bb_re = f_re * b_re - f_im * b_im
    bb_im = f_re * b_im + f_im * b_re
    x_re = jnp.einsum('lbgh,gph->lbgp', u, bb_re)
    x_im = jnp.einsum('lbgh,gph->lbgp', u, bb_im)
    shp = (L, 1) + lam_re.shape
    l_re = jnp.broadcast_to(lam_re[None, None], shp)
    l_im = jnp.broadcast_to(lam_im[None, None], shp)

    def combine(e1, e2):
        a1r, a1i, b1r, b1i = e1
        a2r, a2i, b2r, b2i = e2
        return (a2r * a1r - a2i * a1i,
                a2r * a1i + a2i * a1r,
                a2r * b1r - a2i * b1i + b2r,
                a2r * b1i + a2i * b1r + b2i)

    _, _, s_re, s_im = lax.associative_scan(combine, (l_re, l_im, x_re, x_im), axis=0)
    return (jnp.einsum('lbgp,ghp->lbgh', s_re, c_re.astype(F32))
            - jnp.einsum('lbgp,ghp->lbgh', s_im, c_im.astype(F32)))


def s5_mixer(u, a_re, a_im, log_dt, b_re, b_im, c_re, c_im, d_skip, w_glu, b_glu):
    bsz, L, _ = u.shape
    uf = u.astype(F32)
    ug = jnp.transpose(uf.reshape(bsz, L, N_SSM_GROUPS, SSM_GROUP), (1, 0, 2, 3))
    y_f = ssm_scan_dir(ug, a_re[0], a_im[0], log_dt[0], b_re[0], b_im[0], c_re[0], c_im[0])
    y_b = jnp.flip(ssm_scan_dir(jnp.flip(ug, axis=0), a_re[1], a_im[1], log_dt[1],
                                b_re[1], b_im[1], c_re[1], c_im[1]), axis=0)
    y = jnp.transpose(y_f + y_b, (1, 0, 2, 3)).reshape(bsz, L, D_SSM) + d_skip.astype(F32) * uf
    y = jax.nn.gelu(y)
    y = y * jax.nn.sigmoid(y @ w_glu.astype(F32) + b_glu.astype(F32))
    return y.astype(u.dtype)


def neighborhood_attention(h, w_qkv, rpb, w_out):
    bsz, L, _ = h.shape
    rows = L // GRID_W
    kh = min(MAX_KH, rows)
    qkv = (h @ w_qkv).reshape(bsz, rows, GRID_W, 3, N_HEADS, HEAD_DIM)
    q = qkv[:, :, :, 0] * (HEAD_DIM ** -0.5)
    k = qkv[:, :, :, 1]
    v = qkv[:, :, :, 2]
    col = jnp.arange(GRID_W)
    col_start = jnp.clip(col - KW // 2, 0, GRID_W - KW)
    col_idx = col_start[:, None] + jnp.arange(KW)[None, :]
    col_bias_idx = col_idx - col[:, None] + (KW - 1)
    rpb_c = rpb.astype(F32)[:, :, col_bias_idx]

    def row_step(r):
        r0 = jnp.clip(r - kh // 2, 0, rows - kh)
        q_r = lax.dynamic_index_in_dim(q, r, axis=1, keepdims=False)
        k_band = lax.dynamic_slice_in_dim(k, r0, kh, axis=1)
        v_band = lax.dynamic_slice_in_dim(v, r0, kh, axis=1)
        k_nb = k_band[:, :, col_idx]
        v_nb = v_band[:, :, col_idx]
        row_bias_idx = r0 + jnp.arange(kh) - r + (MAX_KH - 1)
        bias = jnp.transpose(jnp.take(rpb_c, row_bias_idx, axis=1), (0, 2, 1, 3))
        s = jnp.einsum('bqhd,biqjhd->bhqij', q_r, k_nb).astype(F32) + bias[None]
        p = jax.nn.softmax(s.reshape(bsz, N_HEADS, GRID_W, kh * KW), axis=-1)
        p = p.reshape(bsz, N_HEADS, GRID_W, kh, KW).astype(v.dtype)
        return jnp.einsum('bhqij,biqjhd->bqhd', p, v_nb)

    o = lax.map(row_step, jnp.arange(rows))
    o = jnp.transpose(o, (1, 0, 2, 3, 4)).reshape(bsz, L, D_ATTN)
    return o @ w_out


def setup_inputs(seed: int = 0) -> dict:
    key = jax.random.key(seed)
    ks = jax.random.split(key, 24)
    nrm = jax.random.normal
    G, P, H = N_SSM_GROUPS, SSM_STATE, SSM_GROUP
    x = nrm(ks[0], (BATCH, SEQ, D_MODEL), F32)
    norm_g = 1.0 + 0.02 * nrm(ks[1], (DEPTH, 6, D_MODEL), F32)
    ffn_w_gate = nrm(ks[2], (DEPTH, 2, D_MODEL, D_FF), F32) * D_MODEL ** -0.5
    ffn_w_up = nrm(ks[3], (DEPTH, 2, D_MODEL, D_FF), F32) * D_MODEL ** -0.5
    ffn_w_down = nrm(ks[4], (DEPTH, 2, D_FF, D_MODEL), F32) * D_FF ** -0.5
    ab_w_in = nrm(ks[5], (N_AB, D_MODEL, D_POOL + D_SSM), F32) * D_MODEL ** -0.5
    pool_w = nrm(ks[6], (N_AB, N_POOL_GROUPS, POOL_GROUP, POOL_GROUP), F32) * POOL_GROUP ** -0.5
    pool_scale = 1.0 + 0.02 * nrm(ks[7], (N_AB, D_POOL), F32)
    n_idx = jnp.arange(P, dtype=F32)
    ssm_A_re = -0.5 + 0.01 * nrm(ks[8], (N_AB, 2, G, P), F32)
    ssm_A_im = math.pi * n_idx + 0.01 * nrm(ks[9], (N_AB, 2, G, P), F32)
    ssm_log_dt = jax.random.uniform(ks[10], (N_AB, 2, G), F32, math.log(DT_MIN), math.log(DT_MAX))
    ssm_B_re = nrm(ks[11], (N_AB, 2, G, P, H), F32) * (2.0 * H) ** -0.5
    ssm_B_im = nrm(ks[12], (N_AB, 2, G, P, H), F32) * (2.0 * H) ** -0.5
    ssm_C_re = nrm(ks[13], (N_AB, 2, G, H, P), F32) * (2.0 * P) ** -0.5
    ssm_C_im = nrm(ks[14], (N_AB, 2, G, H, P), F32) * (2.0 * P) ** -0.5
    ssm_D = nrm(ks[15], (N_AB, D_SSM), F32)
    ssm_w_glu = nrm(ks[16], (N_AB, D_SSM, D_SSM), F32) * D_SSM ** -0.5
    ssm_b_glu = 0.01 * nrm(ks[17], (N_AB, D_SSM), F32)
    ab_w_out = nrm(ks[18], (N_AB, D_POOL + D_SSM, D_MODEL), F32) * (D_POOL + D_SSM) ** -0.5
    na_w_qkv = nrm(ks[19], (N_C, D_MODEL, 3 * D_ATTN), F32) * D_MODEL ** -0.5
    na_rpb = 0.02 * nrm(ks[20], (N_C, N_HEADS, 2 * MAX_KH - 1, 2 * KW - 1), F32)
    na_w_out = nrm(ks[21], (N_C, D_ATTN, D_MODEL), F32) * D_ATTN ** -0.5
    return {'x': x, 'norm_g': norm_g, 'ffn_w_gate': ffn_w_gate, 'ffn_w_up': ffn_w_up,
            'ffn_w_down': ffn_w_down, 'ab_w_in': ab_w_in, 'pool_w': pool_w,
            'pool_scale': pool_scale, 'ssm_A_re': ssm_A_re, 'ssm_A_im': ssm_A_im,
            'ssm_log_dt': ssm_log_dt, 'ssm_B_re': ssm_B_re, 'ssm_B_im': ssm_B_im,
            'ssm_C_re': ssm_C_re, 'ssm_C_im': ssm_C_im, 'ssm_D': ssm_D,
            'ssm_w_glu': ssm_w_glu, 'ssm_b_glu': ssm_b_glu, 'ab_w_out': ab_w_out,
            'na_w_qkv': na_w_qkv, 'na_rpb': na_rpb, 'na_w_out': na_w_out}


def reference(x, norm_g, ffn_w_gate, ffn_w_up, ffn_w_down, ab_w_in, pool_w, pool_scale,
              ssm_A_re, ssm_A_im, ssm_log_dt, ssm_B_re, ssm_B_im, ssm_C_re, ssm_C_im,
              ssm_D, ssm_w_glu, ssm_b_glu, ab_w_out, na_w_qkv, na_rpb, na_w_out):
    h = x
    for layer in range(DEPTH):
        g = norm_g[layer]
        f = swiglu_ffn(rms_norm(h, g[0]), ffn_w_gate[layer, 0], ffn_w_up[layer, 0], ffn_w_down[layer, 0])
        h = h + 0.5 * rms_norm(f, g[1])
        hn = rms_norm(h, g[2])
        i = layer // 2
        if layer % 2 == 0:
            z = hn @ ab_w_in[i]
            ya = pool_mixer(z[..., :D_POOL], pool_w[i], pool_scale[i])
            yb = s5_mixer(z[..., D_POOL:], ssm_A_re[i], ssm_A_im[i], ssm_log_dt[i],
                          ssm_B_re[i], ssm_B_im[i], ssm_C_re[i], ssm_C_im[i],
                          ssm_D[i], ssm_w_glu[i], ssm_b_glu[i])
            m = jnp.concatenate([ya, yb], axis=-1) @ ab_w_out[i]
        else:
            m = neighborhood_attention(hn, na_w_qkv[i], na_rpb[i], na_w_out[i])
        h = h + rms_norm(m, g[3])
        f = swiglu_ffn(rms_norm(h, g[4]), ffn_w_gate[layer, 1], ffn_w_up[layer, 1], ffn_w_down[layer, 1])
        h = h + 0.5 * rms_norm(f, g[5])
    return h
```

```python
import numpy as np
import os
DBG = int(os.environ.get('DBG', '9'))
import concourse.bass as bass
import concourse.mybir as mybir
from concourse.bass_utils import run_bass_kernel_spmd

F32 = mybir.dt.float32
BF16 = mybir.dt.bfloat16
AF = mybir.ActivationFunctionType
ALU = mybir.AluOpType

D = 1024
L = 8192
DFF = 2816
NJ = DFF // 128
NK = D // 128
DEPTH = 4
NT = 1024
NTILES = L // NT
EPS = 1e-6

ALL_STEPS = ["L%d%s" % (l, c) for l in range(DEPTH) for c in "amc"]


class Region:
    __slots__ = ("name", "last_w", "readers")

    def __init__(self, name):
        self.name = name
        self.last_w = None
        self.readers = {}


class Op:
    __slots__ = ("eng", "fn", "deps", "is_dma", "key", "signals", "sigval", "idx")

    def __init__(self, eng, fn, is_dma=False, key=None):
        self.eng = eng
        self.fn = fn
        self.deps = []
        self.is_dma = is_dma
        self.key = key
        self.signals = False
        self.sigval = 0
        self.idx = 0


class Prog:
    ENGS = ("pe", "act", "dve", "pool", "sp")

    def __init__(self, nc):
        self.nc = nc
        self.ops = []
        self.dma_last = {}
        self.last_op = {}
        self.pending = {}

    def _track(self, op, reads, writes):
        deps = set()
        for r in reads:
            if r.last_w is not None:
                deps.add(r.last_w)
        for w in writes:
            if w.last_w is not None:
                deps.add(w.last_w)
            for o in w.readers.values():
                deps.add(o)
        if op.eng in self.pending:
            for o in self.pending.pop(op.eng):
                deps.add(o)
        deps.discard(op)
        op.deps = list(deps)
        if not op.is_dma:
            self.last_op[op.eng] = op
        for r in reads:
            r.readers[op.eng if not op.is_dma else ("dma", id(op))] = op
        for w in writes:
            w.last_w = op
            w.readers = {}
        op.idx = len(self.ops)
        self.ops.append(op)

    def barrier(self):
        allops = list(self.last_op.values()) + list(self.dma_last.values())
        self.pending = {e: list(allops) for e in self.ENGS}

    def op(self, eng, fn, reads=(), writes=()):
        o = Op(eng, fn)
        self._track(o, reads, writes)
        return o

    def dma(self, queue, fn, reads, writes, key):
        o = Op(queue, fn, is_dma=True, key=key)
        o.signals = True
        self._track(o, reads, writes)
        prev = self.dma_last.get(key)
        if prev is not None and prev not in o.deps:
            o.deps.append(prev)
        self.dma_last[key] = o
        return o

    def emit(self, final_dma_ops):
        nc = self.nc
        for o in self.ops:
            for d in o.deps:
                if d.is_dma:
                    d.signals = True
                elif d.eng == o.eng and not o.is_dma and d.eng == "pe":
                    continue
                else:
                    d.signals = True
        for o in final_dma_ops:
            o.signals = True
        cnt = {}
        keys = []
        for o in self.ops:
            if not o.signals:
                continue
            k = ("dma", o.key) if o.is_dma else ("eng", o.eng)
            if k not in cnt:
                cnt[k] = 0
                keys.append(k)
            cnt[k] += 16 if o.is_dma else 1
            o.sigval = cnt[k]
        self.nsig = dict(cnt)
        import contextlib
        with contextlib.ExitStack() as st:
            sems = {}
            for i, k in enumerate(keys):
                sems[k] = st.enter_context(nc.semaphore("s%d" % i))
            block = st.enter_context(nc.Block())
            per_eng = {e: [o for o in self.ops if o.eng == e] for e in self.ENGS}

            def run(eng_name, e):
                waited = {}
                for o in per_eng[eng_name]:
                    for d in o.deps:
                        if not d.signals:
                            continue
                        if (not d.is_dma) and (not o.is_dma) and d.eng == "pe" and o.eng == "pe":
                            continue
                        k = ("dma", d.key) if d.is_dma else ("eng", d.eng)
                        if waited.get(k, 0) >= d.sigval:
                            continue
                        waited[k] = d.sigval
                        e.wait_ge(sems[k], d.sigval)
                    inst = o.fn(e)
                    if o.signals:
                        k = ("dma", o.key) if o.is_dma else ("eng", o.eng)
                        inst.then_inc(sems[k], 16 if o.is_dma else 1)
                if eng_name == "pool":
                    for o in final_dma_ops:
                        k = ("dma", o.key)
                        if waited.get(k, 0) < o.sigval:
                            waited[k] = o.sigval
                            e.wait_ge(sems[k], o.sigval)

            @block.tensor
            def _(e):
                run("pe", e)

            @block.scalar
            def _(e):
                run("act", e)

            @block.vector
            def _(e):
                run("dve", e)

            @block.gpsimd
            def _(e):
                run("pool", e)

            @block.sync
            def _(e):
                run("sp", e)


import math
I32 = mybir.dt.int32
GRID_W = 64
NROWS = L // GRID_W
NH = 16
PI = math.pi


def build_program(steps, ntiles=NTILES):
    nc = bass.Bass("TRN2", target_bir_lowering=False)
    P = Prog(nc)
    Lr = ntiles * NT

    def dram_in(name, shape, dt=F32):
        return nc.dram_tensor(name, list(shape), dt, kind="ExternalInput").ap()

    xT = dram_in("xT", [D, L])
    gT = dram_in("gT", [128, DEPTH * 6 * NK])
    wgu32 = dram_in("wgu", [DEPTH * 2 * NJ, 128, 2 * NK * 128])
    wd32 = dram_in("wd", [DEPTH * 2 * NK, 128, NJ * 128])
    ident32 = dram_in("ident", [128, 128])
    shift32 = dram_in("shiftm", [128, 128])
    jvec32 = dram_in("jvec", [128, NT])
    csts32 = dram_in("csts", [128, 32])
    cinv32 = dram_in("cinv", [128, 2 * 4 * 8])
    w_in32 = dram_in("w_in", [2, 128, NK, D])
    w_out32 = dram_in("w_out", [2, 128, NK, D])
    w_glu32 = dram_in("w_glu", [2, 128, 4, 512])
    pool_w32 = dram_in("pool_w", [2, 128, 4, 128])
    pvec32 = dram_in("pvec", [2, 128, 12])
    s5a32 = dram_in("s5a", [2, 2, 3, 128, 32])
    s5b32 = dram_in("s5b", [2, 2, 4, 128, 512])
    wq32 = dram_in("wq", [2, 128, NK, D])
    wk32 = dram_in("wk", [2, 128, NK, D])
    wv32 = dram_in("wv", [2, 128, NK, D])
    wo32 = dram_in("wo", [2, 128, NK, D])
    rpbU32 = dram_in("rpbU", [2, 8, 128, 15 * 64])
    colmask32 = dram_in("colmask", [128, 64])
    outT = nc.dram_tensor("outT", [D, L], F32, kind="ExternalOutput").ap()

    hT = nc.dram_tensor("hT", [D, L], F32).ap()
    wgu16 = nc.dram_tensor("wgu16", [DEPTH * 2 * NJ, 128, 2 * NK * 128], BF16).ap()
    wd16 = nc.dram_tensor("wd16", [DEPTH * 2 * NK, 128, NJ * 128], BF16).ap()
    zT = nc.dram_tensor("zT", [D, L], F32).ap()
    ydT = nc.dram_tensor("ydT", [2, 512, L], F32).ap()
    qT = nc.dram_tensor("qT", [D, L], BF16).ap()
    kT = nc.dram_tensor("kT", [D, L], BF16).ap()
    vtok = nc.dram_tensor("vtok", [L, D], BF16).ap()

    import contextlib
    st = contextlib.ExitStack()
    with st:
        def sb(name, shape, dt):
            return st.enter_context(nc.sbuf_tensor(name, list(shape), dt))

        h_sb = sb("h_sb", [128, NK, NT], F32)
        f_sb = sb("f_sb", [128, NK, NT], F32)
        hn_sb = sb("hn_sb", [128, NK, NT], BF16)
        act_sb = sb("act_sb", [128, NJ, NT], BF16)
        NWGU = 3
        NWD = 2
        wgu_sb = [sb("wgu%d" % i, [128, 2 * NK * 128], BF16) for i in range(NWGU)]
        wd_sb = [sb("wd%d" % i, [128, NJ * 128], BF16) for i in range(NWD)]
        sq_sb = [sb("sq%d" % i, [128, NT], BF16) for i in range(2)]
        sg_sb = [sb("sg%d" % i, [128, NT], F32) for i in range(2)]
        rstd_sb = sb("rstd", [128, NT], F32)
        ones_sb = sb("ones", [128, 128], BF16)
        ident16 = sb("ident16", [128, 128], BF16)
        identf = sb("identf", [128, 128], F32)
        shiftf = sb("shiftf", [128, 128], F32)
        csts = sb("csts_sb", [128, 32], F32)
        g_sb = sb("g_sb", [128, DEPTH * 6 * NK], F32)
        gh_sb = sb("gh_sb", [128, DEPTH * 6 * NK], F32)
        mixbuf = sb("mixbuf", [128, NK, NT], F32)
        mixflat = mixbuf[:].rearrange("p k n -> p (k n)")
        mx0 = mixflat[:, 0:4096].bitcast(BF16).rearrange("p (k n) -> p k n", k=NK)
        parbuf = mixflat[:, 4096:8192]
        h_sb2 = mixbuf
        pt_buf = sb("pt_buf", [128, 3, 256], BF16)
        pvec = sb("pvec_sb", [128, 12], F32)
        cinv = sb("cinv_sb", [128, 64], F32)
        qrow = sb("qrow", [128, 2, NK, 64], BF16)
        carry = sb("carry", [128, 4], F32)
        recb = sb("recb", [128, 2, 64], F32)
        NPS = 4
        ps = [st.enter_context(nc.psum_tensor("ps%d" % i, [128, NT], F32)) for i in range(NPS)]

        R = Region
        r_h = [R("h%d" % k) for k in range(NK)]
        r_h2 = [R("h2_%d" % k) for k in range(NK)]
        r_f = [R("f%d" % k) for k in range(NK)]
        r_hn = [R("hn%d" % k) for k in range(NK)]
        r_act = [R("act%d" % j) for j in range(NJ)]
        r_wgu = [R("wgu%d" % i) for i in range(NWGU)]
        r_wd = [R("wd%d" % i) for i in range(NWD)]
        r_sq = [R("sq%d" % i) for i in range(2)]
        r_sg = [R("sg%d" % i) for i in range(2)]
        r_rstd = R("rstd")
        r_ps = [R("ps%d" % i) for i in range(NPS)]
        r_const = R("const")
        r_g = R("g")
        r_mx0 = R("mx0")
        r_par = R("par")
        r_smallw = R("smallw")
        r_pvec = R("pvec")
        r_qrow = [R("qrow0"), R("qrow1")]
        r_carry = R("carry")
        r_rec = [R("rec0"), R("rec1")]
        r_hT = [[R("hT%d_%d" % (t, k)) for k in range(NK)] for t in range(NTILES)]
        r_zT = [[R("zT%d_%d" % (t, k)) for k in range(NK)] for t in range(NTILES)]
        r_yd = [[[R("yd%d_%d_%d" % (d, k, c)) for c in range(NTILES)] for k in range(4)] for d in range(2)]
        r_qT = [R("qT%d" % t) for t in range(NTILES)]
        r_kT = [R("kT%d" % t) for t in range(NTILES)]
        r_vt = [R("vt%d" % t) for t in range(NTILES)]
        r_wgu16 = {}
        r_wd16 = {}

        state = {"ps": 0, "sq": 0, "sg": 0, "wgu": 0, "wd": 0}

        def rot(name, n):
            v = state[name]
            state[name] = (v + 1) % n
            return v

        P.op("dve", lambda e: e.memset(ones_sb[:], 1.0), writes=[r_const])
        P.dma("sp", lambda e: e.dma_start(out=g_sb[:], in_=gT), [], [r_g], key="g")
        P.op("dve", lambda e: e.tensor_scalar(gh_sb[:], g_sb[:], 0.5, None, ALU.mult), reads=[r_g], writes=[r_g])
        P.dma("sp", lambda e: e.dma_start(out=identf[:], in_=ident32), [], [r_const], key="c")
        P.dma("sp", lambda e: e.dma_start(out=shiftf[:], in_=shift32), [], [r_const], key="c")
        P.dma("sp", lambda e: e.dma_start(out=csts[:], in_=csts32), [], [r_const], key="c")
        P.dma("sp", lambda e: e.dma_start(out=cinv[:], in_=cinv32), [], [r_const], key="c")
        P.op("dve", lambda e: e.tensor_copy(ident16[:], identf[:]), reads=[r_const], writes=[r_const])

        cast_rr = {"i": 0}

        def cast_ffn_weights(fid):
            for j in range(NJ):
                kk = cast_rr["i"] % 8
                cast_rr["i"] += 1
                r_wgu16[(fid, j)] = R("wgu16_%d_%d" % (fid, j))
                P.dma("pool", lambda e, j=j: e.dma_start(out=wgu16[fid * NJ + j], in_=wgu32[fid * NJ + j]),
                      [], [r_wgu16[(fid, j)]], key="cast%d" % kk)
            for m in range(NK):
                kk = cast_rr["i"] % 8
                cast_rr["i"] += 1
                r_wd16[(fid, m)] = R("wd16_%d_%d" % (fid, m))
                P.dma("pool", lambda e, m=m: e.dma_start(out=wd16[fid * NK + m], in_=wd32[fid * NK + m]),
                      [], [r_wd16[(fid, m)]], key="cast%d" % kk)

        wstream = []
        wpos = {"issued": 0}
        wslot = {}
        slot_prev = {}

        def issue_weight_loads(pos):
            while wpos["issued"] < len(wstream) and wpos["issued"] <= pos + 4:
                q = wpos["issued"]
                kind, fid, idx = wstream[q]
                n = NWGU if kind == "gu" else NWD
                s = state["wgu" if kind == "gu" else "wd"]
                if slot_prev.get((kind, s), -1) >= pos:
                    break
                rot("wgu" if kind == "gu" else "wd", n)
                slot_prev[(kind, s)] = q
                wslot[q] = s
                if kind == "gu":
                    P.dma("sp", lambda e, s=s, fid=fid, idx=idx: e.dma_start(out=wgu_sb[s][:], in_=wgu16[fid * NJ + idx]),
                          [r_wgu16[(fid, idx)]], [r_wgu[s]], key="wgu%d" % s)
                else:
                    P.dma("sp", lambda e, s=s, fid=fid, idx=idx: e.dma_start(out=wd_sb[s][:], in_=wd16[fid * NK + idx]),
                          [r_wd16[(fid, idx)]], [r_wd[s]], key="wd%d" % s)
                wpos["issued"] += 1

        wcur = {"pos": 0}

        def next_weight(kind, fid, idx):
            pos = wcur["pos"]
            assert wstream[pos] == (kind, fid, idx), (wstream[pos], kind, fid, idx)
            issue_weight_loads(pos)
            assert pos in wslot
            wcur["pos"] += 1
            return wslot[pos]

        def gcol(layer, n, k, half=False):
            c = (layer * 6 + n) * NK + k
            tt = gh_sb if half else g_sb
            return tt[:, c:c + 1]

        def rms_stats(src_tiles, src_regs):
            pi = rot("ps", NPS)
            for k in range(NK):
                si = rot("sq", 2)
                P.op("act", lambda e, k=k, si=si: e.activation(out=sq_sb[si][:], in_=src_tiles[k], func=AF.Square),
                     reads=[src_regs[k]], writes=[r_sq[si]])
                for hf in range(NT // 512):
                    P.op("pe", lambda e, k=k, si=si, hf=hf, pi=pi: e.matmul(ps[pi][:, hf * 512:(hf + 1) * 512], ones_sb[:], sq_sb[si][:, hf * 512:(hf + 1) * 512],
                                                                            start=(k == 0), stop=(k == NK - 1)),
                         reads=[r_sq[si], r_const], writes=[r_ps[pi]])
            P.op("dve", lambda e, pi=pi: e.tensor_scalar(rstd_sb[:], ps[pi][:], 1.0 / D, EPS, ALU.mult, ALU.add),
                 reads=[r_ps[pi]], writes=[r_rstd])
            P.op("act", lambda e: e.activation(out=rstd_sb[:], in_=rstd_sb[:], func=AF.Sqrt), reads=[r_rstd], writes=[r_rstd])
            P.op("dve", lambda e: e.reciprocal(rstd_sb[:], rstd_sb[:]), reads=[r_rstd], writes=[r_rstd])

        def load_h(t, src, src_regs, hb=0):
            tok = slice(t * NT, (t + 1) * NT)
            hbuf = h_sb if hb == 0 else h_sb2
            rh = r_h if hb == 0 else r_h2
            for k in range(NK):
                P.dma("sp", lambda e, k=k: e.dma_start(out=hbuf[:, k, :], in_=src[k * 128:(k + 1) * 128, tok]),
                      [src_regs[t][k]], [rh[k]], key="hload%d" % k)

        def load_norm(layer, n, t, src, src_regs, hb=0):
            hbuf = h_sb if hb == 0 else h_sb2
            rh = r_h if hb == 0 else r_h2
            load_h(t, src, src_regs, hb)
            rms_stats([hbuf[:, k, :] for k in range(NK)], rh)
            for k in range(NK):
                P.op("dve", lambda e, k=k: e.scalar_tensor_tensor(out=hn_sb[:, k, :], in0=hbuf[:, k, :], scalar=gcol(layer, n, k),
                                                                  in1=rstd_sb[:], op0=ALU.mult, op1=ALU.mult),
                     reads=[rh[k], r_rstd, r_g], writes=[r_hn[k]])

        def proj8(wfn, wregs, rhs, rhs_regs, nk=NK):
            pi = rot("ps", NPS)
            for k in range(nk):
                for hf in range(NT // 512):
                    P.op("pe", lambda e, k=k, hf=hf, pi=pi: e.matmul(ps[pi][:, hf * 512:(hf + 1) * 512], wfn(k), rhs(k, hf),
                                                                     start=(k == 0), stop=(k == nk - 1)),
                         reads=list(wregs) + [rhs_regs[k]], writes=[r_ps[pi]])
            return pi

        def residual_store(layer, n, half, t, dst, dst_regs, hb=0):
            tok = slice(t * NT, (t + 1) * NT)
            hbuf = h_sb if hb == 0 else h_sb2
            rh = r_h if hb == 0 else r_h2
            rms_stats([f_sb[:, m, :] for m in range(NK)], r_f)
            stores = []
            for m in range(NK):
                P.op("dve", lambda e, m=m: e.scalar_tensor_tensor(out=f_sb[:, m, :], in0=f_sb[:, m, :], scalar=gcol(layer, n, m, half=half),
                                                                  in1=rstd_sb[:], op0=ALU.mult, op1=ALU.mult),
                     reads=[r_f[m], r_rstd, r_g], writes=[r_f[m]])
                eng_ = "pool" if m % 3 == 2 else "dve"
                P.op(eng_, lambda e, m=m: e.tensor_tensor(f_sb[:, m, :], f_sb[:, m, :], hbuf[:, m, :], ALU.add),
                     reads=[r_f[m], rh[m]], writes=[r_f[m]])
                o = P.dma("pool", lambda e, m=m: e.dma_start(out=dst[m * 128:(m + 1) * 128, tok], in_=f_sb[:, m, :]),
                          [r_f[m]], [dst_regs[t][m]], key="hstore%d" % m)
                stores.append(o)
            return stores

        evac_rr = {"i": 0}

        def evac_copy(out_ap, in_ap, reads, writes, scale=None):
            evac_rr["i"] += 1
            if evac_rr["i"] % 2 == 0:
                if scale is None:
                    P.op("dve", lambda e: e.tensor_copy(out_ap, in_ap), reads=reads, writes=writes)
                else:
                    P.op("dve", lambda e: e.tensor_scalar(out_ap, in_ap, scale, None, ALU.mult), reads=reads, writes=writes)
            else:
                if scale is None:
                    P.op("act", lambda e: e.activation(out=out_ap, in_=in_ap, func=AF.Copy), reads=reads, writes=writes)
                else:
                    P.op("act", lambda e: e.activation(out=out_ap, in_=in_ap, func=AF.Copy, scale=scale), reads=reads, writes=writes)

        def ffn_step(layer, f, src, src_regs, dst, dst_regs):
            fid = layer * 2 + f
            n0 = 0 if f == 0 else 4
            stores = []
            load_norm(layer, n0, 0, src, src_regs, hb=0)
            for t in range(ntiles):
                hb = t % 2
                for j in range(NJ):
                    s = next_weight("gu", fid, j)
                    pg = rot("ps", NPS)
                    pu = rot("ps", NPS)
                    for gu, pi in ((0, pg), (1, pu)):
                        for k in range(NK):
                            for hf in range(NT // 512):
                                P.op("pe", lambda e, s=s, gu=gu, k=k, hf=hf, pi=pi: e.matmul(
                                    ps[pi][:, hf * 512:(hf + 1) * 512],
                                    wgu_sb[s][:, (gu * NK + k) * 128:(gu * NK + k + 1) * 128],
                                    hn_sb[:, k, hf * 512:(hf + 1) * 512], start=(k == 0), stop=(k == NK - 1)),
                                     reads=[r_wgu[s], r_hn[k]], writes=[r_ps[pi]])
                    gi = rot("sg", 2)
                    P.op("act", lambda e, gi=gi, pg=pg: e.activation(out=sg_sb[gi][:], in_=ps[pg][:], func=AF.Silu),
                         reads=[r_ps[pg]], writes=[r_sg[gi]])
                    P.op("dve", lambda e, gi=gi, pu=pu, j=j: e.tensor_tensor(act_sb[:, j, :], sg_sb[gi][:], ps[pu][:], ALU.mult),
                         reads=[r_sg[gi], r_ps[pu]], writes=[r_act[j]])
                for m in range(NK):
                    s = next_weight("d", fid, m)
                    pi = rot("ps", NPS)
                    for j in range(NJ):
                        for hf in range(NT // 512):
                            P.op("pe", lambda e, s=s, j=j, hf=hf, pi=pi, m=m: e.matmul(
                                ps[pi][:, hf * 512:(hf + 1) * 512], wd_sb[s][:, j * 128:(j + 1) * 128],
                                act_sb[:, j, hf * 512:(hf + 1) * 512], start=(j == 0), stop=(j == NJ - 1)),
                                 reads=[r_wd[s], r_act[j]], writes=[r_ps[pi]])
                    evac_copy(f_sb[:, m, :], ps[pi][:], [r_ps[pi]], [r_f[m]])
                    if m == 2 and t + 1 < ntiles:
                        load_norm(layer, n0, t + 1, src, src_regs, hb=(t + 1) % 2)
                stores += residual_store(layer, n0 + 1, True, t, dst, dst_regs, hb=hb)
            return stores

        def mixer_even(layer, src, src_regs, dst, dst_regs):
            i = layer // 2
            assert ntiles == NTILES, "even mixer needs the full sequence"
            P.barrier()
            P.dma("pool", lambda e: e.dma_start(out=mx0[:], in_=w_in32[i]), [], [r_mx0], key="mx0")
            for t in range(ntiles):
                tok = slice(t * NT, (t + 1) * NT)
                load_norm(layer, 2, t, src, src_regs)
                for o in range(NK):
                    pi = proj8(lambda k, o=o: mx0[:, k, o * 128:(o + 1) * 128], [r_mx0],
                               lambda k, hf: hn_sb[:, k, hf * 512:(hf + 1) * 512], r_hn)
                    evac_copy(f_sb[:, o, :], ps[pi][:], [r_ps[pi]], [r_f[o]])
                    P.dma("pool", lambda e, o=o, tok=tok: e.dma_start(out=zT[o * 128:(o + 1) * 128, tok], in_=f_sb[:, o, :]),
                          [r_f[o]], [r_zT[t][o]], key="zst%d" % o)
            P.barrier()
            Ysb = h_sb[:].rearrange("p k n -> p (k n)")
            u16 = hn_sb[:].rearrange("p k n -> p (k n)")
            pb = parbuf
            c_B1, c_B2, c_C1, c_C2 = 0, 512, 1024, 1536
            c_sm = 2048
            c_bbT = 2560
            c_MT = 3072
            c_pad = 3328

            def sm(ix):
                return pb[:, c_sm + ix * 32: c_sm + (ix + 1) * 32]
            ARE, AIM, DT, MAG, ANGN, LRE, LIM, FRE, FIM, TA, TB, CT, ST, DEN, TC, TD = range(16)
            KI = sg_sb[0][:].bitcast(I32)
            Ubuf = [sg_sb[1], rstd_sb]
            rU = [r_sg[1], r_rstd]
            rKI = r_sg[0]

            def dv(fn, reads=(r_par,), writes=(r_par,)):
                P.op("dve", fn, reads=list(reads), writes=list(writes))

            def ac(fn, reads=(r_par,), writes=(r_par,)):
                P.op("act", fn, reads=list(reads), writes=list(writes))

            def sincos_small(dst_sin, dst_cos, ang_over_2pi):
                for dst, off in ((dst_sin, 0.0), (dst_cos, 0.25)):
                    dv(lambda e, off=off: e.tensor_scalar(sm(TC), ang_over_2pi, 1.0, off, ALU.mult, ALU.add))
                    dv(lambda e: e.tensor_copy(KI[:, 0:32], sm(TC)), writes=[r_par, rKI])
                    dv(lambda e: e.tensor_copy(sm(TD), KI[:, 0:32]), reads=[r_par, rKI])
                    dv(lambda e: e.tensor_tensor(sm(TC), sm(TC), sm(TD), ALU.subtract))
                    ac(lambda e, dst=dst: e.activation(out=dst, in_=sm(TC), func=AF.Sin, scale=2 * PI))

            for d in range(2):
                P.dma("sp", lambda e, d=d: e.dma_start(out=sm(ARE), in_=s5a32[i, d, 0]), [], [r_par], key="par")
                P.dma("sp", lambda e, d=d: e.dma_start(out=sm(AIM), in_=s5a32[i, d, 1]), [], [r_par], key="par")
                P.dma("sp", lambda e, d=d: e.dma_start(out=sm(DT), in_=s5a32[i, d, 2]), [], [r_par], key="par")
                for q_, c_ in enumerate((c_B1, c_B2, c_C1, c_C2)):
                    P.dma("sp", lambda e, q_=q_, c_=c_, d=d: e.dma_start(out=pb[:, c_:c_ + 512], in_=s5b32[i, d, q_]), [], [r_par], key="par")
                ac(lambda e: e.activation(out=sm(DT), in_=sm(DT), func=AF.Exp))
                dv(lambda e: e.tensor_scalar(sm(ARE), sm(ARE), -1e-4, None, ALU.min))
                dv(lambda e: e.tensor_tensor(sm(TA), sm(ARE), sm(DT), ALU.mult))
                ac(lambda e: e.activation(out=sm(MAG), in_=sm(TA), func=AF.Exp))
                dv(lambda e: e.tensor_tensor(sm(ANGN), sm(AIM), sm(DT), ALU.mult))
                dv(lambda e: e.tensor_scalar(sm(ANGN), sm(ANGN), 1.0 / (2 * PI), None, ALU.mult))
                sincos_small(sm(LIM), sm(LRE), sm(ANGN))
                dv(lambda e: e.tensor_tensor(sm(LRE), sm(LRE), sm(MAG), ALU.mult))
                dv(lambda e: e.tensor_tensor(sm(LIM), sm(LIM), sm(MAG), ALU.mult))
                dv(lambda e: e.tensor_scalar(sm(TA), sm(ANGN), float(NT), None, ALU.mult))
                sincos_small(sm(ST), sm(CT), sm(TA))
                dv(lambda e: e.tensor_scalar(sm(ST), sm(ST), csts[:, 0:1], None, ALU.mult), reads=[r_par, r_const])
                dv(lambda e: e.tensor_scalar(sm(LRE), sm(LRE), -1.0, None, ALU.add))
                dv(lambda e: e.tensor_tensor(sm(DEN), sm(ARE), sm(ARE), ALU.mult))
                dv(lambda e: e.tensor_tensor(sm(TA), sm(AIM), sm(AIM), ALU.mult))
                dv(lambda e: e.tensor_tensor(sm(DEN), sm(DEN), sm(TA), ALU.add))
                dv(lambda e: e.reciprocal(sm(DEN), sm(DEN)))
                dv(lambda e: e.tensor_tensor(sm(TA), sm(LRE), sm(ARE), ALU.mult))
                dv(lambda e: e.tensor_tensor(sm(TB), sm(LIM), sm(AIM), ALU.mult))
                dv(lambda e: e.tensor_tensor(sm(TA), sm(TA), sm(TB), ALU.add))
                dv(lambda e: e.tensor_tensor(sm(FRE), sm(TA), sm(DEN), ALU.mult))
                dv(lambda e: e.tensor_tensor(sm(TA), sm(LIM), sm(ARE), ALU.mult))
                dv(lambda e: e.tensor_tensor(sm(TB), sm(LRE), sm(AIM), ALU.mult))
                dv(lambda e: e.tensor_tensor(sm(TA), sm(TA), sm(TB), ALU.subtract))
                dv(lambda e: e.tensor_tensor(sm(FIM), sm(TA), sm(DEN), ALU.mult))
                dv(lambda e: e.tensor_scalar(sm(FIM), sm(FIM), csts[:, 1:2], None, ALU.mult), reads=[r_par, r_const])
                B1v = pb[:, c_B1:c_B1 + 512].rearrange("p (g h) -> p g h", h=16)
                B2v = pb[:, c_B2:c_B2 + 512].rearrange("p (g h) -> p g h", h=16)
                dv(lambda e: e.tensor_tensor(B1v, B1v, sm(FRE).unsqueeze(2).to_broadcast([128, 32, 16]), ALU.mult))
                dv(lambda e: e.tensor_tensor(B2v, B2v, sm(FIM).unsqueeze(2).to_broadcast([128, 32, 16]), ALU.mult))
                dv(lambda e: e.tensor_tensor(pb[:, c_B1:c_B1 + 512], pb[:, c_B1:c_B1 + 512], pb[:, c_B2:c_B2 + 512], ALU.add))
                dv(lambda e: e.tensor_scalar(pb[:, c_C1:c_C1 + 512], pb[:, c_C1:c_C1 + 512], csts[:, 0:1], None, ALU.mult), reads=[r_par, r_const])
                dv(lambda e: e.tensor_scalar(pb[:, c_C2:c_C2 + 512], pb[:, c_C2:c_C2 + 512], -1.0, None, ALU.mult))
                for gt in range(4):
                    pi = rot("ps", 3)
                    P.op("pe", lambda e, gt=gt, pi=pi: e.matmul(ps[pi][:, 0:128], pb[:, c_B1 + gt * 128: c_B1 + (gt + 1) * 128], identf[:], start=True, stop=True),
                         reads=[r_par, r_const], writes=[r_ps[pi]])
                    dv(lambda e, gt=gt, pi=pi: e.tensor_copy(pb[:, c_bbT + gt * 128: c_bbT + (gt + 1) * 128], ps[pi][:, 0:128]),
                       reads=[r_par, r_ps[pi]])
                kbuf2 = act_sb[:, 14:16, :].rearrange("p a b -> p (a b)").bitcast(F32)
                kbufs = [(sg_sb[0][:], [rKI]), (kbuf2, [r_act[14], r_act[15]])]

                def prep_g(gt, g8, stage=None):
                    g = gt * 8 + g8
                    par = g % 2
                    tabs = []
                    for which in range(2):
                        a_ = 6 + 4 * par + 2 * which
                        tabs.append((act_sb[:, a_:a_ + 2, :].rearrange("p a b -> p (a b)").bitcast(F32), [r_act[a_], r_act[a_ + 1]]))
                    (sinT, r_sin), (cosT, r_cos) = tabs
                    for st_ in ((0, 1, 2, 3) if stage is None else (stage,)):
                        if st_ > 3:
                            continue
                        for wi, (tab, rtab, off) in enumerate(((sinT, r_sin, 0.0), (cosT, r_cos, 0.25))):
                            ub = Ubuf[wi]
                            rub = rU[wi]
                            kb, rkb = kbufs[wi]
                            bcol = csts[:, 19:20] if off == 0.0 else csts[:, 18:19]
                            if st_ == 0:
                                P.op("act", lambda e, ub=ub, bcol=bcol: e.activation(out=ub[:], in_=jvec_sb[:], func=AF.Identity, scale=sm(ANGN)[:, g:g + 1], bias=bcol),
                                     reads=[r_par, r_const], writes=[rub])
                            elif st_ == 1:
                                P.op("dve", lambda e, ub=ub, kb=kb: e.tensor_scalar(kb, ub[:], 12582912.0, 12582912.0, ALU.add, ALU.subtract), reads=[rub], writes=rkb)
                            elif st_ == 2:
                                P.op("pool", lambda e, ub=ub, kb=kb: e.tensor_tensor(ub[:], ub[:], kb, ALU.subtract), reads=[rub] + rkb, writes=[rub])
                            else:
                                P.op("act", lambda e, ub=ub, tab=tab: e.activation(out=tab, in_=ub[:], func=AF.Sin, scale=2 * PI),
                                     reads=[rub], writes=rtab)
                    if stage is not None and stage != 4:
                        return None
                    padb = act_sb[:, 4 + par, :]
                    r_pad = r_act[4 + par]
                    BBp = padb[:, 0:128]
                    BBs = padb[:, 128:256]
                    C1p = padb[:, 256:384]
                    C2p = padb[:, 384:512]
                    bbT = pb[:, c_bbT + gt * 128: c_bbT + (gt + 1) * 128]
                    gm = csts[:, 2 + g8: 3 + g8]
                    ngm = csts[:, 10 + g8: 11 + g8]
                    P.op("act", lambda e: e.activation(out=BBp, in_=bbT, func=AF.Identity, scale=gm), reads=[r_par, r_const], writes=[r_pad])
                    P.op("act", lambda e: e.activation(out=BBs[:, 0:64], in_=bbT[:, 64:128], func=AF.Identity, scale=gm), reads=[r_par, r_const], writes=[r_pad])
                    P.op("act", lambda e: e.activation(out=BBs[:, 64:128], in_=bbT[:, 0:64], func=AF.Identity, scale=ngm), reads=[r_par, r_const], writes=[r_pad])
                    P.op("act", lambda e: e.activation(out=padb[:, 256:512], in_=pb[:, c_B1:c_B1 + 256], func=AF.Copy, scale=0.0), reads=[r_par], writes=[r_pad])
                    P.op("act", lambda e: e.activation(out=C1p[:, g8 * 16:(g8 + 1) * 16], in_=pb[:, c_C1 + g * 16: c_C1 + (g + 1) * 16], func=AF.Copy), reads=[r_par], writes=[r_pad])
                    P.op("act", lambda e: e.activation(out=C2p[:, g8 * 16:(g8 + 1) * 16], in_=pb[:, c_C2 + g * 16: c_C2 + (g + 1) * 16], func=AF.Copy), reads=[r_par], writes=[r_pad])
                    MT = pb[:, c_MT + par * 128: c_MT + (par + 1) * 128]
                    r_MT = r_mt[par]
                    P.op("dve", lambda e: e.tensor_scalar(MT, identf[:], sm(CT)[:, g:g + 1], None, ALU.mult), reads=[r_par, r_const], writes=[r_MT])
                    P.op("dve", lambda e: e.scalar_tensor_tensor(out=MT, in0=shiftf[:], scalar=sm(ST)[:, g:g + 1], in1=MT, op0=ALU.mult, op1=ALU.add),
                         reads=[r_par, r_const, r_MT], writes=[r_MT])
                    return dict(cos=cosT, sin=sinT, r_cos=r_cos, r_sin=r_sin, BBp=BBp, BBs=BBs, C1p=C1p, C2p=C2p, r_pad=r_pad, MT=MT, r_MT=r_MT)

                nxt_first = None
                for gt in range(4):
                    for k in range(NK):
                        P.dma("pool", lambda e, gt=gt, k=k: e.dma_start(out=hn_sb[:, k, :], in_=zT[512 + gt * 128: 512 + (gt + 1) * 128, k * NT:(k + 1) * NT]),
                              [r_zT[k][4 + gt]], [r_hn[k]], key="uld%d" % k)
                    its = []
                    for g8 in range(8):
                        for ci in range(ntiles):
                            its.append((g8, ci))
                    curs = {}
                    if gt == 0:
                        curs[0] = prep_g(0, 0)
                    else:
                        curs[0] = nxt_first
                    PA, PB, PY, PC = 0, 1, 2, 3

                    def S1pe(n):
                        g8, ci = its[n]
                        cur = curs[g8]
                        c = ci if d == 0 else ntiles - 1 - ci
                        t0 = c * NT
                        for lh, pi in ((cur["BBp"], PA), (cur["BBs"], PB)):
                            for hf in range(2):
                                P.op("pe", lambda e, lh=lh, pi=pi, hf=hf, t0=t0: e.matmul(ps[pi][:, hf * 512:(hf + 1) * 512], lh, u16[:, t0 + hf * 512: t0 + (hf + 1) * 512], start=True, stop=True),
                                     reads=[cur["r_pad"], r_hn[c]], writes=[r_ps[pi]])

                    def S1v(n):
                        g8, ci = its[n]
                        cur = curs[g8]
                        q2 = n % 2
                        Wa = f_sb[:, 0 + q2, :]
                        Wb = f_sb[:, 2 + q2, :]
                        A_in = ps[PA][:, :] if d == 0 else ps[PA][:, ::-1]
                        B_in = ps[PB][:, :] if d == 0 else ps[PB][:, ::-1]
                        P.op("dve", lambda e: e.tensor_tensor(Wa, cur["cos"], A_in, ALU.mult), reads=cur["r_cos"] + [r_ps[PA]], writes=[r_f[0 + q2]])
                        P.op("dve", lambda e: e.tensor_tensor(ps[PY][:, :], cur["sin"], B_in, ALU.mult), reads=cur["r_sin"] + [r_ps[PB]], writes=[r_ps[PY]])
                        P.op("dve", lambda e: e.tensor_tensor(Wa, Wa, ps[PY][:, :], ALU.add), reads=[r_f[0 + q2], r_ps[PY]], writes=[r_f[0 + q2]])

                    def S2(n):
                        g8, ci = its[n]
                        cur = curs[g8]
                        g = gt * 8 + g8
                        q2 = n % 2
                        Wa = f_sb[:, 0 + q2, :]
                        Wc = f_sb[:, 4 + q2, :]
                        rcol = sm(MAG)[:, g:g + 1]
                        init = 0.0 if ci == 0 else carry[:, 0:1]
                        P.op("dve", lambda e: e.tensor_tensor_scan(Wc, rcol.to_broadcast([128, NT]), Wa, init, ALU.mult, ALU.add),
                             reads=[r_f[0 + q2], r_par, r_carry], writes=[r_f[4 + q2]])
                        if ci < ntiles - 1:
                            P.op("pe", lambda e: e.matmul(ps[PC][:, 0:1], cur["MT"], Wc[:, NT - 1:NT], start=True, stop=True),
                                 reads=[cur["r_MT"], r_f[4 + q2]], writes=[r_ps[PC]])
                            P.op("act", lambda e: e.activation(out=carry[:, 0:1], in_=ps[PC][:, 0:1], func=AF.Copy), reads=[r_ps[PC]], writes=[r_carry])

                    def S3pool(n):
                        g8, ci = its[n]
                        cur = curs[g8]
                        q2 = n % 2
                        Wc = f_sb[:, 4 + q2, :]
                        p1 = act_sb[:, 0 + q2, :]
                        p2 = act_sb[:, 2 + q2, :]
                        p1o = p1 if d == 0 else p1[:, ::-1]
                        p2o = p2 if d == 0 else p2[:, ::-1]
                        P.op("pool", lambda e: e.tensor_tensor(p1o, cur["cos"], Wc, ALU.mult), reads=cur["r_cos"] + [r_f[4 + q2]], writes=[r_act[0 + q2]])
                        P.op("pool", lambda e: e.tensor_tensor(p2o, cur["sin"], Wc, ALU.mult), reads=cur["r_sin"] + [r_f[4 + q2]], writes=[r_act[2 + q2]])

                    def S3pe(n):
                        g8, ci = its[n]
                        cur = curs[g8]
                        q2 = n % 2
                        p1 = act_sb[:, 0 + q2, :]
                        p2 = act_sb[:, 2 + q2, :]
                        for hf in range(2):
                            P.op("pe", lambda e, hf=hf: e.matmul(ps[PY][:, hf * 512:(hf + 1) * 512], cur["C1p"], p1[:, hf * 512:(hf + 1) * 512], start=True, stop=False),
                                 reads=[cur["r_pad"], r_act[0 + q2]], writes=[r_ps[PY]])
                            P.op("pe", lambda e, hf=hf: e.matmul(ps[PY][:, hf * 512:(hf + 1) * 512], cur["C2p"], p2[:, hf * 512:(hf + 1) * 512], start=False, stop=True),
                                 reads=[cur["r_pad"], r_act[2 + q2]], writes=[r_ps[PY]])

                    def S4(n):
                        g8, ci = its[n]
                        c = ci if d == 0 else ntiles - 1 - ci
                        t0 = c * NT
                        q2 = n % 2
                        Yst = f_sb[:, 6 + q2, :]
                        P.op("act", lambda e: e.activation(out=Yst, in_=ps[PY][:], func=AF.Copy), reads=[r_ps[PY]], writes=[r_f[6 + q2]])
                        row0 = gt * 128 + g8 * 16
                        P.dma("sp", lambda e, d=d: e.dma_start(out=ydT[d, row0:row0 + 16, t0:t0 + NT], in_=Yst[g8 * 16:(g8 + 1) * 16, :]),
                              [r_f[6 + q2]], [r_yd[d][gt][c]], key="yst%d" % q2)

                    NI = len(its)
                    S1pe(0)
                    S1v(0)
                    S1pe(1)
                    for n in range(NI + 1):
                        if n < NI:
                            g8, ci = its[n]
                            if 1 <= ci <= 5 and not (g8 == 7 and gt == 3):
                                ngt, ng8 = (gt, g8 + 1) if g8 < 7 else (gt + 1, 0)
                                res_ = prep_g(ngt, ng8, stage=ci - 1)
                                if ci == 5:
                                    if g8 < 7:
                                        curs[g8 + 1] = res_
                                    else:
                                        nxt_first = res_
                        if n + 1 < NI:
                            S1v(n + 1)
                        if n + 2 < NI:
                            S1pe(n + 2)
                        if n >= 1:
                            S3pe(n - 1)
                            S4(n - 1)
                        if n < NI:
                            S2(n)
                            S3pool(n)
            P.barrier()
            P.dma("pool", lambda e: e.dma_start(out=mx0[:], in_=w_out32[i]), [], [r_mx0], key="mx0")
            smallw = parbuf[:].bitcast(BF16)
            wglu16 = smallw[:, 0:2048].rearrange("p (k c) -> p k c", k=4)
            poolw16 = smallw[:, 2048:2560].rearrange("p (k c) -> p k c", k=4)
            P.dma("pool", lambda e: e.dma_start(out=wglu16, in_=w_glu32[i]), [], [r_smallw], key="sw0")
            P.dma("pool", lambda e: e.dma_start(out=poolw16, in_=pool_w32[i]), [], [r_smallw], key="sw1")
            P.dma("sp", lambda e: e.dma_start(out=pvec[:], in_=pvec32[i]), [], [r_pvec], key="pvec")
            fflat = f_sb[:].rearrange("p k n -> p (k n)")
            NP = NT + 16
            r_zp = R("zp")
            WIN = (2, 4, 8, 16)
            stores_all = []
            for t in range(ntiles):
                tok = slice(t * NT, (t + 1) * NT)
                t0 = t * NT
                load_h(t, src, src_regs)
                zp = [fflat[:, gi * NP:(gi + 1) * NP] for gi in range(4)]
                wk_ = [fflat[:, (4 + q_) * NP:(5 + q_) * NP] for q_ in range(3)]
                lo = max(t0 - 8, 0)
                hi = min(t0 + NT + 8, L)
                for gi in range(4):
                    if t == 0:
                        P.op("pool", lambda e, gi=gi: e.memset(zp[gi][:, 0:8], 0.0), reads=[], writes=[r_zp] + r_f)
                    if t == NTILES - 1:
                        P.op("pool", lambda e, gi=gi: e.memset(zp[gi][:, NT + 8:NT + 16], 0.0), reads=[], writes=[r_zp] + r_f)
                    rr = [r_zT[tt][gi] for tt in range(max(t - 1, 0), min(t + 2, NTILES))]
                    P.dma("sp", lambda e, gi=gi, lo=lo, hi=hi, t0=t0: e.dma_start(out=zp[gi][:, lo - (t0 - 8): hi - (t0 - 8)], in_=zT[gi * 128:(gi + 1) * 128, lo:hi]),
                          rr, [r_zp] + r_f, key="zpl%d" % gi)
                for gi in range(4):
                    w = WIN[gi]
                    z = zp[gi]
                    a2, a4, a8 = wk_
                    pr_ = act_sb[:, gi, :]
                    if gi == 0:
                        P.op("dve", lambda e, z=z, a2=a2: e.tensor_tensor(a2[:, 0:NT], z[:, 7:7 + NT], z[:, 8:8 + NT], ALU.add), reads=[r_zp], writes=[r_zp])
                        s_ = a2[:, 0:NT]
                    else:
                        P.op("dve", lambda e, z=z, a2=a2: e.tensor_tensor(a2[:, 0:NP - 1], z[:, 0:NP - 1], z[:, 1:NP], ALU.add), reads=[r_zp], writes=[r_zp])
                        if gi == 1:
                            P.op("dve", lambda e, a2=a2, a4=a4: e.tensor_tensor(a4[:, 0:NT], a2[:, 6:6 + NT], a2[:, 8:8 + NT], ALU.add), reads=[r_zp], writes=[r_zp])
                            s_ = a4[:, 0:NT]
                        else:
                            P.op("dve", lambda e, a2=a2, a4=a4: e.tensor_tensor(a4[:, 0:NP - 3], a2[:, 0:NP - 3], a2[:, 2:NP - 1], ALU.add), reads=[r_zp], writes=[r_zp])
                            if gi == 2:
                                P.op("dve", lambda e, a4=a4, a8=a8: e.tensor_tensor(a8[:, 0:NT], a4[:, 4:4 + NT], a4[:, 8:8 + NT], ALU.add), reads=[r_zp], writes=[r_zp])
                                s_ = a8[:, 0:NT]
                            else:
                                P.op("dve", lambda e, a4=a4, a8=a8: e.tensor_tensor(a8[:, 0:NP - 7], a4[:, 0:NP - 7], a4[:, 4:NP - 3], ALU.add), reads=[r_zp], writes=[r_zp])
                                P.op("dve", lambda e, a2=a2, a8=a8: e.tensor_tensor(a2[:, 0:NT], a8[:, 0:NT], a8[:, 8:8 + NT], ALU.add), reads=[r_zp], writes=[r_zp])
                                s_ = a2[:, 0:NT]
                    P.op("dve", lambda e, s_=s_, z=z, pr_=pr_, w=w: e.scalar_tensor_tensor(out=pr_, in0=s_, scalar=1.0 / w, in1=z[:, 8:8 + NT], op0=ALU.mult, op1=ALU.subtract),
                         reads=[r_zp], writes=[r_act[gi]])
                    if t == 0:
                        P.op("dve", lambda e, s_=s_, gi=gi: e.tensor_tensor(s_[:, 0:8], s_[:, 0:8], cinv[:, gi * 8:(gi + 1) * 8], ALU.mult), reads=[r_zp, r_const], writes=[r_zp])
                        P.op("dve", lambda e, s_=s_, z=z, pr_=pr_: e.tensor_tensor(pr_[:, 0:8], s_[:, 0:8], z[:, 8:16], ALU.subtract), reads=[r_zp, r_act[gi]], writes=[r_act[gi]])
                    if t == NTILES - 1:
                        P.op("dve", lambda e, s_=s_, gi=gi: e.tensor_tensor(s_[:, NT - 8:NT], s_[:, NT - 8:NT], cinv[:, 32 + gi * 8: 32 + (gi + 1) * 8], ALU.mult), reads=[r_zp, r_const], writes=[r_zp])
                        P.op("dve", lambda e, s_=s_, z=z, pr_=pr_: e.tensor_tensor(pr_[:, NT - 8:NT], s_[:, NT - 8:NT], z[:, NT:NT + 8], ALU.subtract), reads=[r_zp, r_act[gi]], writes=[r_act[gi]])
                    pi = rot("ps", NPS)
                    for hf in range(2):
                        P.op("pe", lambda e, gi=gi, hf=hf, pi=pi, pr_=pr_: e.matmul(ps[pi][:, hf * 512:(hf + 1) * 512], poolw16[:, gi, :], pr_[:, hf * 512:(hf + 1) * 512], start=True, stop=True),
                             reads=[r_smallw, r_act[gi]], writes=[r_ps[pi]])
                    P.op("act", lambda e, gi=gi, pi=pi: e.activation(out=act_sb[:, 8 + gi, :], in_=ps[pi][:], func=AF.Identity, scale=pvec[:, gi:gi + 1]),
                         reads=[r_ps[pi], r_pvec], writes=[r_act[8 + gi]])
                ygf = hn_sb[:].rearrange("p k n -> p (k n)").bitcast(F32).rearrange("p (k n) -> p k n", k=4)
                for kt in range(4):
                    A_, B_, C_ = sg_sb[0], sg_sb[1], rstd_sb
                    P.dma("sp", lambda e, kt=kt, tok=tok: e.dma_start(out=A_[:], in_=ydT[0, kt * 128:(kt + 1) * 128, tok]), [r_yd[0][kt][t]], [r_sg[0]], key="ya")
                    P.dma("sp", lambda e, kt=kt, tok=tok: e.dma_start(out=B_[:], in_=ydT[1, kt * 128:(kt + 1) * 128, tok]), [r_yd[1][kt][t]], [r_sg[1]], key="yb")
                    P.dma("sp", lambda e, kt=kt, tok=tok: e.dma_start(out=C_[:], in_=zT[512 + kt * 128: 512 + (kt + 1) * 128, tok]), [r_zT[t][4 + kt]], [r_rstd], key="yu")
                    P.op("pool", lambda e: e.tensor_tensor(A_[:], A_[:], B_[:], ALU.add), reads=[r_sg[0], r_sg[1]], writes=[r_sg[0]])
                    P.op("dve", lambda e, kt=kt: e.scalar_tensor_tensor(out=A_[:], in0=C_[:], scalar=pvec[:, 4 + kt: 5 + kt], in1=A_[:], op0=ALU.mult, op1=ALU.add),
                         reads=[r_sg[0], r_rstd, r_pvec], writes=[r_sg[0]])
                    P.op("act", lambda e: e.activation(out=B_[:], in_=A_[:], func=AF.Square), reads=[r_sg[0]], writes=[r_sg[1]])
                    P.op("dve", lambda e: e.tensor_scalar(B_[:], B_[:], 0.044715, 1.0, ALU.mult, ALU.add), reads=[r_sg[1]], writes=[r_sg[1]])
                    P.op("dve", lambda e: e.tensor_tensor(B_[:], B_[:], A_[:], ALU.mult), reads=[r_sg[0], r_sg[1]], writes=[r_sg[1]])
                    P.op("act", lambda e: e.activation(out=B_[:], in_=B_[:], func=AF.Sigmoid, scale=2.0 * math.sqrt(2.0 / PI)), reads=[r_sg[1]], writes=[r_sg[1]])
                    P.op("dve", lambda e, kt=kt: e.tensor_tensor(ygf[:, kt, :], A_[:], B_[:], ALU.mult), reads=[r_sg[0], r_sg[1]], writes=[r_hn[2 * kt], r_hn[2 * kt + 1]])
                    P.op("act", lambda e, kt=kt: e.activation(out=act_sb[:, 4 + kt, :], in_=ygf[:, kt, :], func=AF.Copy), reads=[r_hn[2 * kt], r_hn[2 * kt + 1]], writes=[r_act[4 + kt]])
                for o in range(4):
                    pi = proj8(lambda k, o=o: wglu16[:, k, o * 128:(o + 1) * 128], [r_smallw],
                               lambda k, hf: act_sb[:, 4 + k, hf * 512:(hf + 1) * 512], r_act[4:8], nk=4)
                    P.op("act", lambda e, o=o, pi=pi: e.activation(out=sg_sb[1][:], in_=ps[pi][:], func=AF.Sigmoid, bias=pvec[:, 8 + o: 9 + o]),
                         reads=[r_ps[pi], r_pvec], writes=[r_sg[1]])
                    P.op("dve", lambda e, o=o: e.tensor_tensor(act_sb[:, 12 + o, :], ygf[:, o, :], sg_sb[1][:], ALU.mult),
                         reads=[r_sg[1], r_hn[2 * o], r_hn[2 * o + 1]], writes=[r_act[12 + o]])
                for o in range(NK):
                    pi = proj8(lambda k, o=o: mx0[:, k, o * 128:(o + 1) * 128], [r_mx0],
                               lambda k, hf: act_sb[:, 8 + k, hf * 512:(hf + 1) * 512], r_act[8:16])
                    evac_copy(f_sb[:, o, :], ps[pi][:], [r_ps[pi], r_zp], [r_f[o], r_zp])
                stores_all += residual_store(layer, 3, False, t, dst, dst_regs)
            return stores_all

        def mixer_odd(layer, src, src_regs, dst, dst_regs):
            i = layer // 2
            P.barrier()
            wq16 = act_sb[:, 0:8, :]
            wk16 = act_sb[:, 8:16, :]
            r_wq = R("wq")
            r_wk = R("wk")
            P.dma("pool", lambda e: e.dma_start(out=wq16, in_=wq32[i]), [], [r_wq], key="mxq")
            P.dma("pool", lambda e: e.dma_start(out=wk16, in_=wk32[i]), [], [r_wk], key="mxk")
            P.dma("pool", lambda e: e.dma_start(out=mx0[:], in_=wv32[i]), [], [r_mx0], key="mx0")
            for t in range(ntiles):
                tok = slice(t * NT, (t + 1) * NT)
                load_norm(layer, 2, t, src, src_regs)
                for (w16, rw, dstT, rd, sc) in ((wq16, r_wq, qT, r_qT, 0.125), (wk16, r_wk, kT, r_kT, None)):
                    for o in range(NK):
                        pi = proj8(lambda k, o=o, w16=w16: w16[:, k, o * 128:(o + 1) * 128], [rw],
                                   lambda k, hf: hn_sb[:, k, hf * 512:(hf + 1) * 512], r_hn)
                        si = rot("sq", 2)
                        evac_copy(sq_sb[si][:], ps[pi][:], [r_ps[pi]], [r_sq[si]], scale=sc)
                        P.dma("pool", lambda e, o=o, si=si, dstT=dstT, tok=tok: e.dma_start(out=dstT[o * 128:(o + 1) * 128, tok], in_=sq_sb[si][:]),
                              [r_sq[si]], [rd[t]], key="qk%d" % si)
                for tb in range(NT // 128):
                    pi = rot("ps", NPS)
                    for hf in range(2):
                        for k in range(NK):
                            P.op("pe", lambda e, tb=tb, hf=hf, k=k, pi=pi: e.matmul(ps[pi][:, hf * 512:(hf + 1) * 512], hn_sb[:, k, tb * 128:(tb + 1) * 128],
                                                                                    mx0[:, k, hf * 512:(hf + 1) * 512], start=(k == 0), stop=(k == NK - 1)),
                                 reads=[r_hn[k], r_mx0], writes=[r_ps[pi]])
                    si = rot("sq", 2)
                    evac_copy(sq_sb[si][:], ps[pi][:], [r_ps[pi]], [r_sq[si]])
                    P.dma("pool", lambda e, tb=tb, si=si, t=t: e.dma_start(out=vtok[t * NT + tb * 128: t * NT + (tb + 1) * 128, :], in_=sq_sb[si][:]),
                          [r_sq[si]], [r_vt[t]], key="qk%d" % si)
            P.barrier()
            P.dma("pool", lambda e: e.dma_start(out=mx0[:], in_=wo32[i]), [], [r_mx0], key="mx0")
            U16 = parbuf[:].bitcast(BF16)[:, 0:8 * 960].rearrange("p (h n c) -> p h n c", h=8, n=15)
            r_U = R("U16")
            cm = carry
            colm = sg_sb[1]
            P.dma("sp", lambda e: e.dma_start(out=colm[:, 0:64], in_=colmask32), [], [r_sg[1]], key="colm")
            for hm in range(8):
                P.dma("sp", lambda e, hm=hm: e.dma_start(out=sg_sb[0][:, 0:960], in_=rpbU32[i, hm]), [], [r_sg[0]], key="rpbu")
                P.op("dve", lambda e, hm=hm: e.tensor_tensor(U16[:, hm, :, :], sg_sb[0][:, 0:960].rearrange("p (n c) -> p n c", n=15),
                                                             colm[:, 0:64].unsqueeze(1).to_broadcast([128, 15, 64]), ALU.add),
                     reads=[r_sg[0], r_sg[1]], writes=[r_U])
            kflat = act_sb[:, 0:12, :].rearrange("p a b -> p (a b)").rearrange("p (k n) -> p k n", k=8)
            r_kb = R("kband")
            NVS = 10
            r_vs = [R("vs%d" % s) for s in range(NVS)]
            vstate = {}
            pt_sb = [pt_buf[:, q_, :] for q_ in range(3)]
            r_pt = [R("pt0"), R("pt1"), R("pt2")]
            cnt = {"pt": 0, "ps_s": 0, "ps_o": 0, "q": 0, "rec": 0}
            kT3 = kT.rearrange("(k p) l -> p k l", p=128)
            qT3 = qT.rearrange("(k p) l -> p k l", p=128)
            stores_all = []

            def ensure_v(startrow):
                s = startrow % NVS
                if vstate.get(s) == startrow:
                    return s
                vstate[s] = startrow
                tl = (startrow * 64) // NT
                th = (startrow * 64 + 127) // NT
                P.dma("sp", lambda e, s=s, startrow=startrow: e.dma_start(out=act_sb[:, 12 + s, :], in_=vtok[startrow * 64: startrow * 64 + 128, :]),
                      [r_vt[tl], r_vt[th]], [r_vs[s]], key="vs%d" % s)
                return s

            nrows_run = Lr // GRID_W if ntiles == NTILES else (Lr // GRID_W - 16)
            for b in range(nrows_run // 16):
                t = b
                load_h(t, src, src_regs)
                ra = max(16 * b - 4, 0)
                rb = min(16 * b + 20, NROWS)
                nkb = (rb - ra) * 64
                tls = sorted(set([(ra * 64) // NT, (rb * 64 - 1) // NT, b]))
                P.dma("sp", lambda e, ra=ra, nkb=nkb: e.dma_start(out=kflat[:, :, 0:nkb], in_=kT3[:, :, ra * 64: ra * 64 + nkb]),
                      [r_kT[x] for x in tls], [r_kb], key="kband")
                rowinfo = {}

                def SA(n, b=b, ra=ra, rowinfo=rowinfo):
                    rr_, h = divmod(n, NH)
                    if h == 0:
                        r = 16 * b + rr_
                        r0 = min(max(r - 4, 0), NROWS - 8)
                        qi = cnt["q"] % 2
                        cnt["q"] += 1
                        P.dma("sp", lambda e, qi=qi, r=r: e.dma_start(out=qrow[:, qi, :, :], in_=qT3[:, :, r * 64:(r + 1) * 64]),
                              [r_qT[b]], [r_qrow[qi]], key="qrow%d" % qi)
                        vsl = [ensure_v(r0 + 2 * kt) for kt in range(4)]
                        rowinfo[rr_] = dict(n0=r0 - r + 7, koff=(r0 - ra) * 64, qi=qi, vsl=vsl)
                    ri_ = rowinfo[rr_]
                    n0, koff, qi = ri_["n0"], ri_["koff"], ri_["qi"]
                    hk = h // 2
                    pr = slice((h % 2) * 64, (h % 2) * 64 + 64)
                    pS = cnt["ps_s"] % 2
                    cnt["ps_s"] += 1
                    for kt in range(4):
                        P.op("pe", lambda e, kt=kt: e.matmul(
                            ps[pS][:, kt * 64:(kt + 1) * 64], kflat[pr, hk, koff + kt * 128: koff + (kt + 1) * 128], qrow[pr, qi, hk, :], start=True, stop=False),
                             reads=[r_kb, r_qrow[qi]], writes=[r_ps[pS]])
                        P.op("pe", lambda e, kt=kt: e.matmul(
                            ps[pS][:, kt * 64:(kt + 1) * 64], U16[pr, h // 2, n0 + 2 * kt: n0 + 2 * kt + 2, :].rearrange("p n c -> p (n c)"),
                            ident16[pr, pr], start=False, stop=True),
                             reads=[r_U, r_const], writes=[r_ps[pS]])
                    pti = cnt["pt"] % 3
                    cnt["pt"] += 1
                    P.op("act", lambda e: e.activation(out=pt_sb[pti], in_=ps[pS][:, 0:256], func=AF.Exp),
                         reads=[r_ps[pS]], writes=[r_pt[pti]])
                    ri_[("pti", h)] = pti

                def SB(n, rowinfo=rowinfo):
                    rr_, h = divmod(n, NH)
                    ri_ = rowinfo[rr_]
                    vsl = ri_["vsl"]
                    pti = ri_[("pti", h)]
                    hk = h // 2
                    pr = slice((h % 2) * 64, (h % 2) * 64 + 64)
                    pO = 2 + cnt["ps_o"] % 2
                    cnt["ps_o"] += 1
                    for kt in range(4):
                        P.op("pe", lambda e, kt=kt, s=vsl[kt]: e.matmul(
                            ps[pO][:, 0:64], act_sb[:, 12 + s, hk * 128:(hk + 1) * 128], pt_sb[pti][:, kt * 64:(kt + 1) * 64], start=(kt == 0), stop=(kt == 3)),
                             reads=[r_vs[vsl[kt]], r_pt[pti]], writes=[r_ps[pO]])
                    for kt in range(4):
                        P.op("pe", lambda e, kt=kt: e.matmul(
                            ps[pO][:, 64:128], ones_sb[:], pt_sb[pti][:, kt * 64:(kt + 1) * 64], start=(kt == 0), stop=(kt == 3)),
                             reads=[r_const, r_pt[pti]], writes=[r_ps[pO]])
                    ri = cnt["rec"] % 2
                    cnt["rec"] += 1
                    P.op("dve", lambda e: e.reciprocal(recb[pr, ri, :], ps[pO][pr, 64:128]), reads=[r_ps[pO]], writes=[r_rec[ri]])
                    P.op("dve", lambda e: e.tensor_tensor(hn_sb[pr, hk, rr_ * 64:(rr_ + 1) * 64], ps[pO][pr, 0:64], recb[pr, ri, :], ALU.mult),
                         reads=[r_ps[pO], r_rec[ri]], writes=[r_hn[hk]])

                NI = 16 * NH
                SA(0)
                for n in range(NI):
                    if n + 1 < NI:
                        SA(n + 1)
                    SB(n)
                for o in range(NK):
                    pi = proj8(lambda k, o=o: mx0[:, k, o * 128:(o + 1) * 128], [r_mx0],
                               lambda k, hf: hn_sb[:, k, hf * 512:(hf + 1) * 512], r_hn)
                    evac_copy(f_sb[:, o, :], ps[pi][:], [r_ps[pi]], [r_f[o]])
                stores_all += residual_store(layer, 3, False, t, dst, dst_regs)
            return stores_all

        jvec_sb = sb("jvec_sb", [128, NT], F32)
        P.dma("sp", lambda e: e.dma_start(out=jvec_sb[:], in_=jvec32), [], [r_const], key="c")
        r_mt = [R("mt0"), R("mt1")]

        ffn_steps = [st_ for st_ in steps if st_[2] in "ac"]
        fids = [int(st_[1]) * 2 + (0 if st_[2] == "a" else 1) for st_ in ffn_steps]
        for fid in fids:
            for t in range(ntiles):
                for j in range(NJ):
                    wstream.append(("gu", fid, j))
                for m in range(NK):
                    wstream.append(("d", fid, m))

        r_x = [[R("x%d_%d" % (t, k)) for k in range(NK)] for t in range(NTILES)]
        final = []
        if fids:
            cast_ffn_weights(fids[0])
        nf = 0
        for si, step in enumerate(steps):
            layer = int(step[1])
            first = (si == 0)
            last = (si == len(steps) - 1)
            src = xT if first else hT
            dst = outT if last else hT
            sregs = r_x if first else r_hT
            if step[2] in "ac":
                f = 0 if step[2] == "a" else 1
                nf += 1
                if nf < len(fids):
                    cast_ffn_weights(fids[nf])
                stores = ffn_step(layer, f, src, sregs, dst, r_hT)
            elif layer % 2 == 0:
                stores = mixer_even(layer, src, sregs, dst, r_hT)
                P.barrier()
            else:
                stores = mixer_odd(layer, src, sregs, dst, r_hT)
                P.barrier()
            if last:
                final += stores
        P.emit(final)
    return nc


_CACHE = {}


def _prep_weights(inputs):
    f32 = np.float32
    A = np.ascontiguousarray
    out = {}
    out["gT"] = A(inputs["norm_g"].reshape(DEPTH, 6, NK, 128).transpose(3, 0, 1, 2).reshape(128, DEPTH * 6 * NK))
    wg = inputs["ffn_w_gate"].reshape(DEPTH, 2, NK, 128, NJ, 128)
    wu = inputs["ffn_w_up"].reshape(DEPTH, 2, NK, 128, NJ, 128)
    wgu = np.stack([wg, wu], axis=0)
    out["wgu"] = A(wgu.transpose(1, 2, 5, 4, 0, 3, 6)).reshape(DEPTH * 2 * NJ, 128, 2 * NK * 128)
    wd = inputs["ffn_w_down"].reshape(DEPTH, 2, NJ, 128, NK, 128)
    out["wd"] = A(wd.transpose(0, 1, 4, 3, 2, 5)).reshape(DEPTH * 2 * NK, 128, NJ * 128)
    out["ident"] = np.eye(128, dtype=f32)
    sh = np.zeros((128, 128), f32)
    for k in range(128):
        sh[k, (k + 64) % 128] = 1.0
    out["shiftm"] = sh
    out["jvec"] = A(np.broadcast_to(np.arange(NT, dtype=f32)[None, :], (128, NT)))
    cs = np.zeros((128, 32), f32)
    cs[:64, 0] = 1.0
    cs[64:, 0] = -1.0
    cs[:64, 1] = -1.0
    cs[64:, 1] = 1.0
    for g8 in range(8):
        cs[g8 * 16:(g8 + 1) * 16, 2 + g8] = 1.0
        cs[g8 * 16:(g8 + 1) * 16, 10 + g8] = -1.0
    cs[:, 18] = 0.25
    out["csts"] = cs
    cinv = np.zeros((2, 4, 8), f32)
    tt = np.arange(L)
    for gi, w in enumerate((2, 4, 8, 16)):
        lo = w // 2
        hi = w - 1 - lo
        cntv = (np.clip(tt + hi + 1, 0, L) - np.clip(tt - lo, 0, L)).astype(f32)
        cinv[0, gi] = 1.0 / cntv[:8]
        cinv[1, gi] = 1.0 / cntv[L - 8:]
    out["cinv"] = A(np.broadcast_to(cinv.reshape(1, 64), (128, 64)))

    def pk(w):
        n, kk, c = w.shape
        return A(w.reshape(n, kk // 128, 128, c).transpose(0, 2, 1, 3))
    out["w_in"] = pk(inputs["ab_w_in"])
    out["w_out"] = pk(inputs["ab_w_out"])
    out["w_glu"] = pk(inputs["ssm_w_glu"])
    out["pool_w"] = A(inputs["pool_w"].transpose(0, 2, 1, 3))
    pv = np.zeros((2, 128, 12), f32)
    pv[:, :, 0:4] = inputs["pool_scale"].reshape(2, 4, 128).transpose(0, 2, 1)
    pv[:, :, 4:8] = inputs["ssm_D"].reshape(2, 4, 128).transpose(0, 2, 1)
    pv[:, :, 8:12] = inputs["ssm_b_glu"].reshape(2, 4, 128).transpose(0, 2, 1)
    out["pvec"] = pv

    def dup(a):
        a = a.transpose(0, 1, 3, 2)
        return np.concatenate([a, a], axis=2)
    s5a = np.stack([dup(inputs["ssm_A_re"]), dup(inputs["ssm_A_im"]),
                    np.broadcast_to(inputs["ssm_log_dt"][:, :, None, :], (2, 2, 128, 32))], axis=2)
    out["s5a"] = A(s5a.astype(f32))
    bre = inputs["ssm_B_re"].transpose(0, 1, 3, 2, 4).reshape(2, 2, 64, 512)
    bim = inputs["ssm_B_im"].transpose(0, 1, 3, 2, 4).reshape(2, 2, 64, 512)
    cre = inputs["ssm_C_re"].transpose(0, 1, 4, 2, 3).reshape(2, 2, 64, 512)
    cim = inputs["ssm_C_im"].transpose(0, 1, 4, 2, 3).reshape(2, 2, 64, 512)
    s5b = np.stack([np.concatenate([bre, bim], axis=2), np.concatenate([bim, bre], axis=2),
                    np.concatenate([cre, cim], axis=2), np.concatenate([cim, cre], axis=2)], axis=2)
    out["s5b"] = A(s5b.astype(f32))
    wqkv = inputs["na_w_qkv"]
    out["wq"] = pk(wqkv[:, :, 0:D])
    out["wk"] = pk(wqkv[:, :, D:2 * D])
    out["wv"] = pk(wqkv[:, :, 2 * D:3 * D])
    out["wo"] = pk(inputs["na_w_out"])
    c = np.arange(64)
    cj = np.arange(64)
    idx = np.clip(cj[None, :] - c[:, None] + 15, 0, 30)
    rp = inputs["na_rpb"][:, :, :, idx]
    rp = rp.transpose(0, 1, 3, 2, 4).reshape(2, 8, 2, 64, 15 * 64)
    out["rpbU"] = A(rp.reshape(2, 8, 128, 960))
    c0 = np.clip(c - 8, 0, 48)
    valid = (cj[None, :] >= c0[:, None]) & (cj[None, :] < c0[:, None] + 16)
    cmask = np.where(valid, 0.0, -30000.0).astype(f32)
    out["colmask"] = A(np.concatenate([cmask, cmask], axis=0))
    return out


def kernel(**inputs):
    x = inputs["x"]
    B = x.shape[0]
    if "nc" not in _CACHE:
        _CACHE["nc"] = build_program(ALL_STEPS)
    nc = _CACHE["nc"]
    shared = _prep_weights(inputs)
    in_maps = []
    for b in range(B):
        m = dict(shared)
        m["xT"] = np.ascontiguousarray(x[b].T)
        in_maps.append(m)
    res = run_bass_kernel_spmd(nc, in_maps, core_ids=list(range(B)))
    out = np.stack([np.ascontiguousarray(r["outT"].T) for r in res.results], axis=0)
    return out.astype(np.float32)
```

```python
import numpy as np
import os
DBG = int(os.environ.get('DBG', '9'))
import concourse.bass as bass
import concourse.mybir as mybir
from concourse.bass_utils import run_bass_kernel_spmd

F32 = mybir.dt.float32
BF16 = mybir.dt.bfloat16
AF = mybir.ActivationFunctionType
ALU = mybir.AluOpType

D = 1024
L = 8192
DFF = 2816
NJ = DFF // 128
NK = D // 128
DEPTH = 4
NT = 1024
NTILES = L // NT
EPS = 1e-6

ALL_STEPS = ["L%d%s" % (l, c) for l in range(DEPTH) for c in "amc"]


class Region:
    __slots__ = ("name", "last_w", "readers")

    def __init__(self, name):
        self.name = name
        self.last_w = None
        self.readers = {}


class Op:
    __slots__ = ("eng", "fn", "deps", "is_dma", "key", "signals", "sigval", "idx")

    def __init__(self, eng, fn, is_dma=False, key=None):
        self.eng = eng
        self.fn = fn
        self.deps = []
        self.is_dma = is_dma
        self.key = key
        self.signals = False
        self.sigval = 0
        self.idx = 0


class Prog:
    ENGS = ("pe", "act", "dve", "pool", "sp")

    def __init__(self, nc):
        self.nc = nc
        self.ops = []
        self.dma_last = {}
        self.last_op = {}
        self.pending = {}

    def _track(self, op, reads, writes):
        deps = set()
        for r in reads:
            if r.last_w is not None:
                deps.add(r.last_w)
        for w in writes:
            if w.last_w is not None:
                deps.add(w.last_w)
            for o in w.readers.values():
                deps.add(o)
        if op.eng in self.pending:
            for o in self.pending.pop(op.eng):
                deps.add(o)
        deps.discard(op)
        op.deps = list(deps)
        if not op.is_dma:
            self.last_op[op.eng] = op
        for r in reads:
            r.readers[op.eng if not op.is_dma else ("dma", id(op))] = op
        for w in writes:
            w.last_w = op
            w.readers = {}
        op.idx = len(self.ops)
        self.ops.append(op)

    def barrier(self):
        allops = list(self.last_op.values()) + list(self.dma_last.values())
        self.pending = {e: list(allops) for e in self.ENGS}

    def op(self, eng, fn, reads=(), writes=()):
        o = Op(eng, fn)
        self._track(o, reads, writes)
        return o

    def dma(self, queue, fn, reads, writes, key):
        o = Op(queue, fn, is_dma=True, key=key)
        o.signals = True
        self._track(o, reads, writes)
        prev = self.dma_last.get(key)
        if prev is not None and prev not in o.deps:
            o.deps.append(prev)
        self.dma_last[key] = o
        return o

    def emit(self, final_dma_ops):
        nc = self.nc
        for o in self.ops:
            for d in o.deps:
                if d.is_dma:
                    d.signals = True
                elif d.eng == o.eng and not o.is_dma and d.eng == "pe":
                    continue
                else:
                    d.signals = True
        for o in final_dma_ops:
            o.signals = True
        cnt = {}
        keys = []
        for o in self.ops:
            if not o.signals:
                continue
            k = ("dma", o.key) if o.is_dma else ("eng", o.eng)
            if k not in cnt:
                cnt[k] = 0
                keys.append(k)
            cnt[k] += 16 if o.is_dma else 1
            o.sigval = cnt[k]
        self.nsig = dict(cnt)
        import contextlib
        with contextlib.ExitStack() as st:
            sems = {}
            for i, k in enumerate(keys):
                sems[k] = st.enter_context(nc.semaphore("s%d" % i))
            block = st.enter_context(nc.Block())
            per_eng = {e: [o for o in self.ops if o.eng == e] for e in self.ENGS}

            def run(eng_name, e):
                waited = {}
                for o in per_eng[eng_name]:
                    for d in o.deps:
                        if not d.signals:
                            continue
                        if (not d.is_dma) and (not o.is_dma) and d.eng == "pe" and o.eng == "pe":
                            continue
                        k = ("dma", d.key) if d.is_dma else ("eng", d.eng)
                        if waited.get(k, 0) >= d.sigval:
                            continue
                        waited[k] = d.sigval
                        e.wait_ge(sems[k], d.sigval)
                    inst = o.fn(e)
                    if o.signals:
                        k = ("dma", o.key) if o.is_dma else ("eng", o.eng)
                        inst.then_inc(sems[k], 16 if o.is_dma else 1)
                if eng_name == "pool":
                    for o in final_dma_ops:
                        k = ("dma", o.key)
                        if waited.get(k, 0) < o.sigval:
                            waited[k] = o.sigval
                            e.wait_ge(sems[k], o.sigval)

            @block.tensor
            def _(e):
                run("pe", e)

            @block.scalar
            def _(e):
                run("act", e)

            @block.vector
            def _(e):
                run("dve", e)

            @block.gpsimd
            def _(e):
                run("pool", e)

            @block.sync
            def _(e):
                run("sp", e)


import math
I32 = mybir.dt.int32
GRID_W = 64
NROWS = L // GRID_W
NH = 16
PI = math.pi


def build_program(steps, ntiles=NTILES):
    nc = bass.Bass("TRN2", target_bir_lowering=False)
    P = Prog(nc)
    Lr = ntiles * NT

    def dram_in(name, shape, dt=F32):
        return nc.dram_tensor(name, list(shape), dt, kind="ExternalInput").ap()

    xT = dram_in("xT", [D, L])
    gT = dram_in("gT", [128, DEPTH * 6 * NK])
    wgu32 = dram_in("wgu", [DEPTH * 2 * NJ, 128, 2 * NK * 128])
    wd32 = dram_in("wd", [DEPTH * 2 * NK, 128, NJ * 128])
    ident32 = dram_in("ident", [128, 128])
    shift32 = dram_in("shiftm", [128, 128])
    jvec32 = dram_in("jvec", [128, NT])
    csts32 = dram_in("csts", [128, 32])
    cinv32 = dram_in("cinv", [128, 2 * 4 * 8])
    w_in32 = dram_in("w_in", [2, 128, NK, D])
    w_out32 = dram_in("w_out", [2, 128, NK, D])
    w_glu32 = dram_in("w_glu", [2, 128, 4, 512])
    pool_w32 = dram_in("pool_w", [2, 128, 4, 128])
    pvec32 = dram_in("pvec", [2, 128, 12])
    s5a32 = dram_in("s5a", [2, 2, 3, 128, 32])
    s5b32 = dram_in("s5b", [2, 2, 4, 128, 512])
    wq32 = dram_in("wq", [2, 128, NK, D])
    wk32 = dram_in("wk", [2, 128, NK, D])
    wv32 = dram_in("wv", [2, 128, NK, D])
    wo32 = dram_in("wo", [2, 128, NK, D])
    rpbU32 = dram_in("rpbU", [2, 8, 128, 15 * 64])
    colmask32 = dram_in("colmask", [128, 64])
    outT = nc.dram_tensor("outT", [D, L], F32, kind="ExternalOutput").ap()

    hT = nc.dram_tensor("hT", [D, L], F32).ap()
    wgu16 = nc.dram_tensor("wgu16", [DEPTH * 2 * NJ, 128, 2 * NK * 128], BF16).ap()
    wd16 = nc.dram_tensor("wd16", [DEPTH * 2 * NK, 128, NJ * 128], BF16).ap()
    zT = nc.dram_tensor("zT", [D, L], F32).ap()
    ydT = nc.dram_tensor("ydT", [2, 512, L], F32).ap()
    qT = nc.dram_tensor("qT", [D, L], BF16).ap()
    kT = nc.dram_tensor("kT", [D, L], BF16).ap()
    vtok = nc.dram_tensor("vtok", [L, D], BF16).ap()

    import contextlib
    st = contextlib.ExitStack()
    with st:
        def sb(name, shape, dt):
            return st.enter_context(nc.sbuf_tensor(name, list(shape), dt))

        h_sb = sb("h_sb", [128, NK, NT], F32)
        f_sb = sb("f_sb", [128, NK, NT], F32)
        hn_sb = sb("hn_sb", [128, NK, NT], BF16)
        act_sb = sb("act_sb", [128, NJ, NT], BF16)
        NWGU = 3
        NWD = 2
        wgu_sb = [sb("wgu%d" % i, [128, 2 * NK * 128], BF16) for i in range(NWGU)]
        wd_sb = [sb("wd%d" % i, [128, NJ * 128], BF16) for i in range(NWD)]
        sq_sb = [sb("sq%d" % i, [128, NT], BF16) for i in range(2)]
        sg_sb = [sb("sg%d" % i, [128, NT], F32) for i in range(2)]
        rstd_sb = sb("rstd", [128, NT], F32)
        ones_sb = sb("ones", [128, 128], BF16)
        ident16 = sb("ident16", [128, 128], BF16)
        identf = sb("identf", [128, 128], F32)
        shiftf = sb("shiftf", [128, 128], F32)
        csts = sb("csts_sb", [128, 32], F32)
        g_sb = sb("g_sb", [128, DEPTH * 6 * NK], F32)
        gh_sb = sb("gh_sb", [128, DEPTH * 6 * NK], F32)
        mixbuf = sb("mixbuf", [128, NK, NT], F32)
        mixflat = mixbuf[:].rearrange("p k n -> p (k n)")
        mx0 = mixflat[:, 0:4096].bitcast(BF16).rearrange("p (k n) -> p k n", k=NK)
        parbuf = mixflat[:, 4096:8192]
        h_sb2 = mixbuf
        pt_buf = sb("pt_buf", [128, 3, 256], BF16)
        pvec = sb("pvec_sb", [128, 12], F32)
        cinv = sb("cinv_sb", [128, 64], F32)
        qrow = sb("qrow", [128, 2, NK, 64], BF16)
        carry = sb("carry", [128, 4], F32)
        recb = sb("recb", [128, 2, 64], F32)
        NPS = 4
        ps = [st.enter_context(nc.psum_tensor("ps%d" % i, [128, NT], F32)) for i in range(NPS)]

        R = Region
        r_h = [R("h%d" % k) for k in range(NK)]
        r_h2 = [R("h2_%d" % k) for k in range(NK)]
        r_f = [R("f%d" % k) for k in range(NK)]
        r_hn = [R("hn%d" % k) for k in range(NK)]
        r_act = [R("act%d" % j) for j in range(NJ)]
        r_wgu = [R("wgu%d" % i) for i in range(NWGU)]
        r_wd = [R("wd%d" % i) for i in range(NWD)]
        r_sq = [R("sq%d" % i) for i in range(2)]
        r_sg = [R("sg%d" % i) for i in range(2)]
        r_rstd = R("rstd")
        r_ps = [R("ps%d" % i) for i in range(NPS)]
        r_const = R("const")
        r_g = R("g")
        r_mx0 = R("mx0")
        r_par = R("par")
        r_smallw = R("smallw")
        r_pvec = R("pvec")
        r_qrow = [R("qrow0"), R("qrow1")]
        r_carry = R("carry")
        r_rec = [R("rec0"), R("rec1")]
        r_hT = [[R("hT%d_%d" % (t, k)) for k in range(NK)] for t in range(NTILES)]
        r_zT = [[R("zT%d_%d" % (t, k)) for k in range(NK)] for t in range(NTILES)]
        r_yd = [[[R("yd%d_%d_%d" % (d, k, c)) for c in range(NTILES)] for k in range(4)] for d in range(2)]
        r_qT = [R("qT%d" % t) for t in range(NTILES)]
        r_kT = [R("kT%d" % t) for t in range(NTILES)]
        r_vt = [R("vt%d" % t) for t in range(NTILES)]
        r_wgu16 = {}
        r_wd16 = {}

        state = {"ps": 0, "sq": 0, "sg": 0, "wgu": 0, "wd": 0}

        def rot(name, n):
            v = state[name]
            state[name] = (v + 1) % n
            return v

        P.op("dve", lambda e: e.memset(ones_sb[:], 1.0), writes=[r_const])
        P.dma("sp", lambda e: e.dma_start(out=g_sb[:], in_=gT), [], [r_g], key="g")
        P.op("dve", lambda e: e.tensor_scalar(gh_sb[:], g_sb[:], 0.5, None, ALU.mult), reads=[r_g], writes=[r_g])
        P.dma("sp", lambda e: e.dma_start(out=identf[:], in_=ident32), [], [r_const], key="c")
        P.dma("sp", lambda e: e.dma_start(out=shiftf[:], in_=shift32), [], [r_const], key="c")
        P.dma("sp", lambda e: e.dma_start(out=csts[:], in_=csts32), [], [r_const], key="c")
        P.dma("sp", lambda e: e.dma_start(out=cinv[:], in_=cinv32), [], [r_const], key="c")
        P.op("dve", lambda e: e.tensor_copy(ident16[:], identf[:]), reads=[r_const], writes=[r_const])

        cast_rr = {"i": 0}

        def cast_ffn_weights(fid):
            for j in range(NJ):
                kk = cast_rr["i"] % 8
                cast_rr["i"] += 1
                r_wgu16[(fid, j)] = R("wgu16_%d_%d" % (fid, j))
                P.dma("pool", lambda e, j=j: e.dma_start(out=wgu16[fid * NJ + j], in_=wgu32[fid * NJ + j]),
                      [], [r_wgu16[(fid, j)]], key="cast%d" % kk)
            for m in range(NK):
                kk = cast_rr["i"] % 8
                cast_rr["i"] += 1
                r_wd16[(fid, m)] = R("wd16_%d_%d" % (fid, m))
                P.dma("pool", lambda e, m=m: e.dma_start(out=wd16[fid * NK + m], in_=wd32[fid * NK + m]),
                      [], [r_wd16[(fid, m)]], key="cast%d" % kk)

        wstream = []
        wpos = {"issued": 0}
        wslot = {}
        slot_prev = {}

        def issue_weight_loads(pos):
            while wpos["issued"] < len(wstream) and wpos["issued"] <= pos + 4:
                q = wpos["issued"]
                kind, fid, idx = wstream[q]
                n = NWGU if kind == "gu" else NWD
                s = state["wgu" if kind == "gu" else "wd"]
                if slot_prev.get((kind, s), -1) >= pos:
                    break
                rot("wgu" if kind == "gu" else "wd", n)
                slot_prev[(kind, s)] = q
                wslot[q] = s
                if kind == "gu":
                    P.dma("sp", lambda e, s=s, fid=fid, idx=idx: e.dma_start(out=wgu_sb[s][:], in_=wgu16[fid * NJ + idx]),
                          [r_wgu16[(fid, idx)]], [r_wgu[s]], key="wgu%d" % s)
                else:
                    P.dma("sp", lambda e, s=s, fid=fid, idx=idx: e.dma_start(out=wd_sb[s][:], in_=wd16[fid * NK + idx]),
                          [r_wd16[(fid, idx)]], [r_wd[s]], key="wd%d" % s)
                wpos["issued"] += 1

        wcur = {"pos": 0}

        def next_weight(kind, fid, idx):
            pos = wcur["pos"]
            assert wstream[pos] == (kind, fid, idx), (wstream[pos], kind, fid, idx)
            issue_weight_loads(pos)
            assert pos in wslot
            wcur["pos"] += 1
            return wslot[pos]

        def gcol(layer, n, k, half=False):
            c = (layer * 6 + n) * NK + k
            tt = gh_sb if half else g_sb
            return tt[:, c:c + 1]

        def rms_stats(src_tiles, src_regs):
            pi = rot("ps", NPS)
            for k in range(NK):
                si = rot("sq", 2)
                P.op("act", lambda e, k=k, si=si: e.activation(out=sq_sb[si][:], in_=src_tiles[k], func=AF.Square),
                     reads=[src_regs[k]], writes=[r_sq[si]])
                for hf in range(NT // 512):
                    P.op("pe", lambda e, k=k, si=si, hf=hf, pi=pi: e.matmul(ps[pi][:, hf * 512:(hf + 1) * 512], ones_sb[:], sq_sb[si][:, hf * 512:(hf + 1) * 512],
                                                                            start=(k == 0), stop=(k == NK - 1)),
                         reads=[r_sq[si], r_const], writes=[r_ps[pi]])
            P.op("dve", lambda e, pi=pi: e.tensor_scalar(rstd_sb[:], ps[pi][:], 1.0 / D, EPS, ALU.mult, ALU.add),
                 reads=[r_ps[pi]], writes=[r_rstd])
            P.op("act", lambda e: e.activation(out=rstd_sb[:], in_=rstd_sb[:], func=AF.Sqrt), reads=[r_rstd], writes=[r_rstd])
            P.op("dve", lambda e: e.reciprocal(rstd_sb[:], rstd_sb[:]), reads=[r_rstd], writes=[r_rstd])

        def load_h(t, src, src_regs, hb=0):
            tok = slice(t * NT, (t + 1) * NT)
            hbuf = h_sb if hb == 0 else h_sb2
            rh = r_h if hb == 0 else r_h2
            for k in range(NK):
                P.dma("sp", lambda e, k=k: e.dma_start(out=hbuf[:, k, :], in_=src[k * 128:(k + 1) * 128, tok]),
                      [src_regs[t][k]], [rh[k]], key="hload%d" % k)

        def load_norm(layer, n, t, src, src_regs, hb=0):
            hbuf = h_sb if hb == 0 else h_sb2
            rh = r_h if hb == 0 else r_h2
            load_h(t, src, src_regs, hb)
            rms_stats([hbuf[:, k, :] for k in range(NK)], rh)
            for k in range(NK):
                P.op("dve", lambda e, k=k: e.scalar_tensor_tensor(out=hn_sb[:, k, :], in0=hbuf[:, k, :], scalar=gcol(layer, n, k),
                                                                  in1=rstd_sb[:], op0=ALU.mult, op1=ALU.mult),
                     reads=[rh[k], r_rstd, r_g], writes=[r_hn[k]])

        def proj8(wfn, wregs, rhs, rhs_regs, nk=NK):
            pi = rot("ps", NPS)
            for k in range(nk):
                for hf in range(NT // 512):
                    P.op("pe", lambda e, k=k, hf=hf, pi=pi: e.matmul(ps[pi][:, hf * 512:(hf + 1) * 512], wfn(k), rhs(k, hf),
                                                                     start=(k == 0), stop=(k == nk - 1)),
                         reads=list(wregs) + [rhs_regs[k]], writes=[r_ps[pi]])
            return pi

        def residual_store(layer, n, half, t, dst, dst_regs, hb=0):
            tok = slice(t * NT, (t + 1) * NT)
            hbuf = h_sb if hb == 0 else h_sb2
            rh = r_h if hb == 0 else r_h2
            rms_stats([f_sb[:, m, :] for m in range(NK)], r_f)
            stores = []
            for m in range(NK):
                P.op("dve", lambda e, m=m: e.scalar_tensor_tensor(out=f_sb[:, m, :], in0=f_sb[:, m, :], scalar=gcol(layer, n, m, half=half),
                                                                  in1=rstd_sb[:], op0=ALU.mult, op1=ALU.mult),
                     reads=[r_f[m], r_rstd, r_g], writes=[r_f[m]])
                eng_ = "pool" if m % 3 == 2 else "dve"
                P.op(eng_, lambda e, m=m: e.tensor_tensor(f_sb[:, m, :], f_sb[:, m, :], hbuf[:, m, :], ALU.add),
                     reads=[r_f[m], rh[m]], writes=[r_f[m]])
                o = P.dma("pool", lambda e, m=m: e.dma_start(out=dst[m * 128:(m + 1) * 128, tok], in_=f_sb[:, m, :]),
                          [r_f[m]], [dst_regs[t][m]], key="hstore%d" % m)
                stores.append(o)
            return stores

        evac_rr = {"i": 0}

        def evac_copy(out_ap, in_ap, reads, writes, scale=None):
            evac_rr["i"] += 1
            if evac_rr["i"] % 2 == 0:
                if scale is None:
                    P.op("dve", lambda e: e.tensor_copy(out_ap, in_ap), reads=reads, writes=writes)
                else:
                    P.op("dve", lambda e: e.tensor_scalar(out_ap, in_ap, scale, None, ALU.mult), reads=reads, writes=writes)
            else:
                if scale is None:
                    P.op("act", lambda e: e.activation(out=out_ap, in_=in_ap, func=AF.Copy), reads=reads, writes=writes)
                else:
                    P.op("act", lambda e: e.activation(out=out_ap, in_=in_ap, func=AF.Copy, scale=scale), reads=reads, writes=writes)

        def ffn_step(layer, f, src, src_regs, dst, dst_regs):
            fid = layer * 2 + f
            n0 = 0 if f == 0 else 4
            stores = []
            load_norm(layer, n0, 0, src, src_regs, hb=0)
            for t in range(ntiles):
                hb = t % 2
                for j in range(NJ):
                    s = next_weight("gu", fid, j)
                    pg = rot("ps", NPS)
                    pu = rot("ps", NPS)
                    for gu, pi in ((0, pg), (1, pu)):
                        for k in range(NK):
                            for hf in range(NT // 512):
                                P.op("pe", lambda e, s=s, gu=gu, k=k, hf=hf, pi=pi: e.matmul(
                                    ps[pi][:, hf * 512:(hf + 1) * 512],
                                    wgu_sb[s][:, (gu * NK + k) * 128:(gu * NK + k + 1) * 128],
                                    hn_sb[:, k, hf * 512:(hf + 1) * 512], start=(k == 0), stop=(k == NK - 1)),
                                     reads=[r_wgu[s], r_hn[k]], writes=[r_ps[pi]])
                    gi = rot("sg", 2)
                    P.op("act", lambda e, gi=gi, pg=pg: e.activation(out=sg_sb[gi][:], in_=ps[pg][:], func=AF.Silu),
                         reads=[r_ps[pg]], writes=[r_sg[gi]])
                    P.op("dve", lambda e, gi=gi, pu=pu, j=j: e.tensor_tensor(act_sb[:, j, :], sg_sb[gi][:], ps[pu][:], ALU.mult),
                         reads=[r_sg[gi], r_ps[pu]], writes=[r_act[j]])
                for m in range(NK):
                    s = next_weight("d", fid, m)
                    pi = rot("ps", NPS)
                    for j in range(NJ):
                        for hf in range(NT // 512):
                            P.op("pe", lambda e, s=s, j=j, hf=hf, pi=pi, m=m: e.matmul(
                                ps[pi][:, hf * 512:(hf + 1) * 512], wd_sb[s][:, j * 128:(j + 1) * 128],
                                act_sb[:, j, hf * 512:(hf + 1) * 512], start=(j == 0), stop=(j == NJ - 1)),
                                 reads=[r_wd[s], r_act[j]], writes=[r_ps[pi]])
                    evac_copy(f_sb[:, m, :], ps[pi][:], [r_ps[pi]], [r_f[m]])
                    if m == 2 and t + 1 < ntiles:
                        load_norm(layer, n0, t + 1, src, src_regs, hb=(t + 1) % 2)
                stores += residual_store(layer, n0 + 1, True, t, dst, dst_regs, hb=hb)
            return stores

        def mixer_even(layer, src, src_regs, dst, dst_regs):
            i = layer // 2
            assert ntiles == NTILES, "even mixer needs the full sequence"
            P.barrier()
            P.dma("pool", lambda e: e.dma_start(out=mx0[:], in_=w_in32[i]), [], [r_mx0], key="mx0")
            for t in range(ntiles):
                tok = slice(t * NT, (t + 1) * NT)
                load_norm(layer, 2, t, src, src_regs)
                for o in range(NK):
                    pi = proj8(lambda k, o=o: mx0[:, k, o * 128:(o + 1) * 128], [r_mx0],
                               lambda k, hf: hn_sb[:, k, hf * 512:(hf + 1) * 512], r_hn)
                    evac_copy(f_sb[:, o, :], ps[pi][:], [r_ps[pi]], [r_f[o]])
                    P.dma("pool", lambda e, o=o, tok=tok: e.dma_start(out=zT[o * 128:(o + 1) * 128, tok], in_=f_sb[:, o, :]),
                          [r_f[o]], [r_zT[t][o]], key="zst%d" % o)
            P.barrier()
            Ysb = h_sb[:].rearrange("p k n -> p (k n)")
            u16 = hn_sb[:].rearrange("p k n -> p (k n)")
            pb = parbuf
            c_B1, c_B2, c_C1, c_C2 = 0, 512, 1024, 1536
            c_sm = 2048
            c_bbT = 2560
            c_MT = 3072
            c_pad = 3328

            def sm(ix):
                return pb[:, c_sm + ix * 32: c_sm + (ix + 1) * 32]
            ARE, AIM, DT, MAG, ANGN, LRE, LIM, FRE, FIM, TA, TB, CT, ST, DEN, TC, TD = range(16)
            KI = sg_sb[0][:].bitcast(I32)
            Ubuf = [sg_sb[1], rstd_sb]
            rU = [r_sg[1], r_rstd]
            rKI = r_sg[0]

            def dv(fn, reads=(r_par,), writes=(r_par,)):
                P.op("dve", fn, reads=list(reads), writes=list(writes))

            def ac(fn, reads=(r_par,), writes=(r_par,)):
                P.op("act", fn, reads=list(reads), writes=list(writes))

            def sincos_small(dst_sin, dst_cos, ang_over_2pi):
                for dst, off in ((dst_sin, 0.0), (dst_cos, 0.25)):
                    dv(lambda e, off=off: e.tensor_scalar(sm(TC), ang_over_2pi, 1.0, off, ALU.mult, ALU.add))
                    dv(lambda e: e.tensor_copy(KI[:, 0:32], sm(TC)), writes=[r_par, rKI])
                    dv(lambda e: e.tensor_copy(sm(TD), KI[:, 0:32]), reads=[r_par, rKI])
                    dv(lambda e: e.tensor_tensor(sm(TC), sm(TC), sm(TD), ALU.subtract))
                    ac(lambda e, dst=dst: e.activation(out=dst, in_=sm(TC), func=AF.Sin, scale=2 * PI))

            for d in range(2):
                P.dma("sp", lambda e, d=d: e.dma_start(out=sm(ARE), in_=s5a32[i, d, 0]), [], [r_par], key="par")
                P.dma("sp", lambda e, d=d: e.dma_start(out=sm(AIM), in_=s5a32[i, d, 1]), [], [r_par], key="par")
                P.dma("sp", lambda e, d=d: e.dma_start(out=sm(DT), in_=s5a32[i, d, 2]), [], [r_par], key="par")
                for q_, c_ in enumerate((c_B1, c_B2, c_C1, c_C2)):
                    P.dma("sp", lambda e, q_=q_, c_=c_, d=d: e.dma_start(out=pb[:, c_:c_ + 512], in_=s5b32[i, d, q_]), [], [r_par], key="par")
                ac(lambda e: e.activation(out=sm(DT), in_=sm(DT), func=AF.Exp))
                dv(lambda e: e.tensor_scalar(sm(ARE), sm(ARE), -1e-4, None, ALU.min))
                dv(lambda e: e.tensor_tensor(sm(TA), sm(ARE), sm(DT), ALU.mult))
                ac(lambda e: e.activation(out=sm(MAG), in_=sm(TA), func=AF.Exp))
                dv(lambda e: e.tensor_tensor(sm(ANGN), sm(AIM), sm(DT), ALU.mult))
                dv(lambda e: e.tensor_scalar(sm(ANGN), sm(ANGN), 1.0 / (2 * PI), None, ALU.mult))
                sincos_small(sm(LIM), sm(LRE), sm(ANGN))
                dv(lambda e: e.tensor_tensor(sm(LRE), sm(LRE), sm(MAG), ALU.mult))
                dv(lambda e: e.tensor_tensor(sm(LIM), sm(LIM), sm(MAG), ALU.mult))
                dv(lambda e: e.tensor_scalar(sm(TA), sm(ANGN), float(NT), None, ALU.mult))
                sincos_small(sm(ST), sm(CT), sm(TA))
                dv(lambda e: e.tensor_scalar(sm(ST), sm(ST), csts[:, 0:1], None, ALU.mult), reads=[r_par, r_const])
                dv(lambda e: e.tensor_scalar(sm(LRE), sm(LRE), -1.0, None, ALU.add))
                dv(lambda e: e.tensor_tensor(sm(DEN), sm(ARE), sm(ARE), ALU.mult))
                dv(lambda e: e.tensor_tensor(sm(TA), sm(AIM), sm(AIM), ALU.mult))
                dv(lambda e: e.tensor_tensor(sm(DEN), sm(DEN), sm(TA), ALU.add))
                dv(lambda e: e.reciprocal(sm(DEN), sm(DEN)))
                dv(lambda e: e.tensor_tensor(sm(TA), sm(LRE), sm(ARE), ALU.mult))
                dv(lambda e: e.tensor_tensor(sm(TB), sm(LIM), sm(AIM), ALU.mult))
                dv(lambda e: e.tensor_tensor(sm(TA), sm(TA), sm(TB), ALU.add))
                dv(lambda e: e.tensor_tensor(sm(FRE), sm(TA), sm(DEN), ALU.mult))
                dv(lambda e: e.tensor_tensor(sm(TA), sm(LIM), sm(ARE), ALU.mult))
                dv(lambda e: e.tensor_tensor(sm(TB), sm(LRE), sm(AIM), ALU.mult))
                dv(lambda e: e.tensor_tensor(sm(TA), sm(TA), sm(TB), ALU.subtract))
                dv(lambda e: e.tensor_tensor(sm(FIM), sm(TA), sm(DEN), ALU.mult))
                dv(lambda e: e.tensor_scalar(sm(FIM), sm(FIM), csts[:, 1:2], None, ALU.mult), reads=[r_par, r_const])
                B1v = pb[:, c_B1:c_B1 + 512].rearrange("p (g h) -> p g h", h=16)
                B2v = pb[:, c_B2:c_B2 + 512].rearrange("p (g h) -> p g h", h=16)
                dv(lambda e: e.tensor_tensor(B1v, B1v, sm(FRE).unsqueeze(2).to_broadcast([128, 32, 16]), ALU.mult))
                dv(lambda e: e.tensor_tensor(B2v, B2v, sm(FIM).unsqueeze(2).to_broadcast([128, 32, 16]), ALU.mult))
                dv(lambda e: e.tensor_tensor(pb[:, c_B1:c_B1 + 512], pb[:, c_B1:c_B1 + 512], pb[:, c_B2:c_B2 + 512], ALU.add))
                dv(lambda e: e.tensor_scalar(pb[:, c_C1:c_C1 + 512], pb[:, c_C1:c_C1 + 512], csts[:, 0:1], None, ALU.mult), reads=[r_par, r_const])
                dv(lambda e: e.tensor_scalar(pb[:, c_C2:c_C2 + 512], pb[:, c_C2:c_C2 + 512], -1.0, None, ALU.mult))
                for gt in range(4):
                    pi = rot("ps", 3)
                    P.op("pe", lambda e, gt=gt, pi=pi: e.matmul(ps[pi][:, 0:128], pb[:, c_B1 + gt * 128: c_B1 + (gt + 1) * 128], identf[:], start=True, stop=True),
                         reads=[r_par, r_const], writes=[r_ps[pi]])
                    dv(lambda e, gt=gt, pi=pi: e.tensor_copy(pb[:, c_bbT + gt * 128: c_bbT + (gt + 1) * 128], ps[pi][:, 0:128]),
                       reads=[r_par, r_ps[pi]])
                kbuf2 = act_sb[:, 14:16, :].rearrange("p a b -> p (a b)").bitcast(F32)
                kbufs = [(sg_sb[0][:], [rKI]), (kbuf2, [r_act[14], r_act[15]])]

                def prep_g(gt, g8, stage=None):
                    g = gt * 8 + g8
                    par = g % 2
                    tabs = []
                    for which in range(2):
                        a_ = 6 + 4 * par + 2 * which
                        tabs.append((act_sb[:, a_:a_ + 2, :].rearrange("p a b -> p (a b)").bitcast(F32), [r_act[a_], r_act[a_ + 1]]))
                    (sinT, r_sin), (cosT, r_cos) = tabs
                    for st_ in ((0, 1, 2, 3) if stage is None else (stage,)):
                        if st_ > 3:
                            continue
                        for wi, (tab, rtab, off) in enumerate(((sinT, r_sin, 0.0), (cosT, r_cos, 0.25))):
                            ub = Ubuf[wi]
                            rub = rU[wi]
                            kb, rkb = kbufs[wi]
                            bcol = csts[:, 19:20] if off == 0.0 else csts[:, 18:19]
                            if st_ == 0:
                                P.op("act", lambda e, ub=ub, bcol=bcol: e.activation(out=ub[:], in_=jvec_sb[:], func=AF.Identity, scale=sm(ANGN)[:, g:g + 1], bias=bcol),
                                     reads=[r_par, r_const], writes=[rub])
                            elif st_ == 1:
                                P.op("dve", lambda e, ub=ub, kb=kb: e.tensor_scalar(kb, ub[:], 12582912.0, 12582912.0, ALU.add, ALU.subtract), reads=[rub], writes=rkb)
                            elif st_ == 2:
                                P.op("pool", lambda e, ub=ub, kb=kb: e.tensor_tensor(ub[:], ub[:], kb, ALU.subtract), reads=[rub] + rkb, writes=[rub])
                            else:
                                P.op("act", lambda e, ub=ub, tab=tab: e.activation(out=tab, in_=ub[:], func=AF.Sin, scale=2 * PI),
                                     reads=[rub], writes=rtab)
                    if stage is not None and stage != 4:
                        return None
                    padb = act_sb[:, 4 + par, :]
                    r_pad = r_act[4 + par]
                    BBp = padb[:, 0:128]
                    BBs = padb[:, 128:256]
                    C1p = padb[:, 256:384]
                    C2p = padb[:, 384:512]
                    bbT = pb[:, c_bbT + gt * 128: c_bbT + (gt + 1) * 128]
                    gm = csts[:, 2 + g8: 3 + g8]
                    ngm = csts[:, 10 + g8: 11 + g8]
                    P.op("act", lambda e: e.activation(out=BBp, in_=bbT, func=AF.Identity, scale=gm), reads=[r_par, r_const], writes=[r_pad])
                    P.op("act", lambda e: e.activation(out=BBs[:, 0:64], in_=bbT[:, 64:128], func=AF.Identity, scale=gm), reads=[r_par, r_const], writes=[r_pad])
                    P.op("act", lambda e: e.activation(out=BBs[:, 64:128], in_=bbT[:, 0:64], func=AF.Identity, scale=ngm), reads=[r_par, r_const], writes=[r_pad])
                    P.op("act", lambda e: e.activation(out=padb[:, 256:512], in_=pb[:, c_B1:c_B1 + 256], func=AF.Copy, scale=0.0), reads=[r_par], writes=[r_pad])
                    P.op("act", lambda e: e.activation(out=C1p[:, g8 * 16:(g8 + 1) * 16], in_=pb[:, c_C1 + g * 16: c_C1 + (g + 1) * 16], func=AF.Copy), reads=[r_par], writes=[r_pad])
                    P.op("act", lambda e: e.activation(out=C2p[:, g8 * 16:(g8 + 1) * 16], in_=pb[:, c_C2 + g * 16: c_C2 + (g + 1) * 16], func=AF.Copy), reads=[r_par], writes=[r_pad])
                    MT = pb[:, c_MT + par * 128: c_MT + (par + 1) * 128]
                    r_MT = r_mt[par]
                    P.op("dve", lambda e: e.tensor_scalar(MT, identf[:], sm(CT)[:, g:g + 1], None, ALU.mult), reads=[r_par, r_const], writes=[r_MT])
                    P.op("dve", lambda e: e.scalar_tensor_tensor(out=MT, in0=shiftf[:], scalar=sm(ST)[:, g:g + 1], in1=MT, op0=ALU.mult, op1=ALU.add),
                         reads=[r_par, r_const, r_MT], writes=[r_MT])
                    return dict(cos=cosT, sin=sinT, r_cos=r_cos, r_sin=r_sin, BBp=BBp, BBs=BBs, C1p=C1p, C2p=C2p, r_pad=r_pad, MT=MT, r_MT=r_MT)

                nxt_first = None
                for gt in range(4):
                    for k in range(NK):
                        P.dma("pool", lambda e, gt=gt, k=k: e.dma_start(out=hn_sb[:, k, :], in_=zT[512 + gt * 128: 512 + (gt + 1) * 128, k * NT:(k + 1) * NT]),
                              [r_zT[k][4 + gt]], [r_hn[k]], key="uld%d" % k)
                    its = []
                    for g8 in range(8):
                        for ci in range(ntiles):
                            its.append((g8, ci))
                    curs = {}
                    if gt == 0:
                        curs[0] = prep_g(0, 0)
                    else:
                        curs[0] = nxt_first
                    PA, PB, PY, PC = 0, 1, 2, 3

                    def S1pe(n):
                        g8, ci = its[n]
                        cur = curs[g8]
                        c = ci if d == 0 else ntiles - 1 - ci
                        t0 = c * NT
                        for lh, pi in ((cur["BBp"], PA), (cur["BBs"], PB)):
                            for hf in range(2):
                                P.op("pe", lambda e, lh=lh, pi=pi, hf=hf, t0=t0: e.matmul(ps[pi][:, hf * 512:(hf + 1) * 512], lh, u16[:, t0 + hf * 512: t0 + (hf + 1) * 512], start=True, stop=True),
                                     reads=[cur["r_pad"], r_hn[c]], writes=[r_ps[pi]])

                    def S1v(n):
                        g8, ci = its[n]
                        cur = curs[g8]
                        q2 = n % 2
                        Wa = f_sb[:, 0 + q2, :]
                        Wb = f_sb[:, 2 + q2, :]
                        A_in = ps[PA][:, :] if d == 0 else ps[PA][:, ::-1]
                        B_in = ps[PB][:, :] if d == 0 else ps[PB][:, ::-1]
                        P.op("dve", lambda e: e.tensor_tensor(Wa, cur["cos"], A_in, ALU.mult), reads=cur["r_cos"] + [r_ps[PA]], writes=[r_f[0 + q2]])
                        P.op("dve", lambda e: e.tensor_tensor(ps[PY][:, :], cur["sin"], B_in, ALU.mult), reads=cur["r_sin"] + [r_ps[PB]], writes=[r_ps[PY]])
                        P.op("dve", lambda e: e.tensor_tensor(Wa, Wa, ps[PY][:, :], ALU.add), reads=[r_f[0 + q2], r_ps[PY]], writes=[r_f[0 + q2]])

                    def S2(n):
                        g8, ci = its[n]
                        cur = curs[g8]
                        g = gt * 8 + g8
                        q2 = n % 2
                        Wa = f_sb[:, 0 + q2, :]
                        Wc = f_sb[:, 4 + q2, :]
                        rcol = sm(MAG)[:, g:g + 1]
                        init = 0.0 if ci == 0 else carry[:, 0:1]
                        P.op("dve", lambda e: e.tensor_tensor_scan(Wc, rcol.to_broadcast([128, NT]), Wa, init, ALU.mult, ALU.add),
                             reads=[r_f[0 + q2], r_par, r_carry], writes=[r_f[4 + q2]])
                        if ci < ntiles - 1:
                            P.op("pe", lambda e: e.matmul(ps[PC][:, 0:1], cur["MT"], Wc[:, NT - 1:NT], start=True, stop=True),
                                 reads=[cur["r_MT"], r_f[4 + q2]], writes=[r_ps[PC]])
                            P.op("act", lambda e: e.activation(out=carry[:, 0:1], in_=ps[PC][:, 0:1], func=AF.Copy), reads=[r_ps[PC]], writes=[r_carry])

                    def S3pool(n):
                        g8, ci = its[n]
                        cur = curs[g8]
                        q2 = n % 2
                        Wc = f_sb[:, 4 + q2, :]
                        p1 = act_sb[:, 0 + q2, :]
                        p2 = act_sb[:, 2 + q2, :]
                        p1o = p1 if d == 0 else p1[:, ::-1]
                        p2o = p2 if d == 0 else p2[:, ::-1]
                        P.op("pool", lambda e: e.tensor_tensor(p1o, cur["cos"], Wc, ALU.mult), reads=cur["r_cos"] + [r_f[4 + q2]], writes=[r_act[0 + q2]])
                        P.op("pool", lambda e: e.tensor_tensor(p2o, cur["sin"], Wc, ALU.mult), reads=cur["r_sin"] + [r_f[4 + q2]], writes=[r_act[2 + q2]])

                    def S3pe(n):
                        g8, ci = its[n]
                        cur = curs[g8]
                        q2 = n % 2
                        p1 = act_sb[:, 0 + q2, :]
                        p2 = act_sb[:, 2 + q2, :]
                        for hf in range(2):
                            P.op("pe", lambda e, hf=hf: e.matmul(ps[PY][:, hf * 512:(hf + 1) * 512], cur["C1p"], p1[:, hf * 512:(hf + 1) * 512], start=True, stop=False),
                                 reads=[cur["r_pad"], r_act[0 + q2]], writes=[r_ps[PY]])
                            P.op("pe", lambda e, hf=hf: e.matmul(ps[PY][:, hf * 512:(hf + 1) * 512], cur["C2p"], p2[:, hf * 512:(hf + 1) * 512], start=False, stop=True),
                                 reads=[cur["r_pad"], r_act[2 + q2]], writes=[r_ps[PY]])

                    def S4(n):
                        g8, ci = its[n]
                        c = ci if d == 0 else ntiles - 1 - ci
                        t0 = c * NT
                        q2 = n % 2
                        Yst = f_sb[:, 6 + q2, :]
                        P.op("act", lambda e: e.activation(out=Yst, in_=ps[PY][:], func=AF.Copy), reads=[r_ps[PY]], writes=[r_f[6 + q2]])
                        row0 = gt * 128 + g8 * 16
                        P.dma("sp", lambda e, d=d: e.dma_start(out=ydT[d, row0:row0 + 16, t0:t0 + NT], in_=Yst[g8 * 16:(g8 + 1) * 16, :]),
                              [r_f[6 + q2]], [r_yd[d][gt][c]], key="yst%d" % q2)

                    NI = len(its)
                    S1pe(0)
                    S1v(0)
                    S1pe(1)
                    for n in range(NI + 1):
                        if n < NI:
                            g8, ci = its[n]
                            if 1 <= ci <= 5 and not (g8 == 7 and gt == 3):
                                ngt, ng8 = (gt, g8 + 1) if g8 < 7 else (gt + 1, 0)
                                res_ = prep_g(ngt, ng8, stage=ci - 1)
                                if ci == 5:
                                    if g8 < 7:
                                        curs[g8 + 1] = res_
                                    else:
                                        nxt_first = res_
                        if n + 1 < NI:
                            S1v(n + 1)
                        if n + 2 < NI:
                            S1pe(n + 2)
                        if n >= 1:
                            S3pe(n - 1)
                            S4(n - 1)
                        if n < NI:
                            S2(n)
                            S3pool(n)
            P.barrier()
            P.dma("pool", lambda e: e.dma_start(out=mx0[:], in_=w_out32[i]), [], [r_mx0], key="mx0")
            smallw = parbuf[:].bitcast(BF16)
            wglu16 = smallw[:, 0:2048].rearrange("p (k c) -> p k c", k=4)
            poolw16 = smallw[:, 2048:2560].rearrange("p (k c) -> p k c", k=4)
            P.dma("pool", lambda e: e.dma_start(out=wglu16, in_=w_glu32[i]), [], [r_smallw], key="sw0")
            P.dma("pool", lambda e: e.dma_start(out=poolw16, in_=pool_w32[i]), [], [r_smallw], key="sw1")
            P.dma("sp", lambda e: e.dma_start(out=pvec[:], in_=pvec32[i]), [], [r_pvec], key="pvec")
            fflat = f_sb[:].rearrange("p k n -> p (k n)")
            NP = NT + 16
            r_zp = R("zp")
            WIN = (2, 4, 8, 16)
            stores_all = []
            for t in range(ntiles):
                tok = slice(t * NT, (t + 1) * NT)
                t0 = t * NT
                load_h(t, src, src_regs)
                zp = [fflat[:, gi * NP:(gi + 1) * NP] for gi in range(4)]
                wk_ = [fflat[:, (4 + q_) * NP:(5 + q_) * NP] for q_ in range(3)]
                lo = max(t0 - 8, 0)
                hi = min(t0 + NT + 8, L)
                for gi in range(4):
                    if t == 0:
                        P.op("pool", lambda e, gi=gi: e.memset(zp[gi][:, 0:8], 0.0), reads=[], writes=[r_zp] + r_f)
                    if t == NTILES - 1:
                        P.op("pool", lambda e, gi=gi: e.memset(zp[gi][:, NT + 8:NT + 16], 0.0), reads=[], writes=[r_zp] + r_f)
                    rr = [r_zT[tt][gi] for tt in range(max(t - 1, 0), min(t + 2, NTILES))]
                    P.dma("sp", lambda e, gi=gi, lo=lo, hi=hi, t0=t0: e.dma_start(out=zp[gi][:, lo - (t0 - 8): hi - (t0 - 8)], in_=zT[gi * 128:(gi + 1) * 128, lo:hi]),
                          rr, [r_zp] + r_f, key="zpl%d" % gi)
                for gi in range(4):
                    w = WIN[gi]
                    z = zp[gi]
                    a2, a4, a8 = wk_
                    pr_ = act_sb[:, gi, :]
                    if gi == 0:
                        P.op("dve", lambda e, z=z, a2=a2: e.tensor_tensor(a2[:, 0:NT], z[:, 7:7 + NT], z[:, 8:8 + NT], ALU.add), reads=[r_zp], writes=[r_zp])
                        s_ = a2[:, 0:NT]
                    else:
                        P.op("dve", lambda e, z=z, a2=a2: e.tensor_tensor(a2[:, 0:NP - 1], z[:, 0:NP - 1], z[:, 1:NP], ALU.add), reads=[r_zp], writes=[r_zp])
                        if gi == 1:
                            P.op("dve", lambda e, a2=a2, a4=a4: e.tensor_tensor(a4[:, 0:NT], a2[:, 6:6 + NT], a2[:, 8:8 + NT], ALU.add), reads=[r_zp], writes=[r_zp])
                            s_ = a4[:, 0:NT]
                        else:
                            P.op("dve", lambda e, a2=a2, a4=a4: e.tensor_tensor(a4[:, 0:NP - 3], a2[:, 0:NP - 3], a2[:, 2:NP - 1], ALU.add), reads=[r_zp], writes=[r_zp])
                            if gi == 2:
                                P.op("dve", lambda e, a4=a4, a8=a8: e.tensor_tensor(a8[:, 0:NT], a4[:, 4:4 + NT], a4[:, 8:8 + NT], ALU.add), reads=[r_zp], writes=[r_zp])
                                s_ = a8[:, 0:NT]
                            else:
                                P.op("dve", lambda e, a4=a4, a8=a8: e.tensor_tensor(a8[:, 0:NP - 7], a4[:, 0:NP - 7], a4[:, 4:NP - 3], ALU.add), reads=[r_zp], writes=[r_zp])
                                P.op("dve", lambda e, a2=a2, a8=a8: e.tensor_tensor(a2[:, 0:NT], a8[:, 0:NT], a8[:, 8:8 + NT], ALU.add), reads=[r_zp], writes=[r_zp])
                                s_ = a2[:, 0:NT]
                    P.op("dve", lambda e, s_=s_, z=z, pr_=pr_, w=w: e.scalar_tensor_tensor(out=pr_, in0=s_, scalar=1.0 / w, in1=z[:, 8:8 + NT], op0=ALU.mult, op1=ALU.subtract),
                         reads=[r_zp], writes=[r_act[gi]])
                    if t == 0:
                        P.op("dve", lambda e, s_=s_, gi=gi: e.tensor_tensor(s_[:, 0:8], s_[:, 0:8], cinv[:, gi * 8:(gi + 1) * 8], ALU.mult), reads=[r_zp, r_const], writes=[r_zp])
                        P.op("dve", lambda e, s_=s_, z=z, pr_=pr_: e.tensor_tensor(pr_[:, 0:8], s_[:, 0:8], z[:, 8:16], ALU.subtract), reads=[r_zp, r_act[gi]], writes=[r_act[gi]])
                    if t == NTILES - 1:
                        P.op("dve", lambda e, s_=s_, gi=gi: e.tensor_tensor(s_[:, NT - 8:NT], s_[:, NT - 8:NT], cinv[:, 32 + gi * 8: 32 + (gi + 1) * 8], ALU.mult), reads=[r_zp, r_const], writes=[r_zp])
                        P.op("dve", lambda e, s_=s_, z=z, pr_=pr_: e.tensor_tensor(pr_[:, NT - 8:NT], s_[:, NT - 8:NT], z[:, NT:NT + 8], ALU.subtract), reads=[r_zp, r_act[gi]], writes=[r_act[gi]])
                    pi = rot("ps", NPS)
                    for hf in range(2):
                        P.op("pe", lambda e, gi=gi, hf=hf, pi=pi, pr_=pr_: e.matmul(ps[pi][:, hf * 512:(hf + 1) * 512], poolw16[:, gi, :], pr_[:, hf * 512:(hf + 1) * 512], start=True, stop=True),
                             reads=[r_smallw, r_act[gi]], writes=[r_ps[pi]])
                    P.op("act", lambda e, gi=gi, pi=pi: e.activation(out=act_sb[:, 8 + gi, :], in_=ps[pi][:], func=AF.Identity, scale=pvec[:, gi:gi + 1]),
                         reads=[r_ps[pi], r_pvec], writes=[r_act[8 + gi]])
                ygf = hn_sb[:].rearrange("p k n -> p (k n)").bitcast(F32).rearrange("p (k n) -> p k n", k=4)
                for kt in range(4):
                    A_, B_, C_ = sg_sb[0], sg_sb[1], rstd_sb
                    P.dma("sp", lambda e, kt=kt, tok=tok: e.dma_start(out=A_[:], in_=ydT[0, kt * 128:(kt + 1) * 128, tok]), [r_yd[0][kt][t]], [r_sg[0]], key="ya")
                    P.dma("sp", lambda e, kt=kt, tok=tok: e.dma_start(out=B_[:], in_=ydT[1, kt * 128:(kt + 1) * 128, tok]), [r_yd[1][kt][t]], [r_sg[1]], key="yb")
                    P.dma("sp", lambda e, kt=kt, tok=tok: e.dma_start(out=C_[:], in_=zT[512 + kt * 128: 512 + (kt + 1) * 128, tok]), [r_zT[t][4 + kt]], [r_rstd], key="yu")
                    P.op("pool", lambda e: e.tensor_tensor(A_[:], A_[:], B_[:], ALU.add), reads=[r_sg[0], r_sg[1]], writes=[r_sg[0]])
                    P.op("dve", lambda e, kt=kt: e.scalar_tensor_tensor(out=A_[:], in0=C_[:], scalar=pvec[:, 4 + kt: 5 + kt], in1=A_[:], op0=ALU.mult, op1=ALU.add),
                         reads=[r_sg[0], r_rstd, r_pvec], writes=[r_sg[0]])
                    P.op("act", lambda e: e.activation(out=B_[:], in_=A_[:], func=AF.Square), reads=[r_sg[0]], writes=[r_sg[1]])
                    P.op("dve", lambda e: e.tensor_scalar(B_[:], B_[:], 0.044715, 1.0, ALU.mult, ALU.add), reads=[r_sg[1]], writes=[r_sg[1]])
                    P.op("dve", lambda e: e.tensor_tensor(B_[:], B_[:], A_[:], ALU.mult), reads=[r_sg[0], r_sg[1]], writes=[r_sg[1]])
                    P.op("act", lambda e: e.activation(out=B_[:], in_=B_[:], func=AF.Sigmoid, scale=2.0 * math.sqrt(2.0 / PI)), reads=[r_sg[1]], writes=[r_sg[1]])
                    P.op("dve", lambda e, kt=kt: e.tensor_tensor(ygf[:, kt, :], A_[:], B_[:], ALU.mult), reads=[r_sg[0], r_sg[1]], writes=[r_hn[2 * kt], r_hn[2 * kt + 1]])
                    P.op("act", lambda e, kt=kt: e.activation(out=act_sb[:, 4 + kt, :], in_=ygf[:, kt, :], func=AF.Copy), reads=[r_hn[2 * kt], r_hn[2 * kt + 1]], writes=[r_act[4 + kt]])
                for o in range(4):
                    pi = proj8(lambda k, o=o: wglu16[:, k, o * 128:(o + 1) * 128], [r_smallw],
                               lambda k, hf: act_sb[:, 4 + k, hf * 512:(hf + 1) * 512], r_act[4:8], nk=4)
                    P.op("act", lambda e, o=o, pi=pi: e.activation(out=sg_sb[1][:], in_=ps[pi][:], func=AF.Sigmoid, bias=pvec[:, 8 + o: 9 + o]),
                         reads=[r_ps[pi], r_pvec], writes=[r_sg[1]])
                    P.op("dve", lambda e, o=o: e.tensor_tensor(act_sb[:, 12 + o, :], ygf[:, o, :], sg_sb[1][:], ALU.mult),
                         reads=[r_sg[1], r_hn[2 * o], r_hn[2 * o + 1]], writes=[r_act[12 + o]])
                for o in range(NK):
                    pi = proj8(lambda k, o=o: mx0[:, k, o * 128:(o + 1) * 128], [r_mx0],
                               lambda k, hf: act_sb[:, 8 + k, hf * 512:(hf + 1) * 512], r_act[8:16])
                    evac_copy(f_sb[:, o, :], ps[pi][:], [r_ps[pi], r_zp], [r_f[o], r_zp])
                stores_all += residual_store(layer, 3, False, t, dst, dst_regs)
            return stores_all

        def mixer_odd(layer, src, src_regs, dst, dst_regs):
            i = layer // 2
            P.barrier()
            wq16 = act_sb[:, 0:8, :]
            wk16 = act_sb[:, 8:16, :]
            r_wq = R("wq")
            r_wk = R("wk")
            P.dma("pool", lambda e: e.dma_start(out=wq16, in_=wq32[i]), [], [r_wq], key="mxq")
            P.dma("pool", lambda e: e.dma_start(out=wk16, in_=wk32[i]), [], [r_wk], key="mxk")
            P.dma("pool", lambda e: e.dma_start(out=mx0[:], in_=wv32[i]), [], [r_mx0], key="mx0")
            for t in range(ntiles):
                tok = slice(t * NT, (t + 1) * NT)
                load_norm(layer, 2, t, src, src_regs)
                for (w16, rw, dstT, rd, sc) in ((wq16, r_wq, qT, r_qT, 0.125), (wk16, r_wk, kT, r_kT, None)):
                    for o in range(NK):
                        pi = proj8(lambda k, o=o, w16=w16: w16[:, k, o * 128:(o + 1) * 128], [rw],
                                   lambda k, hf: hn_sb[:, k, hf * 512:(hf + 1) * 512], r_hn)
                        si = rot("sq", 2)
                        evac_copy(sq_sb[si][:], ps[pi][:], [r_ps[pi]], [r_sq[si]], scale=sc)
                        P.dma("pool", lambda e, o=o, si=si, dstT=dstT, tok=tok: e.dma_start(out=dstT[o * 128:(o + 1) * 128, tok], in_=sq_sb[si][:]),
                              [r_sq[si]], [rd[t]], key="qk%d" % si)
                for tb in range(NT // 128):
                    pi = rot("ps", NPS)
                    for hf in range(2):
                        for k in range(NK):
                            P.op("pe", lambda e, tb=tb, hf=hf, k=k, pi=pi: e.matmul(ps[pi][:, hf * 512:(hf + 1) * 512], hn_sb[:, k, tb * 128:(tb + 1) * 128],
                                                                                    mx0[:, k, hf * 512:(hf + 1) * 512], start=(k == 0), stop=(k == NK - 1)),
                                 reads=[r_hn[k], r_mx0], writes=[r_ps[pi]])
                    si = rot("sq", 2)
                    evac_copy(sq_sb[si][:], ps[pi][:], [r_ps[pi]], [r_sq[si]])
                    P.dma("pool", lambda e, tb=tb, si=si, t=t: e.dma_start(out=vtok[t * NT + tb * 128: t * NT + (tb + 1) * 128, :], in_=sq_sb[si][:]),
                          [r_sq[si]], [r_vt[t]], key="qk%d" % si)
            P.barrier()
            P.dma("pool", lambda e: e.dma_start(out=mx0[:], in_=wo32[i]), [], [r_mx0], key="mx0")
            U16 = parbuf[:].bitcast(BF16)[:, 0:8 * 960].rearrange("p (h n c) -> p h n c", h=8, n=15)
            r_U = R("U16")
            cm = carry
            colm = sg_sb[1]
            P.dma("sp", lambda e: e.dma_start(out=colm[:, 0:64], in_=colmask32), [], [r_sg[1]], key="colm")
            for hm in range(8):
                P.dma("sp", lambda e, hm=hm: e.dma_start(out=sg_sb[0][:, 0:960], in_=rpbU32[i, hm]), [], [r_sg[0]], key="rpbu")
                P.op("dve", lambda e, hm=hm: e.tensor_tensor(U16[:, hm, :, :], sg_sb[0][:, 0:960].rearrange("p (n c) -> p n c", n=15),
                                                             colm[:, 0:64].unsqueeze(1).to_broadcast([128, 15, 64]), ALU.add),
                     reads=[r_sg[0], r_sg[1]], writes=[r_U])
            kflat = act_sb[:, 0:12, :].rearrange("p a b -> p (a b)").rearrange("p (k n) -> p k n", k=8)
            r_kb = R("kband")
            NVS = 10
            r_vs = [R("vs%d" % s) for s in range(NVS)]
            vstate = {}
            pt_sb = [pt_buf[:, q_, :] for q_ in range(3)]
            r_pt = [R("pt0"), R("pt1"), R("pt2")]
            cnt = {"pt": 0, "ps_s": 0, "ps_o": 0, "q": 0, "rec": 0}
            kT3 = kT.rearrange("(k p) l -> p k l", p=128)
            qT3 = qT.rearrange("(k p) l -> p k l", p=128)
            stores_all = []

            def ensure_v(startrow):
                s = startrow % NVS
                if vstate.get(s) == startrow:
                    return s
                vstate[s] = startrow
                tl = (startrow * 64) // NT
                th = (startrow * 64 + 127) // NT
                P.dma("sp", lambda e, s=s, startrow=startrow: e.dma_start(out=act_sb[:, 12 + s, :], in_=vtok[startrow * 64: startrow * 64 + 128, :]),
                      [r_vt[tl], r_vt[th]], [r_vs[s]], key="vs%d" % s)
                return s

            nrows_run = Lr // GRID_W if ntiles == NTILES else (Lr // GRID_W - 16)
            for b in range(nrows_run // 16):
                t = b
                load_h(t, src, src_regs)
                ra = max(16 * b - 4, 0)
                rb = min(16 * b + 20, NROWS)
                nkb = (rb - ra) * 64
                tls = sorted(set([(ra * 64) // NT, (rb * 64 - 1) // NT, b]))
                P.dma("sp", lambda e, ra=ra, nkb=nkb: e.dma_start(out=kflat[:, :, 0:nkb], in_=kT3[:, :, ra * 64: ra * 64 + nkb]),
                      [r_kT[x] for x in tls], [r_kb], key="kband")
                rowinfo = {}

                def row_prologue(rr_, b=b, ra=ra, rowinfo=rowinfo):
                    r = 16 * b + rr_
                    r0 = min(max(r - 4, 0), NROWS - 8)
                    qi = cnt["q"] % 2
                    cnt["q"] += 1
                    P.dma("sp", lambda e, qi=qi, r=r: e.dma_start(out=qrow[:, qi, :, :], in_=qT3[:, :, r * 64:(r + 1) * 64]),
                          [r_qT[b]], [r_qrow[qi]], key="qrow%d" % qi)
                    vsl = [ensure_v(r0 + 2 * kt) for kt in range(4)]
                    rowinfo[rr_] = dict(n0=r0 - r + 7, koff=(r0 - ra) * 64, qi=qi, vsl=vsl)

                row_prologue(0)
                row_prologue(1)

                def SA(n, b=b, ra=ra, rowinfo=rowinfo, row_prologue=row_prologue):
                    rr_, h = divmod(n, NH)
                    if h == 0 and rr_ >= 1 and rr_ + 1 < 16:
                        row_prologue(rr_ + 1)
                    ri_ = rowinfo[rr_]
                    n0, koff, qi = ri_["n0"], ri_["koff"], ri_["qi"]
                    hk = h // 2
                    pr = slice((h % 2) * 64, (h % 2) * 64 + 64)
                    pS = cnt["ps_s"] % 2
                    cnt["ps_s"] += 1
                    for kt in range(4):
                        P.op("pe", lambda e, kt=kt: e.matmul(
                            ps[pS][:, kt * 64:(kt + 1) * 64], kflat[pr, hk, koff + kt * 128: koff + (kt + 1) * 128], qrow[pr, qi, hk, :], start=True, stop=False),
                             reads=[r_kb, r_qrow[qi]], writes=[r_ps[pS]])
                        P.op("pe", lambda e, kt=kt: e.matmul(
                            ps[pS][:, kt * 64:(kt + 1) * 64], U16[pr, h // 2, n0 + 2 * kt: n0 + 2 * kt + 2, :].rearrange("p n c -> p (n c)"),
                            ident16[pr, pr], start=False, stop=True),
                             reads=[r_U, r_const], writes=[r_ps[pS]])
                    pti = cnt["pt"] % 3
                    cnt["pt"] += 1
                    P.op("act", lambda e: e.activation(out=pt_sb[pti], in_=ps[pS][:, 0:256], func=AF.Exp),
                         reads=[r_ps[pS]], writes=[r_pt[pti]])
                    ri_[("pti", h)] = pti

                def SB(n, rowinfo=rowinfo):
                    rr_, h = divmod(n, NH)
                    ri_ = rowinfo[rr_]
                    vsl = ri_["vsl"]
                    pti = ri_[("pti", h)]
                    hk = h // 2
                    pr = slice((h % 2) * 64, (h % 2) * 64 + 64)
                    pO = 2 + cnt["ps_o"] % 2
                    cnt["ps_o"] += 1
                    for kt in range(4):
                        P.op("pe", lambda e, kt=kt, s=vsl[kt]: e.matmul(
                            ps[pO][:, 0:64], act_sb[:, 12 + s, hk * 128:(hk + 1) * 128], pt_sb[pti][:, kt * 64:(kt + 1) * 64], start=(kt == 0), stop=(kt == 3)),
                             reads=[r_vs[vsl[kt]], r_pt[pti]], writes=[r_ps[pO]])
                    for kt in range(4):
                        P.op("pe", lambda e, kt=kt: e.matmul(
                            ps[pO][:, 64:128], ones_sb[:], pt_sb[pti][:, kt * 64:(kt + 1) * 64], start=(kt == 0), stop=(kt == 3)),
                             reads=[r_const, r_pt[pti]], writes=[r_ps[pO]])
                    ri = cnt["rec"] % 2
                    cnt["rec"] += 1
                    P.op("dve", lambda e: e.reciprocal(recb[pr, ri, :], ps[pO][pr, 64:128]), reads=[r_ps[pO]], writes=[r_rec[ri]])
                    P.op("dve", lambda e: e.tensor_tensor(hn_sb[pr, hk, rr_ * 64:(rr_ + 1) * 64], ps[pO][pr, 0:64], recb[pr, ri, :], ALU.mult),
                         reads=[r_ps[pO], r_rec[ri]], writes=[r_hn[hk]])

                NI = 16 * NH
                SA(0)
                for n in range(NI):
                    if n + 1 < NI:
                        SA(n + 1)
                    SB(n)
                for o in range(NK):
                    pi = proj8(lambda k, o=o: mx0[:, k, o * 128:(o + 1) * 128], [r_mx0],
                               lambda k, hf: hn_sb[:, k, hf * 512:(hf + 1) * 512], r_hn)
                    evac_copy(f_sb[:, o, :], ps[pi][:], [r_ps[pi]], [r_f[o]])
                stores_all += residual_store(layer, 3, False, t, dst, dst_regs)
            return stores_all

        jvec_sb = sb("jvec_sb", [128, NT], F32)
        P.dma("sp", lambda e: e.dma_start(out=jvec_sb[:], in_=jvec32), [], [r_const], key="c")
        r_mt = [R("mt0"), R("mt1")]

        ffn_steps = [st_ for st_ in steps if st_[2] in "ac"]
        fids = [int(st_[1]) * 2 + (0 if st_[2] == "a" else 1) for st_ in ffn_steps]
        for fid in fids:
            for t in range(ntiles):
                for j in range(NJ):
                    wstream.append(("gu", fid, j))
                for m in range(NK):
                    wstream.append(("d", fid, m))

        r_x = [[R("x%d_%d" % (t, k)) for k in range(NK)] for t in range(NTILES)]
        final = []
        if fids:
            cast_ffn_weights(fids[0])
        nf = 0
        for si, step in enumerate(steps):
            layer = int(step[1])
            first = (si == 0)
            last = (si == len(steps) - 1)
            src = xT if first else hT
            dst = outT if last else hT
            sregs = r_x if first else r_hT
            if step[2] in "ac":
                f = 0 if step[2] == "a" else 1
                nf += 1
                if nf < len(fids):
                    cast_ffn_weights(fids[nf])
                stores = ffn_step(layer, f, src, sregs, dst, r_hT)
            elif layer % 2 == 0:
                stores = mixer_even(layer, src, sregs, dst, r_hT)
                P.barrier()
            else:
                stores = mixer_odd(layer, src, sregs, dst, r_hT)
                P.barrier()
            if last:
                final += stores
        P.emit(final)
    return nc


_CACHE = {}


def _prep_weights(inputs):
    f32 = np.float32
    A = np.ascontiguousarray
    out = {}
    out["gT"] = A(inputs["norm_g"].reshape(DEPTH, 6, NK, 128).transpose(3, 0, 1, 2).reshape(128, DEPTH * 6 * NK))
    wg = inputs["ffn_w_gate"].reshape(DEPTH, 2, NK, 128, NJ, 128)
    wu = inputs["ffn_w_up"].reshape(DEPTH, 2, NK, 128, NJ, 128)
    wgu = np.stack([wg, wu], axis=0)
    out["wgu"] = A(wgu.transpose(1, 2, 5, 4, 0, 3, 6)).reshape(DEPTH * 2 * NJ, 128, 2 * NK * 128)
    wd = inputs["ffn_w_down"].reshape(DEPTH, 2, NJ, 128, NK, 128)
    out["wd"] = A(wd.transpose(0, 1, 4, 3, 2, 5)).reshape(DEPTH * 2 * NK, 128, NJ * 128)
    out["ident"] = np.eye(128, dtype=f32)
    sh = np.zeros((128, 128), f32)
    for k in range(128):
        sh[k, (k + 64) % 128] = 1.0
    out["shiftm"] = sh
    out["jvec"] = A(np.broadcast_to(np.arange(NT, dtype=f32)[None, :], (128, NT)))
    cs = np.zeros((128, 32), f32)
    cs[:64, 0] = 1.0
    cs[64:, 0] = -1.0
    cs[:64, 1] = -1.0
    cs[64:, 1] = 1.0
    for g8 in range(8):
        cs[g8 * 16:(g8 + 1) * 16, 2 + g8] = 1.0
        cs[g8 * 16:(g8 + 1) * 16, 10 + g8] = -1.0
    cs[:, 18] = 0.25
    out["csts"] = cs
    cinv = np.zeros((2, 4, 8), f32)
    tt = np.arange(L)
    for gi, w in enumerate((2, 4, 8, 16)):
        lo = w // 2
        hi = w - 1 - lo
        cntv = (np.clip(tt + hi + 1, 0, L) - np.clip(tt - lo, 0, L)).astype(f32)
        cinv[0, gi] = 1.0 / cntv[:8]
        cinv[1, gi] = 1.0 / cntv[L - 8:]
    out["cinv"] = A(np.broadcast_to(cinv.reshape(1, 64), (128, 64)))

    def pk(w):
        n, kk, c = w.shape
        return A(w.reshape(n, kk // 128, 128, c).transpose(0, 2, 1, 3))
    out["w_in"] = pk(inputs["ab_w_in"])
    out["w_out"] = pk(inputs["ab_w_out"])
    out["w_glu"] = pk(inputs["ssm_w_glu"])
    out["pool_w"] = A(inputs["pool_w"].transpose(0, 2, 1, 3))
    pv = np.zeros((2, 128, 12), f32)
    pv[:, :, 0:4] = inputs["pool_scale"].reshape(2, 4, 128).transpose(0, 2, 1)
    pv[:, :, 4:8] = inputs["ssm_D"].reshape(2, 4, 128).transpose(0, 2, 1)
    pv[:, :, 8:12] = inputs["ssm_b_glu"].reshape(2, 4, 128).transpose(0, 2, 1)
    out["pvec"] = pv

    def dup(a):
        a = a.transpose(0, 1, 3, 2)
        return np.concatenate([a, a], axis=2)
    s5a = np.stack([dup(inputs["ssm_A_re"]), dup(inputs["ssm_A_im"]),
                    np.broadcast_to(inputs["ssm_log_dt"][:, :, None, :], (2, 2, 128, 32))], axis=2)
    out["s5a"] = A(s5a.astype(f32))
    bre = inputs["ssm_B_re"].transpose(0, 1, 3, 2, 4).reshape(2, 2, 64, 512)
    bim = inputs["ssm_B_im"].transpose(0, 1, 3, 2, 4).reshape(2, 2, 64, 512)
    cre = inputs["ssm_C_re"].transpose(0, 1, 4, 2, 3).reshape(2, 2, 64, 512)
    cim = inputs["ssm_C_im"].transpose(0, 1, 4, 2, 3).reshape(2, 2, 64, 512)
    s5b = np.stack([np.concatenate([bre, bim], axis=2), np.concatenate([bim, bre], axis=2),
                    np.concatenate([cre, cim], axis=2), np.concatenate([cim, cre], axis=2)], axis=2)
    out["s5b"] = A(s5b.astype(f32))
    wqkv = inputs["na_w_qkv"]
    out["wq"] = pk(wqkv[:, :, 0:D])
    out["wk"] = pk(wqkv[:, :, D:2 * D])
    out["wv"] = pk(wqkv[:, :, 2 * D:3 * D])
    out["wo"] = pk(inputs["na_w_out"])
    c = np.arange(64)
    cj = np.arange(64)
    idx = np.clip(cj[None, :] - c[:, None] + 15, 0, 30)
    rp = inputs["na_rpb"][:, :, :, idx]
    rp = rp.transpose(0, 1, 3, 2, 4).reshape(2, 8, 2, 64, 15 * 64)
    out["rpbU"] = A(rp.reshape(2, 8, 128, 960))
    c0 = np.clip(c - 8, 0, 48)
    valid = (cj[None, :] >= c0[:, None]) & (cj[None, :] < c0[:, None] + 16)
    cmask = np.where(valid, 0.0, -30000.0).astype(f32)
    out["colmask"] = A(np.concatenate([cmask, cmask], axis=0))
    return out


def kernel(**inputs):
    x = inputs["x"]
    B = x.shape[0]
    if "nc" not in _CACHE:
        _CACHE["nc"] = build_program(ALL_STEPS)
    nc = _CACHE["nc"]
    shared = _prep_weights(inputs)
    in_maps = []
    for b in range(B):
        m = dict(shared)
        m["xT"] = np.ascontiguousarray(x[b].T)
        in_maps.append(m)
    res = run_bass_kernel_spmd(nc, in_maps, core_ids=list(range(B)))
    out = np.stack([np.ascontiguousarray(r["outT"].T) for r in res.results], axis=0)
    return out.astype(np.float32)
```

```python
import numpy as np
import os
DBG = int(os.environ.get('DBG', '9'))
import concourse.bass as bass
import concourse.mybir as mybir
from concourse.bass_utils import run_bass_kernel_spmd

F32 = mybir.dt.float32
BF16 = mybir.dt.bfloat16
AF = mybir.ActivationFunctionType
ALU = mybir.AluOpType

D = 1024
L = 8192
DFF = 2816
NJ = DFF // 128
NK = D // 128
DEPTH = 4
NT = 1024
NTILES = L // NT
EPS = 1e-6

ALL_STEPS = ["L%d%s" % (l, c) for l in range(DEPTH) for c in "amc"]


class Region:
    __slots__ = ("name", "last_w", "readers")

    def __init__(self, name):
        self.name = name
        self.last_w = None
        self.readers = {}


class Op:
    __slots__ = ("eng", "fn", "deps", "is_dma", "key", "signals", "sigval", "idx")

    def __init__(self, eng, fn, is_dma=False, key=None):
        self.eng = eng
        self.fn = fn
        self.deps = []
        self.is_dma = is_dma
        self.key = key
        self.signals = False
        self.sigval = 0
        self.idx = 0


class Prog:
    ENGS = ("pe", "act", "dve", "pool", "sp")

    def __init__(self, nc):
        self.nc = nc
        self.ops = []
        self.dma_last = {}
        self.last_op = {}
        self.pending = {}

    def _track(self, op, reads, writes):
        deps = set()
        for r in reads:
            if r.last_w is not None:
                deps.add(r.last_w)
        for w in writes:
            if w.last_w is not None:
                deps.add(w.last_w)
            for o in w.readers.values():
                deps.add(o)
        if op.eng in self.pending:
            for o in self.pending.pop(op.eng):
                deps.add(o)
        deps.discard(op)
        op.deps = list(deps)
        if not op.is_dma:
            self.last_op[op.eng] = op
        for r in reads:
            r.readers[op.eng if not op.is_dma else ("dma", id(op))] = op
        for w in writes:
            w.last_w = op
            w.readers = {}
        op.idx = len(self.ops)
        self.ops.append(op)

    def barrier(self):
        allops = list(self.last_op.values()) + list(self.dma_last.values())
        self.pending = {e: list(allops) for e in self.ENGS}

    def op(self, eng, fn, reads=(), writes=()):
        o = Op(eng, fn)
        self._track(o, reads, writes)
        return o

    def dma(self, queue, fn, reads, writes, key):
        o = Op(queue, fn, is_dma=True, key=key)
        o.signals = True
        self._track(o, reads, writes)
        prev = self.dma_last.get(key)
        if prev is not None and prev not in o.deps:
            o.deps.append(prev)
        self.dma_last[key] = o
        return o

    def emit(self, final_dma_ops):
        nc = self.nc
        for o in self.ops:
            for d in o.deps:
                if d.is_dma:
                    d.signals = True
                elif d.eng == o.eng and not o.is_dma and d.eng == "pe":
                    continue
                else:
                    d.signals = True
        for o in final_dma_ops:
            o.signals = True
        cnt = {}
        keys = []
        for o in self.ops:
            if not o.signals:
                continue
            k = ("dma", o.key) if o.is_dma else ("eng", o.eng)
            if k not in cnt:
                cnt[k] = 0
                keys.append(k)
            cnt[k] += 16 if o.is_dma else 1
            o.sigval = cnt[k]
        self.nsig = dict(cnt)
        import contextlib
        with contextlib.ExitStack() as st:
            sems = {}
            for i, k in enumerate(keys):
                sems[k] = st.enter_context(nc.semaphore("s%d" % i))
            block = st.enter_context(nc.Block())
            per_eng = {e: [o for o in self.ops if o.eng == e] for e in self.ENGS}

            def run(eng_name, e):
                waited = {}
                for o in per_eng[eng_name]:
                    for d in o.deps:
                        if not d.signals:
                            continue
                        if (not d.is_dma) and (not o.is_dma) and d.eng == "pe" and o.eng == "pe":
                            continue
                        k = ("dma", d.key) if d.is_dma else ("eng", d.eng)
                        if waited.get(k, 0) >= d.sigval:
                            continue
                        waited[k] = d.sigval
                        e.wait_ge(sems[k], d.sigval)
                    inst = o.fn(e)
                    if o.signals:
                        k = ("dma", o.key) if o.is_dma else ("eng", o.eng)
                        inst.then_inc(sems[k], 16 if o.is_dma else 1)
                if eng_name == "pool":
                    for o in final_dma_ops:
                        k = ("dma", o.key)
                        if waited.get(k, 0) < o.sigval:
                            waited[k] = o.sigval
                            e.wait_ge(sems[k], o.sigval)

            @block.tensor
            def _(e):
                run("pe", e)

            @block.scalar
            def _(e):
                run("act", e)

            @block.vector
            def _(e):
                run("dve", e)

            @block.gpsimd
            def _(e):
                run("pool", e)

            @block.sync
            def _(e):
                run("sp", e)


import math
I32 = mybir.dt.int32
GRID_W = 64
NROWS = L // GRID_W
NH = 16
PI = math.pi


def build_program(steps, ntiles=NTILES):
    nc = bass.Bass("TRN2", target_bir_lowering=False)
    P = Prog(nc)
    Lr = ntiles * NT

    def dram_in(name, shape, dt=F32):
        return nc.dram_tensor(name, list(shape), dt, kind="ExternalInput").ap()

    xT = dram_in("xT", [D, L])
    gT = dram_in("gT", [128, DEPTH * 6 * NK])
    wgu32 = dram_in("wgu", [DEPTH * 2 * NJ, 128, 2 * NK * 128])
    wd32 = dram_in("wd", [DEPTH * 2 * NK, 128, NJ * 128])
    ident32 = dram_in("ident", [128, 128])
    shift32 = dram_in("shiftm", [128, 128])
    jvec32 = dram_in("jvec", [128, NT])
    csts32 = dram_in("csts", [128, 32])
    cinv32 = dram_in("cinv", [128, 2 * 4 * 8])
    w_in32 = dram_in("w_in", [2, 128, NK, D])
    w_out32 = dram_in("w_out", [2, 128, NK, D])
    w_glu32 = dram_in("w_glu", [2, 128, 4, 512])
    pool_w32 = dram_in("pool_w", [2, 128, 4, 128])
    pvec32 = dram_in("pvec", [2, 128, 12])
    s5a32 = dram_in("s5a", [2, 2, 3, 128, 32])
    s5b32 = dram_in("s5b", [2, 2, 4, 128, 512])
    wq32 = dram_in("wq", [2, 128, NK, D])
    wk32 = dram_in("wk", [2, 128, NK, D])
    wv32 = dram_in("wv", [2, 128, NK, D])
    wo32 = dram_in("wo", [2, 128, NK, D])
    rpbU32 = dram_in("rpbU", [2, 8, 128, 15 * 64])
    colmask32 = dram_in("colmask", [128, 64])
    outT = nc.dram_tensor("outT", [D, L], F32, kind="ExternalOutput").ap()

    hT = nc.dram_tensor("hT", [D, L], F32).ap()
    wgu16 = nc.dram_tensor("wgu16", [DEPTH * 2 * NJ, 128, 2 * NK * 128], BF16).ap()
    wd16 = nc.dram_tensor("wd16", [DEPTH * 2 * NK, 128, NJ * 128], BF16).ap()
    zT = nc.dram_tensor("zT", [D, L], F32).ap()
    ydT = nc.dram_tensor("ydT", [2, 512, L], F32).ap()
    qT = nc.dram_tensor("qT", [D, L], BF16).ap()
    kT = nc.dram_tensor("kT", [D, L], BF16).ap()
    vtok = nc.dram_tensor("vtok", [L, D], BF16).ap()

    import contextlib
    st = contextlib.ExitStack()
    with st:
        def sb(name, shape, dt):
            return st.enter_context(nc.sbuf_tensor(name, list(shape), dt))

        h_sb = sb("h_sb", [128, NK, NT], F32)
        f_sb = sb("f_sb", [128, NK, NT], F32)
        hn_sb = sb("hn_sb", [128, NK, NT], BF16)
        act_sb = sb("act_sb", [128, NJ, NT], BF16)
        NWGU = 3
        NWD = 2
        wgu_sb = [sb("wgu%d" % i, [128, 2 * NK * 128], BF16) for i in range(NWGU)]
        wd_sb = [sb("wd%d" % i, [128, NJ * 128], BF16) for i in range(NWD)]
        sq_sb = [sb("sq%d" % i, [128, NT], BF16) for i in range(2)]
        sg_sb = [sb("sg%d" % i, [128, NT], F32) for i in range(2)]
        rstd_sb = sb("rstd", [128, NT], F32)
        ones_sb = sb("ones", [128, 128], BF16)
        ident16 = sb("ident16", [128, 128], BF16)
        identf = sb("identf", [128, 128], F32)
        shiftf = sb("shiftf", [128, 128], F32)
        csts = sb("csts_sb", [128, 32], F32)
        g_sb = sb("g_sb", [128, DEPTH * 6 * NK], F32)
        gh_sb = sb("gh_sb", [128, DEPTH * 6 * NK], F32)
        mixbuf = sb("mixbuf", [128, NK, NT], F32)
        mixflat = mixbuf[:].rearrange("p k n -> p (k n)")
        mx0 = mixflat[:, 0:4096].bitcast(BF16).rearrange("p (k n) -> p k n", k=NK)
        parbuf = mixflat[:, 4096:8192]
        h_sb2 = mixbuf
        pt_buf = sb("pt_buf", [128, 3, 256], BF16)
        pvec = sb("pvec_sb", [128, 12], F32)
        cinv = sb("cinv_sb", [128, 64], F32)
        qrow = sb("qrow", [128, 2, NK, 64], BF16)
        carry = sb("carry", [128, 4], F32)
        recb = sb("recb", [128, 2, 64], F32)
        NPS = 4
        ps = [st.enter_context(nc.psum_tensor("ps%d" % i, [128, NT], F32)) for i in range(NPS)]

        R = Region
        r_h = [R("h%d" % k) for k in range(NK)]
        r_h2 = [R("h2_%d" % k) for k in range(NK)]
        r_f = [R("f%d" % k) for k in range(NK)]
        r_hn = [R("hn%d" % k) for k in range(NK)]
        r_act = [R("act%d" % j) for j in range(NJ)]
        r_wgu = [R("wgu%d" % i) for i in range(NWGU)]
        r_wd = [R("wd%d" % i) for i in range(NWD)]
        r_sq = [R("sq%d" % i) for i in range(2)]
        r_sg = [R("sg%d" % i) for i in range(2)]
        r_rstd = R("rstd")
        r_ps = [R("ps%d" % i) for i in range(NPS)]
        r_const = R("const")
        r_g = R("g")
        r_mx0 = R("mx0")
        r_par = R("par")
        r_smallw = R("smallw")
        r_pvec = R("pvec")
        r_qrow = [R("qrow0"), R("qrow1")]
        r_carry = R("carry")
        r_rec = [R("rec0"), R("rec1")]
        r_hT = [[R("hT%d_%d" % (t, k)) for k in range(NK)] for t in range(NTILES)]
        r_zT = [[R("zT%d_%d" % (t, k)) for k in range(NK)] for t in range(NTILES)]
        r_yd = [[[R("yd%d_%d_%d" % (d, k, c)) for c in range(NTILES)] for k in range(4)] for d in range(2)]
        r_qT = [R("qT%d" % t) for t in range(NTILES)]
        r_kT = [R("kT%d" % t) for t in range(NTILES)]
        r_vt = [R("vt%d" % t) for t in range(NTILES)]
        r_wgu16 = {}
        r_wd16 = {}

        state = {"ps": 0, "sq": 0, "sg": 0, "wgu": 0, "wd": 0}

        def rot(name, n):
            v = state[name]
            state[name] = (v + 1) % n
            return v

        P.op("dve", lambda e: e.memset(ones_sb[:], 1.0), writes=[r_const])
        P.dma("sp", lambda e: e.dma_start(out=g_sb[:], in_=gT), [], [r_g], key="g")
        P.op("dve", lambda e: e.tensor_scalar(gh_sb[:], g_sb[:], 0.5, None, ALU.mult), reads=[r_g], writes=[r_g])
        P.dma("sp", lambda e: e.dma_start(out=identf[:], in_=ident32), [], [r_const], key="c")
        P.dma("sp", lambda e: e.dma_start(out=shiftf[:], in_=shift32), [], [r_const], key="c")
        P.dma("sp", lambda e: e.dma_start(out=csts[:], in_=csts32), [], [r_const], key="c")
        P.dma("sp", lambda e: e.dma_start(out=cinv[:], in_=cinv32), [], [r_const], key="c")
        P.op("dve", lambda e: e.tensor_copy(ident16[:], identf[:]), reads=[r_const], writes=[r_const])

        cast_rr = {"i": 0}

        def cast_ffn_weights(fid):
            for j in range(NJ):
                kk = cast_rr["i"] % 8
                cast_rr["i"] += 1
                r_wgu16[(fid, j)] = R("wgu16_%d_%d" % (fid, j))
                P.dma("pool", lambda e, j=j: e.dma_start(out=wgu16[fid * NJ + j], in_=wgu32[fid * NJ + j]),
                      [], [r_wgu16[(fid, j)]], key="cast%d" % kk)
            for m in range(NK):
                kk = cast_rr["i"] % 8
                cast_rr["i"] += 1
                r_wd16[(fid, m)] = R("wd16_%d_%d" % (fid, m))
                P.dma("pool", lambda e, m=m: e.dma_start(out=wd16[fid * NK + m], in_=wd32[fid * NK + m]),
                      [], [r_wd16[(fid, m)]], key="cast%d" % kk)

        wstream = []
        wpos = {"issued": 0}
        wslot = {}
        slot_prev = {}

        def issue_weight_loads(pos):
            while wpos["issued"] < len(wstream) and wpos["issued"] <= pos + 4:
                q = wpos["issued"]
                kind, fid, idx = wstream[q]
                n = NWGU if kind == "gu" else NWD
                s = state["wgu" if kind == "gu" else "wd"]
                if slot_prev.get((kind, s), -1) >= pos:
                    break
                rot("wgu" if kind == "gu" else "wd", n)
                slot_prev[(kind, s)] = q
                wslot[q] = s
                if kind == "gu":
                    P.dma("sp", lambda e, s=s, fid=fid, idx=idx: e.dma_start(out=wgu_sb[s][:], in_=wgu16[fid * NJ + idx]),
                          [r_wgu16[(fid, idx)]], [r_wgu[s]], key="wgu%d" % s)
                else:
                    P.dma("sp", lambda e, s=s, fid=fid, idx=idx: e.dma_start(out=wd_sb[s][:], in_=wd16[fid * NK + idx]),
                          [r_wd16[(fid, idx)]], [r_wd[s]], key="wd%d" % s)
                wpos["issued"] += 1

        wcur = {"pos": 0}

        def next_weight(kind, fid, idx):
            pos = wcur["pos"]
            assert wstream[pos] == (kind, fid, idx), (wstream[pos], kind, fid, idx)
            issue_weight_loads(pos)
            assert pos in wslot
            wcur["pos"] += 1
            return wslot[pos]

        def gcol(layer, n, k, half=False):
            c = (layer * 6 + n) * NK + k
            tt = gh_sb if half else g_sb
            return tt[:, c:c + 1]

        def rms_stats(src_tiles, src_regs):
            pi = rot("ps", NPS)
            for k in range(NK):
                si = rot("sq", 2)
                P.op("act", lambda e, k=k, si=si: e.activation(out=sq_sb[si][:], in_=src_tiles[k], func=AF.Square),
                     reads=[src_regs[k]], writes=[r_sq[si]])
                for hf in range(NT // 512):
                    P.op("pe", lambda e, k=k, si=si, hf=hf, pi=pi: e.matmul(ps[pi][:, hf * 512:(hf + 1) * 512], ones_sb[:], sq_sb[si][:, hf * 512:(hf + 1) * 512],
                                                                            start=(k == 0), stop=(k == NK - 1)),
                         reads=[r_sq[si], r_const], writes=[r_ps[pi]])
            P.op("dve", lambda e, pi=pi: e.tensor_scalar(rstd_sb[:], ps[pi][:], 1.0 / D, EPS, ALU.mult, ALU.add),
                 reads=[r_ps[pi]], writes=[r_rstd])
            P.op("act", lambda e: e.activation(out=rstd_sb[:], in_=rstd_sb[:], func=AF.Sqrt), reads=[r_rstd], writes=[r_rstd])
            P.op("dve", lambda e: e.reciprocal(rstd_sb[:], rstd_sb[:]), reads=[r_rstd], writes=[r_rstd])

        def load_h(t, src, src_regs, hb=0):
            tok = slice(t * NT, (t + 1) * NT)
            hbuf = h_sb if hb == 0 else h_sb2
            rh = r_h if hb == 0 else r_h2
            for k in range(NK):
                P.dma("sp", lambda e, k=k: e.dma_start(out=hbuf[:, k, :], in_=src[k * 128:(k + 1) * 128, tok]),
                      [src_regs[t][k]], [rh[k]], key="hload%d" % k)

        def load_norm(layer, n, t, src, src_regs, hb=0):
            hbuf = h_sb if hb == 0 else h_sb2
            rh = r_h if hb == 0 else r_h2
            load_h(t, src, src_regs, hb)
            rms_stats([hbuf[:, k, :] for k in range(NK)], rh)
            for k in range(NK):
                P.op("dve", lambda e, k=k: e.scalar_tensor_tensor(out=hn_sb[:, k, :], in0=hbuf[:, k, :], scalar=gcol(layer, n, k),
                                                                  in1=rstd_sb[:], op0=ALU.mult, op1=ALU.mult),
                     reads=[rh[k], r_rstd, r_g], writes=[r_hn[k]])

        def proj8(wfn, wregs, rhs, rhs_regs, nk=NK):
            pi = rot("ps", NPS)
            for k in range(nk):
                for hf in range(NT // 512):
                    P.op("pe", lambda e, k=k, hf=hf, pi=pi: e.matmul(ps[pi][:, hf * 512:(hf + 1) * 512], wfn(k), rhs(k, hf),
                                                                     start=(k == 0), stop=(k == nk - 1)),
                         reads=list(wregs) + [rhs_regs[k]], writes=[r_ps[pi]])
            return pi

        def residual_store(layer, n, half, t, dst, dst_regs, hb=0):
            tok = slice(t * NT, (t + 1) * NT)
            hbuf = h_sb if hb == 0 else h_sb2
            rh = r_h if hb == 0 else r_h2
            rms_stats([f_sb[:, m, :] for m in range(NK)], r_f)
            stores = []
            for m in range(NK):
                P.op("dve", lambda e, m=m: e.scalar_tensor_tensor(out=f_sb[:, m, :], in0=f_sb[:, m, :], scalar=gcol(layer, n, m, half=half),
                                                                  in1=rstd_sb[:], op0=ALU.mult, op1=ALU.mult),
                     reads=[r_f[m], r_rstd, r_g], writes=[r_f[m]])
                eng_ = "pool" if m % 2 == 1 else "dve"
                P.op(eng_, lambda e, m=m: e.tensor_tensor(f_sb[:, m, :], f_sb[:, m, :], hbuf[:, m, :], ALU.add),
                     reads=[r_f[m], rh[m]], writes=[r_f[m]])
                o = P.dma("pool", lambda e, m=m: e.dma_start(out=dst[m * 128:(m + 1) * 128, tok], in_=f_sb[:, m, :]),
                          [r_f[m]], [dst_regs[t][m]], key="hstore%d" % m)
                stores.append(o)
            return stores

        evac_rr = {"i": 0}

        def evac_copy(out_ap, in_ap, reads, writes, scale=None):
            evac_rr["i"] += 1
            if evac_rr["i"] % 2 == 0:
                if scale is None:
                    P.op("dve", lambda e: e.tensor_copy(out_ap, in_ap), reads=reads, writes=writes)
                else:
                    P.op("dve", lambda e: e.tensor_scalar(out_ap, in_ap, scale, None, ALU.mult), reads=reads, writes=writes)
            else:
                if scale is None:
                    P.op("act", lambda e: e.activation(out=out_ap, in_=in_ap, func=AF.Copy), reads=reads, writes=writes)
                else:
                    P.op("act", lambda e: e.activation(out=out_ap, in_=in_ap, func=AF.Copy, scale=scale), reads=reads, writes=writes)

        def ffn_step(layer, f, src, src_regs, dst, dst_regs):
            fid = layer * 2 + f
            n0 = 0 if f == 0 else 4
            stores = []
            load_norm(layer, n0, 0, src, src_regs, hb=0)
            for t in range(ntiles):
                hb = t % 2
                for j in range(NJ):
                    s = next_weight("gu", fid, j)
                    pg = rot("ps", NPS)
                    pu = rot("ps", NPS)
                    for gu, pi in ((0, pg), (1, pu)):
                        for k in range(NK):
                            for hf in range(NT // 512):
                                P.op("pe", lambda e, s=s, gu=gu, k=k, hf=hf, pi=pi: e.matmul(
                                    ps[pi][:, hf * 512:(hf + 1) * 512],
                                    wgu_sb[s][:, (gu * NK + k) * 128:(gu * NK + k + 1) * 128],
                                    hn_sb[:, k, hf * 512:(hf + 1) * 512], start=(k == 0), stop=(k == NK - 1)),
                                     reads=[r_wgu[s], r_hn[k]], writes=[r_ps[pi]])
                    gi = rot("sg", 2)
                    P.op("act", lambda e, gi=gi, pg=pg: e.activation(out=sg_sb[gi][:], in_=ps[pg][:], func=AF.Silu),
                         reads=[r_ps[pg]], writes=[r_sg[gi]])
                    P.op("dve", lambda e, gi=gi, pu=pu, j=j: e.tensor_tensor(act_sb[:, j, :], sg_sb[gi][:], ps[pu][:], ALU.mult),
                         reads=[r_sg[gi], r_ps[pu]], writes=[r_act[j]])
                for m in range(NK):
                    s = next_weight("d", fid, m)
                    pi = rot("ps", NPS)
                    for j in range(NJ):
                        for hf in range(NT // 512):
                            P.op("pe", lambda e, s=s, j=j, hf=hf, pi=pi, m=m: e.matmul(
                                ps[pi][:, hf * 512:(hf + 1) * 512], wd_sb[s][:, j * 128:(j + 1) * 128],
                                act_sb[:, j, hf * 512:(hf + 1) * 512], start=(j == 0), stop=(j == NJ - 1)),
                                 reads=[r_wd[s], r_act[j]], writes=[r_ps[pi]])
                    evac_copy(f_sb[:, m, :], ps[pi][:], [r_ps[pi]], [r_f[m]])
                    if m == 2 and t + 1 < ntiles:
                        load_norm(layer, n0, t + 1, src, src_regs, hb=(t + 1) % 2)
                stores += residual_store(layer, n0 + 1, True, t, dst, dst_regs, hb=hb)
            return stores

        def mixer_even(layer, src, src_regs, dst, dst_regs):
            i = layer // 2
            assert ntiles == NTILES, "even mixer needs the full sequence"
            P.barrier()
            P.dma("pool", lambda e: e.dma_start(out=mx0[:], in_=w_in32[i]), [], [r_mx0], key="mx0")
            for t in range(ntiles):
                tok = slice(t * NT, (t + 1) * NT)
                load_norm(layer, 2, t, src, src_regs)
                for o in range(NK):
                    pi = proj8(lambda k, o=o: mx0[:, k, o * 128:(o + 1) * 128], [r_mx0],
                               lambda k, hf: hn_sb[:, k, hf * 512:(hf + 1) * 512], r_hn)
                    evac_copy(f_sb[:, o, :], ps[pi][:], [r_ps[pi]], [r_f[o]])
                    P.dma("pool", lambda e, o=o, tok=tok: e.dma_start(out=zT[o * 128:(o + 1) * 128, tok], in_=f_sb[:, o, :]),
                          [r_f[o]], [r_zT[t][o]], key="zst%d" % o)
            P.barrier()
            Ysb = h_sb[:].rearrange("p k n -> p (k n)")
            u16 = hn_sb[:].rearrange("p k n -> p (k n)")
            pb = parbuf
            c_B1, c_B2, c_C1, c_C2 = 0, 512, 1024, 1536
            c_sm = 2048
            c_bbT = 2560
            c_MT = 3072
            c_pad = 3328

            def sm(ix):
                return pb[:, c_sm + ix * 32: c_sm + (ix + 1) * 32]
            ARE, AIM, DT, MAG, ANGN, LRE, LIM, FRE, FIM, TA, TB, CT, ST, DEN, TC, TD = range(16)
            KI = sg_sb[0][:].bitcast(I32)
            Ubuf = [sg_sb[1], rstd_sb]
            rU = [r_sg[1], r_rstd]
            rKI = r_sg[0]

            def dv(fn, reads=(r_par,), writes=(r_par,)):
                P.op("dve", fn, reads=list(reads), writes=list(writes))

            def ac(fn, reads=(r_par,), writes=(r_par,)):
                P.op("act", fn, reads=list(reads), writes=list(writes))

            def sincos_small(dst_sin, dst_cos, ang_over_2pi):
                for dst, off in ((dst_sin, 0.0), (dst_cos, 0.25)):
                    dv(lambda e, off=off: e.tensor_scalar(sm(TC), ang_over_2pi, 1.0, off, ALU.mult, ALU.add))
                    dv(lambda e: e.tensor_copy(KI[:, 0:32], sm(TC)), writes=[r_par, rKI])
                    dv(lambda e: e.tensor_copy(sm(TD), KI[:, 0:32]), reads=[r_par, rKI])
                    dv(lambda e: e.tensor_tensor(sm(TC), sm(TC), sm(TD), ALU.subtract))
                    ac(lambda e, dst=dst: e.activation(out=dst, in_=sm(TC), func=AF.Sin, scale=2 * PI))

            for d in range(2):
                P.dma("sp", lambda e, d=d: e.dma_start(out=sm(ARE), in_=s5a32[i, d, 0]), [], [r_par], key="par")
                P.dma("sp", lambda e, d=d: e.dma_start(out=sm(AIM), in_=s5a32[i, d, 1]), [], [r_par], key="par")
                P.dma("sp", lambda e, d=d: e.dma_start(out=sm(DT), in_=s5a32[i, d, 2]), [], [r_par], key="par")
                for q_, c_ in enumerate((c_B1, c_B2, c_C1, c_C2)):
                    P.dma("sp", lambda e, q_=q_, c_=c_, d=d: e.dma_start(out=pb[:, c_:c_ + 512], in_=s5b32[i, d, q_]), [], [r_par], key="par")
                ac(lambda e: e.activation(out=sm(DT), in_=sm(DT), func=AF.Exp))
                dv(lambda e: e.tensor_scalar(sm(ARE), sm(ARE), -1e-4, None, ALU.min))
                dv(lambda e: e.tensor_tensor(sm(TA), sm(ARE), sm(DT), ALU.mult))
                ac(lambda e: e.activation(out=sm(MAG), in_=sm(TA), func=AF.Exp))
                dv(lambda e: e.tensor_tensor(sm(ANGN), sm(AIM), sm(DT), ALU.mult))
                dv(lambda e: e.tensor_scalar(sm(ANGN), sm(ANGN), 1.0 / (2 * PI), None, ALU.mult))
                sincos_small(sm(LIM), sm(LRE), sm(ANGN))
                dv(lambda e: e.tensor_tensor(sm(LRE), sm(LRE), sm(MAG), ALU.mult))
                dv(lambda e: e.tensor_tensor(sm(LIM), sm(LIM), sm(MAG), ALU.mult))
                dv(lambda e: e.tensor_scalar(sm(TA), sm(ANGN), float(NT), None, ALU.mult))
                sincos_small(sm(ST), sm(CT), sm(TA))
                dv(lambda e: e.tensor_scalar(sm(ST), sm(ST), csts[:, 0:1], None, ALU.mult), reads=[r_par, r_const])
                dv(lambda e: e.tensor_scalar(sm(LRE), sm(LRE), -1.0, None, ALU.add))
                dv(lambda e: e.tensor_tensor(sm(DEN), sm(ARE), sm(ARE), ALU.mult))
                dv(lambda e: e.tensor_tensor(sm(TA), sm(AIM), sm(AIM), ALU.mult))
                dv(lambda e: e.tensor_tensor(sm(DEN), sm(DEN), sm(TA), ALU.add))
                dv(lambda e: e.reciprocal(sm(DEN), sm(DEN)))
                dv(lambda e: e.tensor_tensor(sm(TA), sm(LRE), sm(ARE), ALU.mult))
                dv(lambda e: e.tensor_tensor(sm(TB), sm(LIM), sm(AIM), ALU.mult))
                dv(lambda e: e.tensor_tensor(sm(TA), sm(TA), sm(TB), ALU.add))
                dv(lambda e: e.tensor_tensor(sm(FRE), sm(TA), sm(DEN), ALU.mult))
                dv(lambda e: e.tensor_tensor(sm(TA), sm(LIM), sm(ARE), ALU.mult))
                dv(lambda e: e.tensor_tensor(sm(TB), sm(LRE), sm(AIM), ALU.mult))
                dv(lambda e: e.tensor_tensor(sm(TA), sm(TA), sm(TB), ALU.subtract))
                dv(lambda e: e.tensor_tensor(sm(FIM), sm(TA), sm(DEN), ALU.mult))
                dv(lambda e: e.tensor_scalar(sm(FIM), sm(FIM), csts[:, 1:2], None, ALU.mult), reads=[r_par, r_const])
                B1v = pb[:, c_B1:c_B1 + 512].rearrange("p (g h) -> p g h", h=16)
                B2v = pb[:, c_B2:c_B2 + 512].rearrange("p (g h) -> p g h", h=16)
                dv(lambda e: e.tensor_tensor(B1v, B1v, sm(FRE).unsqueeze(2).to_broadcast([128, 32, 16]), ALU.mult))
                dv(lambda e: e.tensor_tensor(B2v, B2v, sm(FIM).unsqueeze(2).to_broadcast([128, 32, 16]), ALU.mult))
                dv(lambda e: e.tensor_tensor(pb[:, c_B1:c_B1 + 512], pb[:, c_B1:c_B1 + 512], pb[:, c_B2:c_B2 + 512], ALU.add))
                dv(lambda e: e.tensor_scalar(pb[:, c_C1:c_C1 + 512], pb[:, c_C1:c_C1 + 512], csts[:, 0:1], None, ALU.mult), reads=[r_par, r_const])
                dv(lambda e: e.tensor_scalar(pb[:, c_C2:c_C2 + 512], pb[:, c_C2:c_C2 + 512], -1.0, None, ALU.mult))
                for gt in range(4):
                    pi = rot("ps", 3)
                    P.op("pe", lambda e, gt=gt, pi=pi: e.matmul(ps[pi][:, 0:128], pb[:, c_B1 + gt * 128: c_B1 + (gt + 1) * 128], identf[:], start=True, stop=True),
                         reads=[r_par, r_const], writes=[r_ps[pi]])
                    dv(lambda e, gt=gt, pi=pi: e.tensor_copy(pb[:, c_bbT + gt * 128: c_bbT + (gt + 1) * 128], ps[pi][:, 0:128]),
                       reads=[r_par, r_ps[pi]])
                kbuf2 = act_sb[:, 14:16, :].rearrange("p a b -> p (a b)").bitcast(F32)
                kbufs = [(sg_sb[0][:], [rKI]), (kbuf2, [r_act[14], r_act[15]])]

                def prep_g(gt, g8, stage=None):
                    g = gt * 8 + g8
                    par = g % 2
                    tabs = []
                    for which in range(2):
                        a_ = 6 + 4 * par + 2 * which
                        tabs.append((act_sb[:, a_:a_ + 2, :].rearrange("p a b -> p (a b)").bitcast(F32), [r_act[a_], r_act[a_ + 1]]))
                    (sinT, r_sin), (cosT, r_cos) = tabs
                    for st_ in ((0, 1, 2, 3) if stage is None else (stage,)):
                        if st_ > 3:
                            continue
                        for wi, (tab, rtab, off) in enumerate(((sinT, r_sin, 0.0), (cosT, r_cos, 0.25))):
                            ub = Ubuf[wi]
                            rub = rU[wi]
                            kb, rkb = kbufs[wi]
                            bcol = csts[:, 19:20] if off == 0.0 else csts[:, 18:19]
                            if st_ == 0:
                                P.op("act", lambda e, ub=ub, bcol=bcol: e.activation(out=ub[:], in_=jvec_sb[:], func=AF.Identity, scale=sm(ANGN)[:, g:g + 1], bias=bcol),
                                     reads=[r_par, r_const], writes=[rub])
                            elif st_ == 1:
                                P.op("dve", lambda e, ub=ub, kb=kb: e.tensor_scalar(kb, ub[:], 12582912.0, 12582912.0, ALU.add, ALU.subtract), reads=[rub], writes=rkb)
                            elif st_ == 2:
                                P.op("pool", lambda e, ub=ub, kb=kb: e.tensor_tensor(ub[:], ub[:], kb, ALU.subtract), reads=[rub] + rkb, writes=[rub])
                            else:
                                P.op("act", lambda e, ub=ub, tab=tab: e.activation(out=tab, in_=ub[:], func=AF.Sin, scale=2 * PI),
                                     reads=[rub], writes=rtab)
                    if stage is not None and stage != 4:
                        return None
                    padb = act_sb[:, 4 + par, :]
                    r_pad = r_act[4 + par]
                    BBp = padb[:, 0:128]
                    BBs = padb[:, 128:256]
                    C1p = padb[:, 256:384]
                    C2p = padb[:, 384:512]
                    bbT = pb[:, c_bbT + gt * 128: c_bbT + (gt + 1) * 128]
                    gm = csts[:, 2 + g8: 3 + g8]
                    ngm = csts[:, 10 + g8: 11 + g8]
                    P.op("act", lambda e: e.activation(out=BBp, in_=bbT, func=AF.Identity, scale=gm), reads=[r_par, r_const], writes=[r_pad])
                    P.op("act", lambda e: e.activation(out=BBs[:, 0:64], in_=bbT[:, 64:128], func=AF.Identity, scale=gm), reads=[r_par, r_const], writes=[r_pad])
                    P.op("act", lambda e: e.activation(out=BBs[:, 64:128], in_=bbT[:, 0:64], func=AF.Identity, scale=ngm), reads=[r_par, r_const], writes=[r_pad])
                    P.op("act", lambda e: e.activation(out=padb[:, 256:512], in_=pb[:, c_B1:c_B1 + 256], func=AF.Copy, scale=0.0), reads=[r_par], writes=[r_pad])
                    P.op("act", lambda e: e.activation(out=C1p[:, g8 * 16:(g8 + 1) * 16], in_=pb[:, c_C1 + g * 16: c_C1 + (g + 1) * 16], func=AF.Copy), reads=[r_par], writes=[r_pad])
                    P.op("act", lambda e: e.activation(out=C2p[:, g8 * 16:(g8 + 1) * 16], in_=pb[:, c_C2 + g * 16: c_C2 + (g + 1) * 16], func=AF.Copy), reads=[r_par], writes=[r_pad])
                    MT = pb[:, c_MT + par * 128: c_MT + (par + 1) * 128]
                    r_MT = r_mt[par]
                    P.op("dve", lambda e: e.tensor_scalar(MT, identf[:], sm(CT)[:, g:g + 1], None, ALU.mult), reads=[r_par, r_const], writes=[r_MT])
                    P.op("dve", lambda e: e.scalar_tensor_tensor(out=MT, in0=shiftf[:], scalar=sm(ST)[:, g:g + 1], in1=MT, op0=ALU.mult, op1=ALU.add),
                         reads=[r_par, r_const, r_MT], writes=[r_MT])
                    return dict(cos=cosT, sin=sinT, r_cos=r_cos, r_sin=r_sin, BBp=BBp, BBs=BBs, C1p=C1p, C2p=C2p, r_pad=r_pad, MT=MT, r_MT=r_MT)

                nxt_first = None
                for gt in range(4):
                    for k in range(NK):
                        P.dma("pool", lambda e, gt=gt, k=k: e.dma_start(out=hn_sb[:, k, :], in_=zT[512 + gt * 128: 512 + (gt + 1) * 128, k * NT:(k + 1) * NT]),
                              [r_zT[k][4 + gt]], [r_hn[k]], key="uld%d" % k)
                    its = []
                    for g8 in range(8):
                        for ci in range(ntiles):
                            its.append((g8, ci))
                    curs = {}
                    if gt == 0:
                        curs[0] = prep_g(0, 0)
                    else:
                        curs[0] = nxt_first
                    PA, PB, PY, PC = 0, 1, 2, 3

                    def S1pe(n):
                        g8, ci = its[n]
                        cur = curs[g8]
                        c = ci if d == 0 else ntiles - 1 - ci
                        t0 = c * NT
                        for lh, pi in ((cur["BBp"], PA), (cur["BBs"], PB)):
                            for hf in range(2):
                                P.op("pe", lambda e, lh=lh, pi=pi, hf=hf, t0=t0: e.matmul(ps[pi][:, hf * 512:(hf + 1) * 512], lh, u16[:, t0 + hf * 512: t0 + (hf + 1) * 512], start=True, stop=True),
                                     reads=[cur["r_pad"], r_hn[c]], writes=[r_ps[pi]])

                    def S1v(n):
                        g8, ci = its[n]
                        cur = curs[g8]
                        q2 = n % 2
                        Wa = f_sb[:, 0 + q2, :]
                        Wb = f_sb[:, 2 + q2, :]
                        A_in = ps[PA][:, :] if d == 0 else ps[PA][:, ::-1]
                        B_in = ps[PB][:, :] if d == 0 else ps[PB][:, ::-1]
                        P.op("dve", lambda e: e.tensor_tensor(Wa, cur["cos"], A_in, ALU.mult), reads=cur["r_cos"] + [r_ps[PA]], writes=[r_f[0 + q2]])
                        P.op("dve", lambda e: e.tensor_tensor(ps[PY][:, :], cur["sin"], B_in, ALU.mult), reads=cur["r_sin"] + [r_ps[PB]], writes=[r_ps[PY]])
                        P.op("dve", lambda e: e.tensor_tensor(Wa, Wa, ps[PY][:, :], ALU.add), reads=[r_f[0 + q2], r_ps[PY]], writes=[r_f[0 + q2]])

                    def S2(n):
                        g8, ci = its[n]
                        cur = curs[g8]
                        g = gt * 8 + g8
                        q2 = n % 2
                        Wa = f_sb[:, 0 + q2, :]
                        Wc = f_sb[:, 4 + q2, :]
                        rcol = sm(MAG)[:, g:g + 1]
                        init = 0.0 if ci == 0 else carry[:, 0:1]
                        P.op("dve", lambda e: e.tensor_tensor_scan(Wc, rcol.to_broadcast([128, NT]), Wa, init, ALU.mult, ALU.add),
                             reads=[r_f[0 + q2], r_par, r_carry], writes=[r_f[4 + q2]])
                        if ci < ntiles - 1:
                            P.op("pe", lambda e: e.matmul(ps[PC][:, 0:1], cur["MT"], Wc[:, NT - 1:NT], start=True, stop=True),
                                 reads=[cur["r_MT"], r_f[4 + q2]], writes=[r_ps[PC]])
                            P.op("act", lambda e: e.activation(out=carry[:, 0:1], in_=ps[PC][:, 0:1], func=AF.Copy), reads=[r_ps[PC]], writes=[r_carry])

                    def S3pool(n):
                        g8, ci = its[n]
                        cur = curs[g8]
                        q2 = n % 2
                        Wc = f_sb[:, 4 + q2, :]
                        p1 = act_sb[:, 0 + q2, :]
                        p2 = act_sb[:, 2 + q2, :]
                        p1o = p1 if d == 0 else p1[:, ::-1]
                        p2o = p2 if d == 0 else p2[:, ::-1]
                        P.op("pool", lambda e: e.tensor_tensor(p1o, cur["cos"], Wc, ALU.mult), reads=cur["r_cos"] + [r_f[4 + q2]], writes=[r_act[0 + q2]])
                        P.op("pool", lambda e: e.tensor_tensor(p2o, cur["sin"], Wc, ALU.mult), reads=cur["r_sin"] + [r_f[4 + q2]], writes=[r_act[2 + q2]])

                    def S3pe(n):
                        g8, ci = its[n]
                        cur = curs[g8]
                        q2 = n % 2
                        p1 = act_sb[:, 0 + q2, :]
                        p2 = act_sb[:, 2 + q2, :]
                        for hf in range(2):
                            P.op("pe", lambda e, hf=hf: e.matmul(ps[PY][:, hf * 512:(hf + 1) * 512], cur["C1p"], p1[:, hf * 512:(hf + 1) * 512], start=True, stop=False),
                                 reads=[cur["r_pad"], r_act[0 + q2]], writes=[r_ps[PY]])
                            P.op("pe", lambda e, hf=hf: e.matmul(ps[PY][:, hf * 512:(hf + 1) * 512], cur["C2p"], p2[:, hf * 512:(hf + 1) * 512], start=False, stop=True),
                                 reads=[cur["r_pad"], r_act[2 + q2]], writes=[r_ps[PY]])

                    def S4(n):
                        g8, ci = its[n]
                        c = ci if d == 0 else ntiles - 1 - ci
                        t0 = c * NT
                        q2 = n % 2
                        Yst = f_sb[:, 6 + q2, :]
                        P.op("act", lambda e: e.activation(out=Yst, in_=ps[PY][:], func=AF.Copy), reads=[r_ps[PY]], writes=[r_f[6 + q2]])
                        row0 = gt * 128 + g8 * 16
                        P.dma("sp", lambda e, d=d: e.dma_start(out=ydT[d, row0:row0 + 16, t0:t0 + NT], in_=Yst[g8 * 16:(g8 + 1) * 16, :]),
                              [r_f[6 + q2]], [r_yd[d][gt][c]], key="yst%d" % q2)

                    NI = len(its)
                    S1pe(0)
                    S1v(0)
                    S1pe(1)
                    for n in range(NI + 1):
                        if n < NI:
                            g8, ci = its[n]
                            if 1 <= ci <= 5 and not (g8 == 7 and gt == 3):
                                ngt, ng8 = (gt, g8 + 1) if g8 < 7 else (gt + 1, 0)
                                res_ = prep_g(ngt, ng8, stage=ci - 1)
                                if ci == 5:
                                    if g8 < 7:
                                        curs[g8 + 1] = res_
                                    else:
                                        nxt_first = res_
                        if n + 1 < NI:
                            S1v(n + 1)
                        if n + 2 < NI:
                            S1pe(n + 2)
                        if n >= 1:
                            S3pe(n - 1)
                            S4(n - 1)
                        if n < NI:
                            S2(n)
                            S3pool(n)
            P.barrier()
            P.dma("pool", lambda e: e.dma_start(out=mx0[:], in_=w_out32[i]), [], [r_mx0], key="mx0")
            smallw = parbuf[:].bitcast(BF16)
            wglu16 = smallw[:, 0:2048].rearrange("p (k c) -> p k c", k=4)
            poolw16 = smallw[:, 2048:2560].rearrange("p (k c) -> p k c", k=4)
            P.dma("pool", lambda e: e.dma_start(out=wglu16, in_=w_glu32[i]), [], [r_smallw], key="sw0")
            P.dma("pool", lambda e: e.dma_start(out=poolw16, in_=pool_w32[i]), [], [r_smallw], key="sw1")
            P.dma("sp", lambda e: e.dma_start(out=pvec[:], in_=pvec32[i]), [], [r_pvec], key="pvec")
            fflat = f_sb[:].rearrange("p k n -> p (k n)")
            NP = NT + 16
            r_zp = R("zp")
            WIN = (2, 4, 8, 16)
            stores_all = []
            for t in range(ntiles):
                tok = slice(t * NT, (t + 1) * NT)
                t0 = t * NT
                load_h(t, src, src_regs)
                zp = [fflat[:, gi * NP:(gi + 1) * NP] for gi in range(4)]
                wk_ = [fflat[:, (4 + q_) * NP:(5 + q_) * NP] for q_ in range(3)]
                lo = max(t0 - 8, 0)
                hi = min(t0 + NT + 8, L)
                for gi in range(4):
                    if t == 0:
                        P.op("pool", lambda e, gi=gi: e.memset(zp[gi][:, 0:8], 0.0), reads=[], writes=[r_zp] + r_f)
                    if t == NTILES - 1:
                        P.op("pool", lambda e, gi=gi: e.memset(zp[gi][:, NT + 8:NT + 16], 0.0), reads=[], writes=[r_zp] + r_f)
                    rr = [r_zT[tt][gi] for tt in range(max(t - 1, 0), min(t + 2, NTILES))]
                    P.dma("sp", lambda e, gi=gi, lo=lo, hi=hi, t0=t0: e.dma_start(out=zp[gi][:, lo - (t0 - 8): hi - (t0 - 8)], in_=zT[gi * 128:(gi + 1) * 128, lo:hi]),
                          rr, [r_zp] + r_f, key="zpl%d" % gi)
                for gi in range(4):
                    w = WIN[gi]
                    z = zp[gi]
                    a2, a4, a8 = wk_
                    pr_ = act_sb[:, gi, :]
                    if gi == 0:
                        P.op("dve", lambda e, z=z, a2=a2: e.tensor_tensor(a2[:, 0:NT], z[:, 7:7 + NT], z[:, 8:8 + NT], ALU.add), reads=[r_zp], writes=[r_zp])
                        s_ = a2[:, 0:NT]
                    else:
                        P.op("dve", lambda e, z=z, a2=a2: e.tensor_tensor(a2[:, 0:NP - 1], z[:, 0:NP - 1], z[:, 1:NP], ALU.add), reads=[r_zp], writes=[r_zp])
                        if gi == 1:
                            P.op("dve", lambda e, a2=a2, a4=a4: e.tensor_tensor(a4[:, 0:NT], a2[:, 6:6 + NT], a2[:, 8:8 + NT], ALU.add), reads=[r_zp], writes=[r_zp])
                            s_ = a4[:, 0:NT]
                        else:
                            P.op("dve", lambda e, a2=a2, a4=a4: e.tensor_tensor(a4[:, 0:NP - 3], a2[:, 0:NP - 3], a2[:, 2:NP - 1], ALU.add), reads=[r_zp], writes=[r_zp])
                            if gi == 2:
                                P.op("dve", lambda e, a4=a4, a8=a8: e.tensor_tensor(a8[:, 0:NT], a4[:, 4:4 + NT], a4[:, 8:8 + NT], ALU.add), reads=[r_zp], writes=[r_zp])
                                s_ = a8[:, 0:NT]
                            else:
                                P.op("dve", lambda e, a4=a4, a8=a8: e.tensor_tensor(a8[:, 0:NP - 7], a4[:, 0:NP - 7], a4[:, 4:NP - 3], ALU.add), reads=[r_zp], writes=[r_zp])
                                P.op("dve", lambda e, a2=a2, a8=a8: e.tensor_tensor(a2[:, 0:NT], a8[:, 0:NT], a8[:, 8:8 + NT], ALU.add), reads=[r_zp], writes=[r_zp])
                                s_ = a2[:, 0:NT]
                    P.op("dve", lambda e, s_=s_, z=z, pr_=pr_, w=w: e.scalar_tensor_tensor(out=pr_, in0=s_, scalar=1.0 / w, in1=z[:, 8:8 + NT], op0=ALU.mult, op1=ALU.subtract),
                         reads=[r_zp], writes=[r_act[gi]])
                    if t == 0:
                        P.op("dve", lambda e, s_=s_, gi=gi: e.tensor_tensor(s_[:, 0:8], s_[:, 0:8], cinv[:, gi * 8:(gi + 1) * 8], ALU.mult), reads=[r_zp, r_const], writes=[r_zp])
                        P.op("dve", lambda e, s_=s_, z=z, pr_=pr_: e.tensor_tensor(pr_[:, 0:8], s_[:, 0:8], z[:, 8:16], ALU.subtract), reads=[r_zp, r_act[gi]], writes=[r_act[gi]])
                    if t == NTILES - 1:
                        P.op("dve", lambda e, s_=s_, gi=gi: e.tensor_tensor(s_[:, NT - 8:NT], s_[:, NT - 8:NT], cinv[:, 32 + gi * 8: 32 + (gi + 1) * 8], ALU.mult), reads=[r_zp, r_const], writes=[r_zp])
                        P.op("dve", lambda e, s_=s_, z=z, pr_=pr_: e.tensor_tensor(pr_[:, NT - 8:NT], s_[:, NT - 8:NT], z[:, NT:NT + 8], ALU.subtract), reads=[r_zp, r_act[gi]], writes=[r_act[gi]])
                    pi = rot("ps", NPS)
                    for hf in range(2):
                        P.op("pe", lambda e, gi=gi, hf=hf, pi=pi, pr_=pr_: e.matmul(ps[pi][:, hf * 512:(hf + 1) * 512], poolw16[:, gi, :], pr_[:, hf * 512:(hf + 1) * 512], start=True, stop=True),
                             reads=[r_smallw, r_act[gi]], writes=[r_ps[pi]])
                    P.op("act", lambda e, gi=gi, pi=pi: e.activation(out=act_sb[:, 8 + gi, :], in_=ps[pi][:], func=AF.Identity, scale=pvec[:, gi:gi + 1]),
                         reads=[r_ps[pi], r_pvec], writes=[r_act[8 + gi]])
                ygf = hn_sb[:].rearrange("p k n -> p (k n)").bitcast(F32).rearrange("p (k n) -> p k n", k=4)
                for kt in range(4):
                    A_, B_, C_ = sg_sb[0], sg_sb[1], rstd_sb
                    P.dma("sp", lambda e, kt=kt, tok=tok: e.dma_start(out=A_[:], in_=ydT[0, kt * 128:(kt + 1) * 128, tok]), [r_yd[0][kt][t]], [r_sg[0]], key="ya")
                    P.dma("sp", lambda e, kt=kt, tok=tok: e.dma_start(out=B_[:], in_=ydT[1, kt * 128:(kt + 1) * 128, tok]), [r_yd[1][kt][t]], [r_sg[1]], key="yb")
                    P.dma("sp", lambda e, kt=kt, tok=tok: e.dma_start(out=C_[:], in_=zT[512 + kt * 128: 512 + (kt + 1) * 128, tok]), [r_zT[t][4 + kt]], [r_rstd], key="yu")
                    P.op("pool", lambda e: e.tensor_tensor(A_[:], A_[:], B_[:], ALU.add), reads=[r_sg[0], r_sg[1]], writes=[r_sg[0]])
                    P.op("dve", lambda e, kt=kt: e.scalar_tensor_tensor(out=A_[:], in0=C_[:], scalar=pvec[:, 4 + kt: 5 + kt], in1=A_[:], op0=ALU.mult, op1=ALU.add),
                         reads=[r_sg[0], r_rstd, r_pvec], writes=[r_sg[0]])
                    P.op("act", lambda e: e.activation(out=B_[:], in_=A_[:], func=AF.Square), reads=[r_sg[0]], writes=[r_sg[1]])
                    P.op("dve", lambda e: e.tensor_scalar(B_[:], B_[:], 0.044715, 1.0, ALU.mult, ALU.add), reads=[r_sg[1]], writes=[r_sg[1]])
                    P.op("dve", lambda e: e.tensor_tensor(B_[:], B_[:], A_[:], ALU.mult), reads=[r_sg[0], r_sg[1]], writes=[r_sg[1]])
                    P.op("act", lambda e: e.activation(out=B_[:], in_=B_[:], func=AF.Sigmoid, scale=2.0 * math.sqrt(2.0 / PI)), reads=[r_sg[1]], writes=[r_sg[1]])
                    P.op("dve", lambda e, kt=kt: e.tensor_tensor(ygf[:, kt, :], A_[:], B_[:], ALU.mult), reads=[r_sg[0], r_sg[1]], writes=[r_hn[2 * kt], r_hn[2 * kt + 1]])
                    P.op("act", lambda e, kt=kt: e.activation(out=act_sb[:, 4 + kt, :], in_=ygf[:, kt, :], func=AF.Copy), reads=[r_hn[2 * kt], r_hn[2 * kt + 1]], writes=[r_act[4 + kt]])
                for o in range(4):
                    pi = proj8(lambda k, o=o: wglu16[:, k, o * 128:(o + 1) * 128], [r_smallw],
                               lambda k, hf: act_sb[:, 4 + k, hf * 512:(hf + 1) * 512], r_act[4:8], nk=4)
                    P.op("act", lambda e, o=o, pi=pi: e.activation(out=sg_sb[1][:], in_=ps[pi][:], func=AF.Sigmoid, bias=pvec[:, 8 + o: 9 + o]),
                         reads=[r_ps[pi], r_pvec], writes=[r_sg[1]])
                    P.op("dve", lambda e, o=o: e.tensor_tensor(act_sb[:, 12 + o, :], ygf[:, o, :], sg_sb[1][:], ALU.mult),
                         reads=[r_sg[1], r_hn[2 * o], r_hn[2 * o + 1]], writes=[r_act[12 + o]])
                for o in range(NK):
                    pi = proj8(lambda k, o=o: mx0[:, k, o * 128:(o + 1) * 128], [r_mx0],
                               lambda k, hf: act_sb[:, 8 + k, hf * 512:(hf + 1) * 512], r_act[8:16])
                    evac_copy(f_sb[:, o, :], ps[pi][:], [r_ps[pi], r_zp], [r_f[o], r_zp])
                stores_all += residual_store(layer, 3, False, t, dst, dst_regs)
            return stores_all

        def mixer_odd(layer, src, src_regs, dst, dst_regs):
            i = layer // 2
            P.barrier()
            wq16 = act_sb[:, 0:8, :]
            wk16 = act_sb[:, 8:16, :]
            r_wq = R("wq")
            r_wk = R("wk")
            P.dma("pool", lambda e: e.dma_start(out=wq16, in_=wq32[i]), [], [r_wq], key="mxq")
            P.dma("pool", lambda e: e.dma_start(out=wk16, in_=wk32[i]), [], [r_wk], key="mxk")
            P.dma("pool", lambda e: e.dma_start(out=mx0[:], in_=wv32[i]), [], [r_mx0], key="mx0")
            for t in range(ntiles):
                tok = slice(t * NT, (t + 1) * NT)
                load_norm(layer, 2, t, src, src_regs)
                for (w16, rw, dstT, rd, sc) in ((wq16, r_wq, qT, r_qT, 0.125), (wk16, r_wk, kT, r_kT, None)):
                    for o in range(NK):
                        pi = proj8(lambda k, o=o, w16=w16: w16[:, k, o * 128:(o + 1) * 128], [rw],
                                   lambda k, hf: hn_sb[:, k, hf * 512:(hf + 1) * 512], r_hn)
                        si = rot("sq", 2)
                        evac_copy(sq_sb[si][:], ps[pi][:], [r_ps[pi]], [r_sq[si]], scale=sc)
                        P.dma("pool", lambda e, o=o, si=si, dstT=dstT, tok=tok: e.dma_start(out=dstT[o * 128:(o + 1) * 128, tok], in_=sq_sb[si][:]),
                              [r_sq[si]], [rd[t]], key="qk%d" % si)
                for tb in range(NT // 128):
                    pi = rot("ps", NPS)
                    for hf in range(2):
                        for k in range(NK):
                            P.op("pe", lambda e, tb=tb, hf=hf, k=k, pi=pi: e.matmul(ps[pi][:, hf * 512:(hf + 1) * 512], hn_sb[:, k, tb * 128:(tb + 1) * 128],
                                                                                    mx0[:, k, hf * 512:(hf + 1) * 512], start=(k == 0), stop=(k == NK - 1)),
                                 reads=[r_hn[k], r_mx0], writes=[r_ps[pi]])
                    si = rot("sq", 2)
                    evac_copy(sq_sb[si][:], ps[pi][:], [r_ps[pi]], [r_sq[si]])
                    P.dma("pool", lambda e, tb=tb, si=si, t=t: e.dma_start(out=vtok[t * NT + tb * 128: t * NT + (tb + 1) * 128, :], in_=sq_sb[si][:]),
                          [r_sq[si]], [r_vt[t]], key="qk%d" % si)
            P.barrier()
            P.dma("pool", lambda e: e.dma_start(out=mx0[:], in_=wo32[i]), [], [r_mx0], key="mx0")
            U16 = parbuf[:].bitcast(BF16)[:, 0:8 * 960].rearrange("p (h n c) -> p h n c", h=8, n=15)
            r_U = R("U16")
            cm = carry
            colm = sg_sb[1]
            P.dma("sp", lambda e: e.dma_start(out=colm[:, 0:64], in_=colmask32), [], [r_sg[1]], key="colm")
            for hm in range(8):
                P.dma("sp", lambda e, hm=hm: e.dma_start(out=sg_sb[0][:, 0:960], in_=rpbU32[i, hm]), [], [r_sg[0]], key="rpbu")
                P.op("dve", lambda e, hm=hm: e.tensor_tensor(U16[:, hm, :, :], sg_sb[0][:, 0:960].rearrange("p (n c) -> p n c", n=15),
                                                             colm[:, 0:64].unsqueeze(1).to_broadcast([128, 15, 64]), ALU.add),
                     reads=[r_sg[0], r_sg[1]], writes=[r_U])
            kflat = act_sb[:, 0:12, :].rearrange("p a b -> p (a b)").rearrange("p (k n) -> p k n", k=8)
            r_kb = R("kband")
            NVS = 10
            r_vs = [R("vs%d" % s) for s in range(NVS)]
            vstate = {}
            pt_sb = [pt_buf[:, q_, :] for q_ in range(3)]
            r_pt = [R("pt0"), R("pt1"), R("pt2")]
            cnt = {"pt": 0, "ps_s": 0, "ps_o": 0, "q": 0, "rec": 0}
            kT3 = kT.rearrange("(k p) l -> p k l", p=128)
            qT3 = qT.rearrange("(k p) l -> p k l", p=128)
            stores_all = []

            def ensure_v(startrow):
                s = startrow % NVS
                if vstate.get(s) == startrow:
                    return s
                vstate[s] = startrow
                tl = (startrow * 64) // NT
                th = (startrow * 64 + 127) // NT
                P.dma("sp", lambda e, s=s, startrow=startrow: e.dma_start(out=act_sb[:, 12 + s, :], in_=vtok[startrow * 64: startrow * 64 + 128, :]),
                      [r_vt[tl], r_vt[th]], [r_vs[s]], key="vs%d" % s)
                return s

            nrows_run = Lr // GRID_W if ntiles == NTILES else (Lr // GRID_W - 16)
            for b in range(nrows_run // 16):
                t = b
                load_h(t, src, src_regs)
                ra = max(16 * b - 4, 0)
                rb = min(16 * b + 20, NROWS)
                nkb = (rb - ra) * 64
                tls = sorted(set([(ra * 64) // NT, (rb * 64 - 1) // NT, b]))
                P.dma("sp", lambda e, ra=ra, nkb=nkb: e.dma_start(out=kflat[:, :, 0:nkb], in_=kT3[:, :, ra * 64: ra * 64 + nkb]),
                      [r_kT[x] for x in tls], [r_kb], key="kband")
                rowinfo = {}

                def row_prologue(rr_, b=b, ra=ra, rowinfo=rowinfo):
                    r = 16 * b + rr_
                    r0 = min(max(r - 4, 0), NROWS - 8)
                    qi = cnt["q"] % 2
                    cnt["q"] += 1
                    P.dma("sp", lambda e, qi=qi, r=r: e.dma_start(out=qrow[:, qi, :, :], in_=qT3[:, :, r * 64:(r + 1) * 64]),
                          [r_qT[b]], [r_qrow[qi]], key="qrow%d" % qi)
                    vsl = [ensure_v(r0 + 2 * kt) for kt in range(4)]
                    rowinfo[rr_] = dict(n0=r0 - r + 7, koff=(r0 - ra) * 64, qi=qi, vsl=vsl)

                row_prologue(0)
                row_prologue(1)

                def SA(n, b=b, ra=ra, rowinfo=rowinfo, row_prologue=row_prologue):
                    rr_, h = divmod(n, NH)
                    if h == 0 and rr_ >= 1 and rr_ + 1 < 16:
                        row_prologue(rr_ + 1)
                    ri_ = rowinfo[rr_]
                    n0, koff, qi = ri_["n0"], ri_["koff"], ri_["qi"]
                    hk = h // 2
                    pr = slice((h % 2) * 64, (h % 2) * 64 + 64)
                    pS = cnt["ps_s"] % 2
                    cnt["ps_s"] += 1
                    for kt in range(4):
                        P.op("pe", lambda e, kt=kt: e.matmul(
                            ps[pS][:, kt * 64:(kt + 1) * 64], kflat[pr, hk, koff + kt * 128: koff + (kt + 1) * 128], qrow[pr, qi, hk, :], start=True, stop=False),
                             reads=[r_kb, r_qrow[qi]], writes=[r_ps[pS]])
                        P.op("pe", lambda e, kt=kt: e.matmul(
                            ps[pS][:, kt * 64:(kt + 1) * 64], U16[pr, h // 2, n0 + 2 * kt: n0 + 2 * kt + 2, :].rearrange("p n c -> p (n c)"),
                            ident16[pr, pr], start=False, stop=True),
                             reads=[r_U, r_const], writes=[r_ps[pS]])
                    pti = cnt["pt"] % 3
                    cnt["pt"] += 1
                    P.op("act", lambda e: e.activation(out=pt_sb[pti], in_=ps[pS][:, 0:256], func=AF.Exp),
                         reads=[r_ps[pS]], writes=[r_pt[pti]])
                    ri_[("pti", h)] = pti

                def SB(n, rowinfo=rowinfo):
                    rr_, h = divmod(n, NH)
                    ri_ = rowinfo[rr_]
                    vsl = ri_["vsl"]
                    pti = ri_[("pti", h)]
                    hk = h // 2
                    pr = slice((h % 2) * 64, (h % 2) * 64 + 64)
                    pO = 2 + cnt["ps_o"] % 2
                    cnt["ps_o"] += 1
                    for kt in range(4):
                        P.op("pe", lambda e, kt=kt, s=vsl[kt]: e.matmul(
                            ps[pO][:, 0:64], act_sb[:, 12 + s, hk * 128:(hk + 1) * 128], pt_sb[pti][:, kt * 64:(kt + 1) * 64], start=(kt == 0), stop=(kt == 3)),
                             reads=[r_vs[vsl[kt]], r_pt[pti]], writes=[r_ps[pO]])
                    for kt in range(4):
                        P.op("pe", lambda e, kt=kt: e.matmul(
                            ps[pO][:, 64:128], ones_sb[:], pt_sb[pti][:, kt * 64:(kt + 1) * 64], start=(kt == 0), stop=(kt == 3)),
                             reads=[r_const, r_pt[pti]], writes=[r_ps[pO]])
                    ri = cnt["rec"] % 2
                    cnt["rec"] += 1
                    P.op("dve", lambda e: e.reciprocal(recb[pr, ri, :], ps[pO][pr, 64:128]), reads=[r_ps[pO]], writes=[r_rec[ri]])
                    P.op("dve", lambda e: e.tensor_tensor(hn_sb[pr, hk, rr_ * 64:(rr_ + 1) * 64], ps[pO][pr, 0:64], recb[pr, ri, :], ALU.mult),
                         reads=[r_ps[pO], r_rec[ri]], writes=[r_hn[hk]])

                NI = 16 * NH
                SA(0)
                for n in range(NI):
                    if n + 1 < NI:
                        SA(n + 1)
                    SB(n)
                for o in range(NK):
                    pi = proj8(lambda k, o=o: mx0[:, k, o * 128:(o + 1) * 128], [r_mx0],
                               lambda k, hf: hn_sb[:, k, hf * 512:(hf + 1) * 512], r_hn)
                    evac_copy(f_sb[:, o, :], ps[pi][:], [r_ps[pi]], [r_f[o]])
                stores_all += residual_store(layer, 3, False, t, dst, dst_regs)
            return stores_all

        jvec_sb = sb("jvec_sb", [128, NT], F32)
        P.dma("sp", lambda e: e.dma_start(out=jvec_sb[:], in_=jvec32), [], [r_const], key="c")
        r_mt = [R("mt0"), R("mt1")]

        ffn_steps = [st_ for st_ in steps if st_[2] in "ac"]
        fids = [int(st_[1]) * 2 + (0 if st_[2] == "a" else 1) for st_ in ffn_steps]
        for fid in fids:
            for t in range(ntiles):
                for j in range(NJ):
                    wstream.append(("gu", fid, j))
                for m in range(NK):
                    wstream.append(("d", fid, m))

        r_x = [[R("x%d_%d" % (t, k)) for k in range(NK)] for t in range(NTILES)]
        final = []
        if fids:
            cast_ffn_weights(fids[0])
        nf = 0
        for si, step in enumerate(steps):
            layer = int(step[1])
            first = (si == 0)
            last = (si == len(steps) - 1)
            src = xT if first else hT
            dst = outT if last else hT
            sregs = r_x if first else r_hT
            if step[2] in "ac":
                f = 0 if step[2] == "a" else 1
                nf += 1
                if nf < len(fids):
                    cast_ffn_weights(fids[nf])
                stores = ffn_step(layer, f, src, sregs, dst, r_hT)
            elif layer % 2 == 0:
                stores = mixer_even(layer, src, sregs, dst, r_hT)
                P.barrier()
            else:
                stores = mixer_odd(layer, src, sregs, dst, r_hT)
                P.barrier()
            if last:
                final += stores
        P.emit(final)
    return nc


_CACHE = {}


def _prep_weights(inputs):
    f32 = np.float32
    A = np.ascontiguousarray
    out = {}
    out["gT"] = A(inputs["norm_g"].reshape(DEPTH, 6, NK, 128).transpose(3, 0, 1, 2).reshape(128, DEPTH * 6 * NK))
    wg = inputs["ffn_w_gate"].reshape(DEPTH, 2, NK, 128, NJ, 128)
    wu = inputs["ffn_w_up"].reshape(DEPTH, 2, NK, 128, NJ, 128)
    wgu = np.stack([wg, wu], axis=0)
    out["wgu"] = A(wgu.transpose(1, 2, 5, 4, 0, 3, 6)).reshape(DEPTH * 2 * NJ, 128, 2 * NK * 128)
    wd = inputs["ffn_w_down"].reshape(DEPTH, 2, NJ, 128, NK, 128)
    out["wd"] = A(wd.transpose(0, 1, 4, 3, 2, 5)).reshape(DEPTH * 2 * NK, 128, NJ * 128)
    out["ident"] = np.eye(128, dtype=f32)
    sh = np.zeros((128, 128), f32)
    for k in range(128):
        sh[k, (k + 64) % 128] = 1.0
    out["shiftm"] = sh
    out["jvec"] = A(np.broadcast_to(np.arange(NT, dtype=f32)[None, :], (128, NT)))
    cs = np.zeros((128, 32), f32)
    cs[:64, 0] = 1.0
    cs[64:, 0] = -1.0
    cs[:64, 1] = -1.0
    cs[64:, 1] = 1.0
    for g8 in range(8):
        cs[g8 * 16:(g8 + 1) * 16, 2 + g8] = 1.0
        cs[g8 * 16:(g8 + 1) * 16, 10 + g8] = -1.0
    cs[:, 18] = 0.25
    out["csts"] = cs
    cinv = np.zeros((2, 4, 8), f32)
    tt = np.arange(L)
    for gi, w in enumerate((2, 4, 8, 16)):
        lo = w // 2
        hi = w - 1 - lo
        cntv = (np.clip(tt + hi + 1, 0, L) - np.clip(tt - lo, 0, L)).astype(f32)
        cinv[0, gi] = 1.0 / cntv[:8]
        cinv[1, gi] = 1.0 / cntv[L - 8:]
    out["cinv"] = A(np.broadcast_to(cinv.reshape(1, 64), (128, 64)))

    def pk(w):
        n, kk, c = w.shape
        return A(w.reshape(n, kk // 128, 128, c).transpose(0, 2, 1, 3))
    out["w_in"] = pk(inputs["ab_w_in"])
    out["w_out"] = pk(inputs["ab_w_out"])
    out["w_glu"] = pk(inputs["ssm_w_glu"])
    out["pool_w"] = A(inputs["pool_w"].transpose(0, 2, 1, 3))
    pv = np.zeros((2, 128, 12), f32)
    pv[:, :, 0:4] = inputs["pool_scale"].reshape(2, 4, 128).transpose(0, 2, 1)
    pv[:, :, 4:8] = inputs["ssm_D"].reshape(2, 4, 128).transpose(0, 2, 1)
    pv[:, :, 8:12] = inputs["ssm_b_glu"].reshape(2, 4, 128).transpose(0, 2, 1)
    out["pvec"] = pv

    def dup(a):
        a = a.transpose(0, 1, 3, 2)
        return np.concatenate([a, a], axis=2)
    s5a = np.stack([dup(inputs["ssm_A_re"]), dup(inputs["ssm_A_im"]),
                    np.broadcast_to(inputs["ssm_log_dt"][:, :, None, :], (2, 2, 128, 32))], axis=2)
    out["s5a"] = A(s5a.astype(f32))
    bre = inputs["ssm_B_re"].transpose(0, 1, 3, 2, 4).reshape(2, 2, 64, 512)
    bim = inputs["ssm_B_im"].transpose(0, 1, 3, 2, 4).reshape(2, 2, 64, 512)
    cre = inputs["ssm_C_re"].transpose(0, 1, 4, 2, 3).reshape(2, 2, 64, 512)
    cim = inputs["ssm_C_im"].transpose(0, 1, 4, 2, 3).reshape(2, 2, 64, 512)
    s5b = np.stack([np.concatenate([bre, bim], axis=2), np.concatenate([bim, bre], axis=2),
                    np.concatenate([cre, cim], axis=2), np.concatenate([cim, cre], axis=2)], axis=2)
    out["s5b"] = A(s5b.astype(f32))
    wqkv = inputs["na_w_qkv"]
    out["wq"] = pk(wqkv[:, :, 0:D])
    out["wk"] = pk(wqkv[:, :, D:2 * D])
    out["wv"] = pk(wqkv[:, :, 2 * D:3 * D])
    out["wo"] = pk(inputs["na_w_out"])
    c = np.arange(64)
    cj = np.arange(64)
    idx = np.clip(cj[None, :] - c[:, None] + 15, 0, 30)
    rp = inputs["na_rpb"][:, :, :, idx]
    rp = rp.transpose(0, 1, 3, 2, 4).reshape(2, 8, 2, 64, 15 * 64)
    out["rpbU"] = A(rp.reshape(2, 8, 128, 960))
    c0 = np.clip(c - 8, 0, 48)
    valid = (cj[None, :] >= c0[:, None]) & (cj[None, :] < c0[:, None] + 16)
    cmask = np.where(valid, 0.0, -30000.0).astype(f32)
    out["colmask"] = A(np.concatenate([cmask, cmask], axis=0))
    return out


def kernel(**inputs):
    x = inputs["x"]
    B = x.shape[0]
    if "nc" not in _CACHE:
        _CACHE["nc"] = build_program(ALL_STEPS)
    nc = _CACHE["nc"]
    shared = _prep_weights(inputs)
    in_maps = []
    for b in range(B):
        m = dict(shared)
        m["xT"] = np.ascontiguousarray(x[b].T)
        in_maps.append(m)
    res = run_bass_kernel_spmd(nc, in_maps, core_ids=list(range(B)))
    out = np.stack([np.ascontiguousarray(r["outT"].T) for r in res.results], axis=0)
    return out.astype(np.float32)
```
